# Optimizing a Trainium2 kernel written in Bass

```python
import jax, jax.numpy as jnp
from jax import lax
import numpy as np


D_MODEL = 1024
BATCH = 8
SEQ = 4096
DEPTH = 2

DN_HEADS = 4
DN_DK = 128
DN_DV = 128
DN_QK = DN_HEADS * DN_DK
DN_V = DN_HEADS * DN_DV
DN_CONV = 5
CHUNK = 64
N_DIR = 2
CF_WIDTH = 512
CF_CONV = 31
N_BRANCH = 2
EPS = 1e-6

SIZES = (2 * DN_QK + DN_V,
         DN_V,
         N_DIR * DN_HEADS,
         N_DIR * DN_HEADS,
         CF_WIDTH,
         CF_WIDTH,
         CF_WIDTH,
         N_BRANCH * D_MODEL)
D_IN = sum(SIZES)

kernel_name = 'hybrid_gated_deltanet_conformer_encoder'


def rms_norm(x, w):
    xf = x.astype(jnp.float32)
    y = xf * lax.rsqrt(jnp.mean(xf * xf, axis=-1, keepdims=True) + EPS)
    return (y * w.astype(jnp.float32)).astype(x.dtype)


def layer_norm(x, w, b):
    xf = x.astype(jnp.float32)
    mu = jnp.mean(xf, axis=-1, keepdims=True)
    xc = xf - mu
    y = xc * lax.rsqrt(jnp.mean(xc * xc, axis=-1, keepdims=True) + EPS)
    return (y * w.astype(jnp.float32) + b.astype(jnp.float32)).astype(x.dtype)


def l2_normalize(t):
    return t * lax.rsqrt(jnp.sum(t * t, axis=-1, keepdims=True) + EPS)


def depthwise_conv_centred(u, w):
    k = w.shape[0]
    return lax.conv_general_dilated(
        u, w[:, None, :].astype(u.dtype), window_strides=(1,),
        padding=[(k // 2, k // 2)], dimension_numbers=('NWC', 'WIO', 'NWC'),
        feature_group_count=u.shape[-1])


def gated_delta_chunked(q, k, v, g, beta):
    bsz, seq, nh, dk = q.shape
    dv = v.shape[-1]
    n = seq // CHUNK
    q = q.reshape(bsz, n, CHUNK, nh, dk).transpose(0, 3, 1, 2, 4) * (dk ** -0.5)
    k = k.reshape(bsz, n, CHUNK, nh, dk).transpose(0, 3, 1, 2, 4)
    v = v.reshape(bsz, n, CHUNK, nh, dv).transpose(0, 3, 1, 2, 4)
    g = jnp.cumsum(g.reshape(bsz, n, CHUNK, nh).transpose(0, 3, 1, 2), axis=-1)
    beta = beta.reshape(bsz, n, CHUNK, nh).transpose(0, 3, 1, 2)

    incl = jnp.tril(jnp.ones((CHUNK, CHUNK), dtype=bool))
    strict = jnp.tril(jnp.ones((CHUNK, CHUNK), dtype=bool), k=-1)
    diff = g[..., :, None] - g[..., None, :]
    decay = jnp.where(incl, jnp.exp(jnp.where(incl, diff, 0.0)), 0.0)

    k_beta = k * beta[..., None]
    lmat = jnp.where(strict, jnp.einsum('bhncd,bhnsd->bhncs', k_beta, k) * decay, 0.0)
    eye = jnp.eye(CHUNK, dtype=jnp.float32)
    rhs = jnp.concatenate([v * beta[..., None], k_beta * jnp.exp(g)[..., None]], axis=-1)
    sol = lax.linalg.triangular_solve(eye + lmat, rhs, left_side=True, lower=True,
                                      unit_diagonal=True)
    value, k_cum = sol[..., :dv], sol[..., dv:]

    attn = jnp.einsum('bhncd,bhnsd->bhncs', q, k) * decay
    q_dec = q * jnp.exp(g)[..., None]
    g_last = g[..., -1]
    k_dec = k * jnp.exp(g_last[..., None] - g)[..., None]

    xs = (jnp.moveaxis(value, 2, 0), jnp.moveaxis(k_cum, 2, 0), jnp.moveaxis(attn, 2, 0),
          jnp.moveaxis(q_dec, 2, 0), jnp.moveaxis(k_dec, 2, 0),
          jnp.moveaxis(jnp.exp(g_last), 2, 0))

    def step(state, inp):
        val_n, kc_n, a_n, qd_n, kd_n, dl_n = inp
        v_new = val_n - jnp.einsum('bhcd,bhde->bhce', kc_n, state)
        o_n = jnp.einsum('bhcd,bhde->bhce', qd_n, state) + jnp.einsum('bhcs,bhse->bhce', a_n, v_new)
        state = state * dl_n[..., None, None] + jnp.einsum('bhcd,bhce->bhde', kd_n, v_new)
        return state, o_n

    s0 = jnp.zeros((bsz, nh, dk, dv), dtype=jnp.float32)
    _, o = lax.scan(step, s0, xs)
    return o.transpose(1, 0, 3, 2, 4).reshape(bsz, seq, nh, dv)


def hybrid_mixer(h, w_in, qkv_conv_w, a_log, dt_bias, dn_norm_w, w_dn_out,
                 cf_conv_w, cf_conv_b, cf_ln_w, cf_ln_b, w_cf_out, gate_b, w_out):
    bsz, seq, _ = h.shape
    pts = []
    acc = 0
    for s in SIZES[:-1]:
        acc += s
        pts.append(acc)
    proj = h @ w_in
    qkv, z_a, a_lg, b_lg, cf_val, cf_glu, z_b, gate_lg = jnp.split(proj, pts, axis=-1)

    qkv = jax.nn.silu(depthwise_conv_centred(qkv, qkv_conv_w))
    q = qkv[..., :DN_QK].astype(jnp.float32).reshape(bsz, seq, DN_HEADS, DN_DK)
    k = qkv[..., DN_QK:2 * DN_QK].astype(jnp.float32).reshape(bsz, seq, DN_HEADS, DN_DK)
    v = qkv[..., 2 * DN_QK:].astype(jnp.float32).reshape(bsz, seq, DN_HEADS, DN_DV)
    q = l2_normalize(q)
    k = l2_normalize(k)
    a_lg = a_lg.astype(jnp.float32).reshape(bsz, seq, N_DIR, DN_HEADS)
    g = -jnp.exp(a_log.astype(jnp.float32)) * jax.nn.softplus(a_lg + dt_bias.astype(jnp.float32))
    beta = jax.nn.sigmoid(b_lg.astype(jnp.float32).reshape(bsz, seq, N_DIR, DN_HEADS))
    o_fwd = gated_delta_chunked(q, k, v, g[:, :, 0], beta[:, :, 0])
    flip = lambda t: jnp.flip(t, axis=1)
    o_bwd = flip(gated_delta_chunked(flip(q), flip(k), flip(v), flip(g[:, :, 1]), flip(beta[:, :, 1])))
    o = o_fwd + o_bwd
    o = o * lax.rsqrt(jnp.mean(o * o, axis=-1, keepdims=True) + EPS) * dn_norm_w.astype(jnp.float32)
    o = o * jax.nn.silu(z_a.astype(jnp.float32).reshape(bsz, seq, DN_HEADS, DN_DV))
    y_a = o.reshape(bsz, seq, DN_V).astype(h.dtype) @ w_dn_out

    u = cf_val * jax.nn.sigmoid(cf_glu)
    u = depthwise_conv_centred(u, cf_conv_w) + cf_conv_b
    u = jax.nn.silu(layer_norm(u, cf_ln_w, cf_ln_b))
    y_b = (u * jax.nn.silu(z_b)) @ w_cf_out

    gate = jax.nn.sigmoid(gate_lg + gate_b).reshape(bsz, seq, N_BRANCH, D_MODEL)
    y = gate[:, :, 0] * y_a + gate[:, :, 1] * y_b
    return y @ w_out


def setup_inputs(seed: int = 0) -> dict:
    key = jax.random.key(seed)
    ks = jax.random.split(key, 17)
    nrm = jax.random.normal
    x = nrm(ks[0], (BATCH, SEQ, D_MODEL), jnp.float32)
    norm_w = 1.0 + 0.05 * nrm(ks[1], (DEPTH, D_MODEL), jnp.float32)
    w_in = nrm(ks[2], (DEPTH, D_MODEL, D_IN), jnp.float32) * D_MODEL ** -0.5
    qkv_conv_w = nrm(ks[3], (DEPTH, DN_CONV, 2 * DN_QK + DN_V), jnp.float32) * DN_CONV ** -0.5
    a_log = jnp.log(jax.random.uniform(ks[4], (DEPTH, N_DIR, DN_HEADS), jnp.float32, 1.0, 16.0))
    dt = jnp.exp(jax.random.uniform(ks[5], (DEPTH, N_DIR, DN_HEADS), jnp.float32,
                                    float(np.log(1e-3)), float(np.log(1e-1))))
    dt_bias = dt + jnp.log(-jnp.expm1(-dt))
    dn_norm_w = 1.0 + 0.05 * nrm(ks[6], (DEPTH, DN_DV), jnp.float32)
    w_dn_out = nrm(ks[7], (DEPTH, DN_V, D_MODEL), jnp.float32) * DN_V ** -0.5
    cf_conv_w = nrm(ks[8], (DEPTH, CF_CONV, CF_WIDTH), jnp.float32) * CF_CONV ** -0.5
    cf_conv_b = 0.02 * nrm(ks[9], (DEPTH, CF_WIDTH), jnp.float32)
    cf_ln_w = 1.0 + 0.05 * nrm(ks[10], (DEPTH, CF_WIDTH), jnp.float32)
    cf_ln_b = 0.02 * nrm(ks[11], (DEPTH, CF_WIDTH), jnp.float32)
    w_cf_out = nrm(ks[12], (DEPTH, CF_WIDTH, D_MODEL), jnp.float32) * CF_WIDTH ** -0.5
    gate_b = 0.02 * nrm(ks[13], (DEPTH, N_BRANCH * D_MODEL), jnp.float32)
    w_out = nrm(ks[14], (DEPTH, D_MODEL, D_MODEL), jnp.float32) * D_MODEL ** -0.5
    final_norm_w = 1.0 + 0.05 * nrm(ks[15], (D_MODEL,), jnp.float32)
    return {'x': x, 'norm_w': norm_w, 'w_in': w_in, 'qkv_conv_w': qkv_conv_w,
            'a_log': a_log, 'dt_bias': dt_bias, 'dn_norm_w': dn_norm_w, 'w_dn_out': w_dn_out,
            'cf_conv_w': cf_conv_w, 'cf_conv_b': cf_conv_b, 'cf_ln_w': cf_ln_w, 'cf_ln_b': cf_ln_b,
            'w_cf_out': w_cf_out, 'gate_b': gate_b, 'w_out': w_out, 'final_norm_w': final_norm_w}


def reference(x, norm_w, w_in, qkv_conv_w, a_log, dt_bias, dn_norm_w, w_dn_out,
              cf_conv_w, cf_conv_b, cf_ln_w, cf_ln_b, w_cf_out, gate_b, w_out, final_norm_w):
    for l in range(DEPTH):
        h = rms_norm(x, norm_w[l])
        x = x + hybrid_mixer(h, w_in[l], qkv_conv_w[l], a_log[l], dt_bias[l], dn_norm_w[l],
                             w_dn_out[l], cf_conv_w[l], cf_conv_b[l], cf_ln_w[l], cf_ln_b[l],
                             w_cf_out[l], gate_b[l], w_out[l])
    return rms_norm(x, final_norm_w)
```

```python
import numpy as np
import concourse.bass as bass
import concourse.mybir as mybir

F32 = mybir.dt.float32
BF16 = mybir.dt.bfloat16
I32 = mybir.dt.int32
AF = mybir.ActivationFunctionType
ALU = mybir.AluOpType
AX = mybir.AxisListType


def _prod(xs):
    r = 1
    for v in xs:
        r *= int(v)
    return r


def box(a):
    t = a.tensor
    name = t.name
    apl = a.ap
    off = int(a.offset)
    space = str(a.space)
    if 'DRAM' in space.upper():
        hi = off + sum((c - 1) * abs(s) for s, c in apl) + 1
        return (name, 0, 1, off, hi)
    if 'PSUM' in space.upper():
        return (name, 0, 128, 0, 1 << 30)
    pstride = _prod(t.shape[1:])
    p0 = off // pstride
    f0 = off % pstride
    ps, pc = apl[0]
    f1 = f0 + sum((c - 1) * abs(s) for s, c in apl[1:]) + 1
    return (name, p0, p0 + pc, f0, f1)


def boxes(a):
    b = box(a)
    space = str(a.space).upper()
    if 'DRAM' in space or 'PSUM' in space:
        return [b]
    apl = a.ap
    if len(apl) < 3:
        return [b]
    s1, c1 = apl[1]
    inner = sum((c - 1) * abs(s) for s, c in apl[2:]) + 1
    if c1 <= 1 or c1 > 16 or s1 <= 0 or inner > s1:
        return [b]
    name, p0, p1, f0, _ = b
    return [(name, p0, p1, f0 + k * s1, f0 + k * s1 + inner) for k in range(c1)]


def bc_last(ap, n):
    return bass.AP(ap.tensor, ap.offset, [list(e) for e in ap.ap] + [[0, n]])


def bc_mid(ap, n):
    l = [list(e) for e in ap.ap]
    return bass.AP(ap.tensor, ap.offset, [l[0], [0, n]] + l[1:])


CP_DEFAULT = '1'


class Sched:
    ROT = 8000
    NDMA = 64

    def __init__(self, nc):
        self.nc = nc
        self.engs = {'pe': nc.tensor, 'act': nc.scalar, 'dve': nc.vector,
                     'pool': nc.gpsimd, 'sp': nc.sync}
        self.ops = []

    def add(self, eng, fn, reads, writes, dma=False):
        reads = [a for a in reads if a is not None]
        writes = [a for a in writes if a is not None]
        w0 = writes[0]
        n = 1
        for s_, c_ in list(w0.ap)[1:]:
            n *= int(c_)
        psrc = any('PSUM' in str(a.space).upper() for a in reads)
        f32 = bool(reads) and all(a.dtype == F32 for a in reads[:2])
        nbytes = n * int(list(w0.ap)[0][1]) * (4 if w0.dtype == F32 else 2)
        self.ops.append(dict(eng=eng, fn=fn, r=[bb for a in reads for bb in boxes(a)],
                             w=[bb for a in writes for bb in boxes(a)], dma=dma, fence=False,
                             n=n, psrc=psrc, f32=f32, nbytes=nbytes, nrd=len(reads)))

    def fence(self):
        self.ops.append(dict(fence=True))

    def capture(self):
        self._saved = self.ops
        self.ops = []

    def release(self):
        lst = self.ops
        self.ops = self._saved
        return lst

    def merge(self, *lists):
        n = max(len(x) for x in lists)
        for i in range(n):
            for x in lists:
                if i < len(x):
                    self.ops.append(x[i])

    def final(self):
        self.ops.append(dict(fence=False, final=True, eng='sp', fn=None, dma=False, r=[], w=[]))

    def mm(self, out, lhsT, rhs, start=True, stop=True):
        self.add('pe', lambda: self.nc.tensor.matmul(out, lhsT, rhs, start=start, stop=stop),
                 [lhsT, rhs], [out])

    def tr(self, out, in_, ident):
        self.add('pe', lambda: self.nc.tensor.transpose(out, in_, ident), [in_, ident], [out])

    def act(self, out, in_, func, bias=None, scale=None, accum_out=None):
        kw = {}
        rd = [in_]
        if bias is not None:
            kw['bias'] = bias
            if not isinstance(bias, (int, float)):
                rd.append(bias)
        if scale is not None:
            kw['scale'] = scale
            if not isinstance(scale, (int, float)):
                rd.append(scale)
        wr = [out]
        if accum_out is not None:
            kw['accum_out'] = accum_out
            wr.append(accum_out)
        self.add('act', lambda: self.nc.scalar.activation(out, in_, func, **kw), rd, wr)
        self.ops[-1]['actg'] = {AF.Silu: 'silu', AF.Sigmoid: 'sig', AF.Sqrt: 'sqrt', AF.Exp: 'exp', AF.Ln: 'exp'}.get(func)

    def tt(self, eng, out, in0, in1, op):
        e = self.engs[eng]
        self.add(eng, lambda: e.tensor_tensor(out, in0, in1, op), [in0, in1], [out])

    def ts(self, eng, out, in0, s1, s2, op0, op1=None):
        e = self.engs[eng]
        rd = [in0] + [s for s in (s1, s2) if s is not None and not isinstance(s, (int, float))]
        if op1 is None:
            self.add(eng, lambda: e.tensor_single_scalar(out, in0, s1, op0), rd, [out])
        else:
            self.add(eng, lambda: e.tensor_scalar(out, in0, s1, s2, op0, op1), rd, [out])

    def stt(self, eng, out, in0, scalar, in1, op0, op1):
        e = self.engs[eng]
        rd = [in0, in1] + ([scalar] if not isinstance(scalar, (int, float)) else [])
        self.add(eng, lambda: e.scalar_tensor_tensor(out, in0, scalar, in1, op0, op1), rd, [out])

    def copy(self, eng, out, in_):
        if eng == 'act':
            self.add('act', lambda: self.nc.scalar.copy(out, in_), [in_], [out])
        else:
            e = self.engs[eng]
            self.add(eng, lambda: e.tensor_copy(out, in_), [in_], [out])

    def memset(self, eng, ap, val):
        e = self.engs[eng]
        self.add(eng, lambda: e.memset(ap, val), [], [ap])

    def dma(self, out, in_, q='sp', **kw):
        e = self.engs[q]
        self.add(q, lambda: e.dma_start(out=out, in_=in_, **kw), [in_], [out], dma=True)

    def dma_cast(self, out, in_):
        self.add('pool', lambda: self.nc.gpsimd.dma_start(out=out, in_=in_), [in_], [out], dma=True)

    @staticmethod
    def _ovl(a, b):
        return a[1] < b[2] and b[1] < a[2] and a[3] < b[4] and b[3] < a[4]

    def _dur(self, op):
        e = op['eng']
        n = op['n']
        if op['dma']:
            return 1200.0 if e == 'pool' else 120.0
        if e == 'pe':
            return max(100.0, (4.0 if op['f32'] else 1.0) * n / 2.35 + 8.0)
        if e == 'act':
            return (200.0 + n) / 1.2
        if e == 'dve':
            if op['psrc']:
                return (n + 200.0) / 0.96
            if op['nrd'] >= 2:
                return (n + 151.0) / 0.96
            return (n / 2.0 + 151.0) / 0.96
        if e == 'pool':
            return 60.0 + 1.95 * n
        return 100.0

    def emit(self, sem_ctx, reorder=True):
        import heapq
        ops_all = self.ops
        CE = ('pe', 'act', 'dve', 'pool')
        ALLE = list(CE) + ['sp']
        segs = [[]]
        final_op = None
        for op in ops_all:
            if op['fence']:
                segs.append([])
            elif op.get('final'):
                final_op = op
            else:
                segs[-1].append(op)
        order = []
        nid = 0
        for seg in segs:
            if not seg:
                continue
            state = {}
            for op in seg:
                op['id'] = nid
                nid += 1
                deps = set()
                me = op['id']
                key = op['eng'] if not op['dma'] else ('d', me)
                for b in op['r']:
                    covered = False
                    lst = state.setdefault(b[0], [])
                    for rec in lst:
                        if self._ovl(rec[0], b):
                            if rec[1] is not None:
                                deps.add(rec[1])
                            if b[4] == (1 << 30):
                                for id2, k2 in rec[2].items():
                                    if k2 != key:
                                        deps.add(id2)
                            rec[2][me] = key
                            rb = rec[0]
                            if rb[1] <= b[1] and b[2] <= rb[2] and rb[3] <= b[3] and b[4] <= rb[4]:
                                covered = True
                    if not covered:
                        lst.append([b, None, {me: key}])
                for b in op['w']:
                    lst = state.setdefault(b[0], [])
                    keep = []
                    for rec in lst:
                        if self._ovl(rec[0], b):
                            if rec[1] is not None:
                                deps.add(rec[1])
                            deps.update(rec[2].keys())
                            rb = rec[0]
                            if b[1] <= rb[1] and rb[2] <= b[2] and b[3] <= rb[3] and rb[4] <= b[4]:
                                continue
                        keep.append(rec)
                    keep.append([b, me, {}])
                    state[b[0]] = keep
                deps.discard(me)
                op['deps'] = deps
            base = seg[0]['id']
            if not reorder:
                order.append(list(seg))
                continue
            nseg = len(seg)
            succ = [[] for _ in range(nseg)]
            npred = [0] * nseg
            for op in seg:
                i = op['id'] - base
                npred[i] = len(op['deps'])
                for d in op['deps']:
                    succ[d - base].append(i)
            ready_t = [0.0] * nseg
            done_t = [0.0] * nseg
            use_cp = CP_DEFAULT == '1'
            blev = [0.0] * nseg
            if use_cp:
                for i in range(nseg - 1, -1, -1):
                    m_ = 0.0
                    for s in succ[i]:
                        if blev[s] > m_:
                            m_ = blev[s]
                    blev[i] = m_ + self._dur(seg[i]) + (2000.0 if seg[i]['dma'] else 0.0)
                for i in range(nseg):
                    blev[i] += seg[i].get('boost', 0.0)
            tfree = {e: 0.0 for e in ALLE}
            fut = {e: [] for e in ALLE}
            avl = {e: [] for e in ALLE}
            for i, op in enumerate(seg):
                if npred[i] == 0:
                    heapq.heappush(fut[op['eng']], (0.0, i))
            new = []
            act_grp = [None]
            LAT = 215.0
            left = nseg
            while left:
                best = None
                for e in ALLE:
                    f, a = fut[e], avl[e]
                    while f and f[0][0] <= tfree[e]:
                        k_ = heapq.heappop(f)[1]
                        heapq.heappush(a, (-blev[k_], k_))
                    if a:
                        cand = (tfree[e], a[0][1], e, True)
                    elif f:
                        cand = (f[0][0], f[0][1], e, False)
                    else:
                        continue
                    if best is None or cand[:2] < best[:2]:
                        best = cand
                st, i, e, from_a = best
                if from_a:
                    if e == 'act' and len(avl[e]) > 1:
                        g0 = seg[i].get('actg')
                        if g0 is not None and g0 != act_grp[0]:
                            cands = [kk for kk in avl[e] if seg[kk[1]].get('actg') in (None, act_grp[0])]
                            if cands:
                                i = min(cands)[1]
                        avl[e].remove((-blev[i], i))
                        heapq.heapify(avl[e])
                    else:
                        heapq.heappop(avl[e])
                else:
                    heapq.heappop(fut[e])
                op = seg[i]
                d = self._dur(op)
                if e == 'act':
                    g = op.get('actg')
                    if g is not None and g != act_grp[0]:
                        d += 1300.0
                        act_grp[0] = g
                tfree[e] = st + d
                fin = st + d
                if op['dma']:
                    fin = st + 1800.0 + op['nbytes'] / 180.0
                done_t[i] = fin
                new.append(op)
                left -= 1
                for s in succ[i]:
                    if fin + LAT > ready_t[s]:
                        ready_t[s] = fin + LAT
                    npred[s] -= 1
                    if npred[s] == 0:
                        heapq.heappush(fut[seg[s]['eng']], (ready_t[s], s))
            order.append(new)
            self.seg_ms = getattr(self, 'seg_ms', []) + [(nseg, round(max(done_t) / 1e3), {e: round(sum(self._dur(o) for o in seg if o['eng'] == e) / 1e3) for e in ALLE})]
        pos = {}
        engcnt = {e: 0 for e in CE}
        dmaidx = {}
        dmaq = {}
        NQ = {'sp': self.NDMA, 'pool': 12, 'act': 16}
        nq = {'sp': 0, 'pool': 0, 'act': 0}
        ndma = 0
        flat = []
        for si, seg in enumerate(order):
            for op in seg:
                op['seg'] = si
                if op['dma']:
                    q_ = op['eng']
                    dmaidx[op['id']] = nq[q_]
                    dmaq[op['id']] = q_
                    nq[q_] += 1
                    ndma += 1
                else:
                    pos[op['id']] = (op['eng'], engcnt[op['eng']])
                    engcnt[op['eng']] += 1
                flat.append(op)
        waited = {e: {p: -1 for p in CE} for e in ALLE}
        waited_d = {e: set() for e in ALLE}
        last_pos = {p: -1 for p in CE}
        need_e = set()
        dma_ids_in_order = {'sp': [], 'pool': [], 'act': []}
        fence_pending = {e: None for e in ALLE}
        cur_seg = 0
        for op in flat:
            if op['seg'] != cur_seg:
                cur_seg = op['seg']
                snap = (dict(last_pos), [d_ for q_ in NQ for d_ in dma_ids_in_order[q_][-NQ[q_]:]])
                for e in ALLE:
                    fence_pending[e] = snap
            eng = op['eng']
            best = {}
            dwait = []
            if fence_pending[eng] is not None:
                lp, dl = fence_pending[eng]
                fence_pending[eng] = None
                for p, s in lp.items():
                    if s >= 0 and not (eng == 'pe' and p == 'pe'):
                        best[p] = s
                for di in dl:
                    if di not in waited_d[eng]:
                        waited_d[eng].add(di)
                        dwait.append(di)
            if op['dma']:
                k = dmaidx[op['id']]
                q_ = op['eng']
                if k >= NQ[q_]:
                    di = dma_ids_in_order[q_][k - NQ[q_]]
                    if di not in waited_d[eng]:
                        waited_d[eng].add(di)
                        dwait.append(di)
            for d in op['deps']:
                if d in dmaidx:
                    if d not in waited_d[eng]:
                        waited_d[eng].add(d)
                        dwait.append(d)
                else:
                    p, s = pos[d]
                    if eng == 'pe' and p == 'pe':
                        continue
                    if s > best.get(p, -1):
                        best[p] = s
            ew = []
            for p, s in best.items():
                if s > waited[eng][p]:
                    waited[eng][p] = s
                    ew.append((p, s))
                    need_e.add((p, s))
            op['ew'] = ew
            op['dw'] = dwait
            if op['dma']:
                dma_ids_in_order[op['eng']].append(op['id'])
            else:
                last_pos[eng] = pos[op['id']][1]
        fin_e = [(p, s) for p, s in last_pos.items() if s > waited['sp'][p]]
        for ps_ in fin_e:
            need_e.add(ps_)
        fin_d = [d for q_ in NQ for d in dma_ids_in_order[q_][-NQ[q_]:] if d not in waited_d['sp']]
        cnt = {e: 0 for e in CE}
        evval = {}
        for op in flat:
            if not op['dma']:
                ps_ = pos[op['id']]
                if ps_ in need_e:
                    c = cnt[ps_[0]]
                    cnt[ps_[0]] += 1
                    evval[ps_] = ('%s_%d' % (ps_[0], c // self.ROT), c % self.ROT + 1)
        sems = {}

        def get_sem(name):
            if name not in sems:
                sems[name] = sem_ctx(name)
            return sems[name]

        def dma_sem(d):
            k = dmaidx[d]
            q_ = dmaq[d]
            return get_sem({'sp': 'dma_%d', 'pool': 'sdma_%d', 'act': 'adma_%d'}[q_] % (k % NQ[q_])), 16 * (k // NQ[q_] + 1)
        nwait = 0
        for op in flat:
            e = self.engs[op['eng']]
            for ps_ in op['ew']:
                sname, val = evval[ps_]
                e.wait_ge(get_sem(sname), val)
                nwait += 1
            for d in op['dw']:
                sm, val = dma_sem(d)
                e.wait_ge(sm, val)
                nwait += 1
            ins = op['fn']()
            if op['dma']:
                sm, _ = dma_sem(op['id'])
                ins.then_inc(sm, 16)
            else:
                ps_ = pos[op['id']]
                if ps_ in evval:
                    ins.then_inc(get_sem(evval[ps_][0]), 1)
        sp = self.engs['sp']
        for ps_ in fin_e:
            sname, val = evval[ps_]
            sp.wait_ge(get_sem(sname), val)
        for d in fin_d:
            sm, val = dma_sem(d)
            sp.wait_ge(sm, val)
        self.stats = dict(nops=len(flat), nwait=nwait, nsem=len(sems), cnt=dict(cnt), ndma=ndma)
        return self.stats


from contextlib import ExitStack
from concourse.bass_utils import run_bass_kernel_spmd

DM = 1024
DIN = 5648
C_Q, C_K, C_V, C_ZA, C_AB, C_CFV, C_CFG, C_ZB, C_GATE = 0, 512, 1024, 1536, 2048, 2064, 2576, 3088, 3600
EPS = 1e-6
NEG = -30000.0
DEPTH = 2
OPT = dict(cb='a', rebal='0', fuseA='1', zip='0')


def build(L, depth=DEPTH, taps=(), nph=99, cph=99, reorder=True):
    nc = bass.Bass("TRN2", target_bir_lowering=False)
    NT = L // 128
    TW = min(512, L)
    NTT = L // TW
    JT = TW // 128
    HAL = 16

    def din(name, shape):
        return nc.dram_tensor(name, shape, F32, kind="ExternalInput").ap()
    x_in = din("x", [L, DM])
    norm_w = din("norm_w", [depth, DM])
    w_in = din("w_in", [depth, DM, DIN])
    qkv_conv_w = din("qkv_conv_w", [depth, 5, 1536])
    a_log = din("a_log", [depth, 8])
    dt_bias = din("dt_bias", [depth, 8])
    dn_norm_w = din("dn_norm_w", [depth, 128])
    w_dn_out = din("w_dn_out", [depth, 512, DM])
    cf_conv_w = din("cf_conv_w", [depth, 31, 512])
    cf_conv_b = din("cf_conv_b", [depth, 512])
    cf_ln_w = din("cf_ln_w", [depth, 512])
    cf_ln_b = din("cf_ln_b", [depth, 512])
    w_cf_out = din("w_cf_out", [depth, 512, DM])
    gate_b = din("gate_b", [depth, 2048])
    w_out = din("w_out", [depth, DM, DM])
    final_norm_w = din("final_norm_w", [DM])
    out = nc.dram_tensor("out", [L, DM], F32, kind="ExternalOutput").ap()
    tapo = {}

    def scr(name, shape, dt):
        return nc.dram_tensor(name, shape, dt).ap()
    xs = scr("xs", [L, DM], F32)
    QtS = scr("QtS", [512, L], BF16)
    KtS = scr("KtS", [512, L], BF16)
    KS = scr("KS", [L, 512], BF16)
    VS = scr("VS", [L, 512], BF16)
    szaS = scr("szaS", [512, L], BF16)
    oS = scr("oS", [L, 512], F32)
    oSb = scr("oSb", [L, 512], F32)
    paS = scr("paS", [512, L], BF16)
    pbS = scr("pbS", [512, L], BF16)

    S = Sched(nc)
    top = ExitStack()

    uniq = {'n': 0}

    def sbt(es, name, shape, dt=F32):
        uniq['n'] += 1
        return es.enter_context(nc.sbuf_tensor("%s_u%d" % (name, uniq['n']), shape, dt))

    psF = [top.enter_context(nc.psum_tensor("psF%d" % i, [128, 512], F32)) for i in range(6)]
    psH = [top.enter_context(nc.psum_tensor("psH%d" % i, [128, 1024], BF16)) for i in range(2)]
    rot = {'f': 0, 'h': 0}

    cur = {'c': None}
    rotc = [0, 0]
    roth = [0, 0]

    def PF():
        c = cur['c']
        if c is not None:
            st_ = cur.get('stage')
            if OPT['cb'] == 'a':
                if st_ == 'scan':
                    return psF[3 * c + 2]
                rotc[c] = (rotc[c] + 1) % 2
                return psF[3 * c + rotc[c]]
            if OPT['cb'] == 'e':
                if st_ == 'scan':
                    return psF[4 + c]
                rotc[0] = (rotc[0] + 1) % 4
                return psF[rotc[0]]
            if OPT['cb'] == 'b':
                return psF[3 * c + {'prep': 0, 'neu': 1, 'scan': 2}[st_]]
            if OPT['cb'] == 'c':
                if st_ != 'scan':
                    return psF[3 * c]
                rotc[c] = (rotc[c] + 1) % 2
                return psF[3 * c + 1 + rotc[c]]
            rotc[c] = (rotc[c] + 1) % 3
            return psF[3 * c + rotc[c]]
        rot['f'] = (rot['f'] + 1) % 6
        return psF[rot['f']]

    def PH():
        c = cur['c']
        if c is not None:
            roth[c] ^= 1
            return psH[c][:, roth[c] * 512:roth[c] * 512 + 512]
        rot['h'] = (rot['h'] + 1) % 2
        return psH[rot['h']][:, 0:512]

    alt = {'i': 0}

    def AD():
        alt['i'] ^= 1
        return 'act' if alt['i'] else 'dve'

    ip_i = sbt(top, "ip_i", [128, 128], I32)
    ij_i = sbt(top, "ij_i", [128, 128], I32)
    ip = sbt(top, "ip", [128, 128])
    ij = sbt(top, "ij", [128, 128])
    cst = sbt(top, "cst", [128, 12, 128])
    ident_b = sbt(top, "ident_b", [128, 128], BF16)
    cstb = sbt(top, "cstb", [128, 4, 128], BF16)
    ones128_b = sbt(top, "ones128_b", [128, 128], BF16)
    ones1_b = sbt(top, "ones1_b", [128, 128], BF16)
    onesLN_b = sbt(top, "onesLN_b", [128, 128], BF16)
    hT = sbt(top, "hT", [128, 8, L], BF16)
    S.add('pool', lambda: nc.gpsimd.iota(ip_i[:], [[0, 128]], base=0, channel_multiplier=1), [], [ip_i[:]])
    S.add('pool', lambda: nc.gpsimd.iota(ij_i[:], [[1, 128]], base=0, channel_multiplier=0), [], [ij_i[:]])
    S.copy('dve', ip[:], ip_i[:])
    S.copy('dve', ij[:], ij_i[:])
    IDENT, LE, LT, GE, GT, NEGF, NEGB, BLKA, BLKB, SAME, T0, T1 = [cst[:, i, :] for i in range(12)]
    S.tt('dve', IDENT, ip[:], ij[:], ALU.is_equal)
    S.ts('dve', T0, ip[:], 64.0, None, ALU.is_ge)
    S.ts('dve', T1, ij[:], 64.0, None, ALU.is_ge)
    S.tt('dve', SAME, T0, T1, ALU.is_equal)
    for dst, op in ((LE, ALU.is_le), (LT, ALU.is_lt), (GE, ALU.is_ge), (GT, ALU.is_gt)):
        S.tt('dve', dst, ip[:], ij[:], op)
        S.tt('dve', dst, dst, SAME, ALU.mult)
    S.ts('dve', NEGF, GE, -1.0, -NEG, ALU.add, ALU.mult)
    S.ts('dve', NEGB, LE, -1.0, -NEG, ALU.add, ALU.mult)
    S.ts('dve', BLKA, ip[:], 64.0, None, ALU.is_lt)
    S.ts('dve', BLKB, ip[:], 64.0, None, ALU.is_ge)
    S.copy('dve', ident_b[:], IDENT)
    GTb, LTb, NEGFb, NEGBb = [cstb[:, i, :] for i in range(4)]
    for dst_, src_ in ((GTb, GT), (LTb, LT), (NEGFb, NEGF), (NEGBb, NEGB)):
        S.copy('dve', dst_, src_)
    S.memset('dve', ones128_b[:], 128.0)
    S.memset('dve', ones1_b[:], 1.0)
    S.memset('dve', onesLN_b[:], 1.0 / 512.0)

    def load_w(es_buf, src, nk, ncols, wst, wdst, eng='pool'):
        srcv = src.rearrange("(c p) n -> p c n", p=128)
        step = 256 if ncols > 256 else ncols
        for c0 in range(0, ncols, step):
            S.dma_cast(wdst[:, 0:nk, c0:c0 + step], srcv[:, :, c0:c0 + step])

    def proj(wb, col0, t0, tw, ps):
        for c in range(8):
            S.mm(ps[:, 0:tw], wb[:, c, col0:col0 + 128], hT[:, c, t0:t0 + tw], start=(c == 0), stop=(c == 7))

    def rsqrt_to(es_tmp, dst, src, eps, scale=1.0, wide=False):
        if wide:
            S.act(dst, src, AF.Ln, bias=eps, scale=scale)
            S.act(dst, dst, AF.Exp, scale=-0.5)
        else:
            S.act(dst, src, AF.Sqrt, bias=eps, scale=scale)
            S.add('dve', lambda: nc.vector.reciprocal(dst, dst), [dst], [dst])

    for l in range(depth):
        x_cur = x_in if l == 0 else xs
        last = (l == depth - 1)
        lay = ExitStack()
        stA = sbt(lay, "stA%d" % l, [128, 128])
        stB = sbt(lay, "stB%d" % l, [128, 128])
        prmA = sbt(lay, "prmA%d" % l, [128, 128])
        prmB = sbt(lay, "prmB%d" % l, [128, 128])
        alog_bc = sbt(lay, "alog%d" % l, [128, 8])
        dtb_bc = sbt(lay, "dtb%d" % l, [128, 8])
        negA = sbt(lay, "negA%d" % l, [128, 8])
        S.memset('dve', stA[:], 0.0)
        S.memset('dve', stB[:], 0.0)
        S.dma(stA[0:8, :], norm_w[l].rearrange("(c p) -> c p", p=128))
        S.dma(stA[8:68, :], qkv_conv_w[l].rearrange("k (c p) -> (k c) p", p=128))
        S.dma(stA[68:69, :], dn_norm_w[l].rearrange("(c p) -> c p", p=128))
        S.dma(stA[69:73, :], cf_conv_b[l].rearrange("(c p) -> c p", p=128))
        S.dma(stA[73:77, :], cf_ln_w[l].rearrange("(c p) -> c p", p=128))
        S.dma(stA[77:81, :], cf_ln_b[l].rearrange("(c p) -> c p", p=128))
        S.dma(stA[81:97, :], gate_b[l].rearrange("(c p) -> c p", p=128))
        S.dma(stB[0:124, :], cf_conv_w[l].rearrange("k (c p) -> (k c) p", p=128))
        S.dma(alog_bc[:], a_log[l].partition_broadcast(128))
        S.dma(dtb_bc[:], dt_bias[l].partition_broadcast(128))
        ps = PF()
        S.tr(ps[:, 0:128], stA[:], IDENT)
        S.copy('dve', prmA[:], ps[:, 0:128])
        ps = PF()
        S.tr(ps[:, 0:128], stB[:], IDENT)
        S.copy('dve', prmB[:], ps[:, 0:128])
        S.act(negA[:], alog_bc[:], AF.Exp)
        S.ts('dve', negA[:], negA[:], -1.0, None, ALU.mult)

        def normw(c): return prmA[:, c:c + 1]
        def qkvw(k, cc): return prmA[:, 8 + k * 12 + cc: 9 + k * 12 + cc]
        dnw = prmA[:, 68:69]
        def cfb(cc): return prmA[:, 69 + cc:70 + cc]
        def lnw(cc): return prmA[:, 73 + cc:74 + cc]
        def lnb(cc): return prmA[:, 77 + cc:78 + cc]
        def gateb(c): return prmA[:, 81 + c:82 + c]
        def cfw(k, cc): return prmB[:, k * 4 + cc:k * 4 + cc + 1]

        tb = ExitStack()
        TAB = sbt(tb, "TAB%d" % l, [128, 8, NT * 8])
        GHL = sbt(tb, "GHL%d" % l, [128, 2, NT * 8], BF16)
        Gt, BETAt, EGCt, BEGCt, EGLGt, DLAt, DLBt = [TAB[:, i, :] for i in range(7)]
        esAB = ExitStack()
        fuseA = OPT['fuseA'] != '0'
        for es in ((esAB,) if not (fuseA and l > 0) else ()):
            nw_bc = sbt(esAB, "nwbc%d" % l, [128, DM])
            S.dma(nw_bc[:], norm_w[l].partition_broadcast(128))
            xb = [sbt(es, "xb%d" % i, [128, DM]) for i in range(4)]
            junk = sbt(es, "junkA", [128, DM])
            xsb = [sbt(es, "xsb%d" % i, [128, DM], BF16) for i in range(4)]
            ssA = sbt(es, "ssA", [128, 8])
            for i in range(NT):
                xt = xb[i % 4]
                xq = xsb[i % 4]
                sq = ssA[:, (i % 4) * 2:(i % 4) * 2 + 1]
                rs = ssA[:, (i % 4) * 2 + 1:(i % 4) * 2 + 2]
                S.dma(xt[:], x_cur[i * 128:(i + 1) * 128, :])
                S.act(junk[:], xt[:], AF.Square, accum_out=sq)
                rsqrt_to(es, rs, sq, EPS, 1.0 / DM)
                S.stt('dve', xq[:], xt[:], rs, nw_bc[:], ALU.mult, ALU.mult)
                for half in range(2):
                    ph = PH()
                    for k in range(4):
                        c = half * 4 + k
                        S.tr(ph[:, k * 128:(k + 1) * 128], xq[:, c * 128:(c + 1) * 128], ident_b[:])
                    S.copy('act' if half == 0 else 'dve', hT[:, half * 4:half * 4 + 4, i * 128:(i + 1) * 128],
                           ph[:, 0:512].rearrange("p (k t) -> p k t", t=128))
        if nph <= 1:
            break
        if 'hT' in taps and l == 0:
            tapo['hT'] = nc.dram_tensor("tap_hT", [128, 8, L], BF16, kind="ExternalOutput").ap()
            S.dma(tapo['hT'], hT[:])

        for es in (esAB,):
            wst = None
            wbf = [sbt(es, "wbfB%d" % i, [128, 8, 512], BF16) for i in range(2)]
            rb = [sbt(es, "rbB%d" % i, [128, L + 2 * HAL], BF16) for i in range(2)]
            dg = [sbt(es, "dgB%d" % i, [128, 5, 128], BF16) for i in range(2)]
            sil = [sbt(es, "silB%d" % i, [128, TW]) for i in range(4)]
            sqb = [sbt(es, "sqB%d" % i, [128, TW], BF16) for i in range(4)]
            sd = [sbt(es, "sdB%d" % i, [128, TW]) for i in range(4)]
            qn = [sbt(es, "qnB%d" % i, [128, TW], BF16) for i in range(4)]
            tok = [sbt(es, "tokB%d" % i, [128, JT, 128], BF16) for i in range(4)]
            for r in rb:
                S.memset('pool', r[:, 0:HAL], 0.0)
                S.memset('pool', r[:, HAL + L:], 0.0)
            n = 0
            for grp, (col0, kind) in enumerate(((C_Q, 'q'), (C_K, 'k'), (C_V, 'v'))):
                wb = wbf[grp % 2]
                load_w(es, w_in[l][:, col0:col0 + 512], 8, 512, wst, wb)
                for h in range(4):
                    cc = grp * 4 + h
                    r = rb[cc % 2]
                    d = dg[cc % 2]
                    for k in range(5):
                        S.act(d[:, k, :], IDENT, AF.Copy, scale=qkvw(k, cc))
                    for tt in range(NTT):
                        ps = PF()
                        proj(wb, h * 128, tt * TW, TW, ps)
                        S.copy('dve', r[:, HAL + tt * TW:HAL + (tt + 1) * TW], ps[:, 0:TW])
                    for tt in range(NTT):
                        n += 1
                        ps = PF()
                        for k in range(5):
                            o = HAL + tt * TW + k - 2
                            S.mm(ps[:, 0:TW], d[:, k, :], r[:, o:o + TW], start=(k == 0), stop=(k == 4))
                        tsl = slice(tt * TW, (tt + 1) * TW)
                        if kind == 'v':
                            vq = qn[n % 4]
                            S.act(vq[:], ps[:, 0:TW], AF.Silu)
                            src_bf = vq
                        else:
                            sl = sil[n % 4]
                            S.act(sl[:], ps[:, 0:TW], AF.Silu)
                            S.tt('pool', sqb[n % 4][:], sl[:], sl[:], ALU.mult)
                            ps3 = PF()
                            S.mm(ps3[:, 0:TW], (ones128_b if kind == 'q' else ones1_b)[:], sqb[n % 4][:])
                            sdd = sd[n % 4]
                            rsqrt_to(es, sdd[:], ps3[:, 0:TW], EPS * (128.0 if kind == 'q' else 1.0), wide=True)
                            src_bf = qn[n % 4]
                            S.tt('dve', src_bf[:], sl[:], sdd[:], ALU.mult)
                            S.dma((QtS if kind == 'q' else KtS)[h * 128:(h + 1) * 128, tsl], src_bf[:], q='act')
                        if kind != 'q':
                            ph = PH()
                            for j in range(JT):
                                S.tr(ph[:, j * 128:(j + 1) * 128], src_bf[:, j * 128:(j + 1) * 128], ident_b[:])
                            tk = tok[n % 4]
                            S.copy('dve', tk[:], ph[:, 0:JT * 128].rearrange("p (j d) -> p j d", d=128))
                            dstS = KS if kind == 'k' else VS
                            S.dma(dstS[tsl, h * 128:(h + 1) * 128].rearrange("(j t) d -> t j d", t=128), tk[:], q='act')
            wb = wbf[1]
            load_w(es, w_in[l][:, C_ZA:C_ZA + 512], 8, 512, wst, wb)
            for h in range(4):
                for tt in range(NTT):
                    n += 1
                    ps = PF()
                    proj(wb, h * 128, tt * TW, TW, ps)
                    S.act(qn[n % 4][:], ps[:, 0:TW], AF.Silu)
                    S.dma(szaS[h * 128:(h + 1) * 128, tt * TW:(tt + 1) * TW], qn[n % 4][:], q='act')
        if nph <= 2:
            break

        for es in (esAB,):
            wst = None
            b3_first = len(S.ops)
            w16 = sbt(es, "w16", [128, 8, 16], BF16)
            AB = sbt(es, "AB", [128, NT, 16])
            tmp = sbt(es, "tmp3", [128, 4, NT * 8])
            load_w(es, w_in[l][:, C_AB:C_AB + 16], 8, 16, wst, w16, eng='dve')
            for sc in range(NT):
                ps = PF()
                for c in range(8):
                    S.mm(ps[:, 0:16], hT[:, c, sc * 128:(sc + 1) * 128], w16[:, c, :], start=(c == 0), stop=(c == 7))
                S.copy(AD(), AB[:, sc, :], ps[:, 0:16])

            def v3(ap2):
                return ap2.rearrange("p (s h) -> p s h", h=8)
            xg, ax, e1, mx = [tmp[:, i, :] for i in range(4)]
            S.tt('dve', v3(xg), AB[:, :, 0:8], bc_mid(dtb_bc[:], NT), ALU.add)
            S.act(ax, xg, AF.Abs)
            S.act(e1, ax, AF.Exp, scale=-1.0)
            S.act(e1, e1, AF.Ln, bias=1.0)
            S.ts('dve', mx, xg, 0.0, None, ALU.max)
            S.tt('dve', mx, mx, e1, ALU.add)
            S.tt('dve', v3(Gt), v3(mx), bc_mid(negA[:], NT), ALU.mult)
            S.act(v3(BETAt), AB[:, :, 8:16], AF.Sigmoid)
            W = NT * 8

            def cum(maskF, maskB, dst):
                psa = PF()
                psb = PF()
                S.mm(psa[:, 0:W], maskF, Gt)
                S.mm(psb[:, 0:W], maskB, Gt)
                S.act(v3(dst)[:, :, 0:4], v3(psa[:, 0:W])[:, :, 0:4], AF.Exp)
                S.act(v3(dst)[:, :, 4:8], v3(psb[:, 0:W])[:, :, 4:8], AF.Exp)
            cum(LE, GE, EGCt)
            cum(GT, LT, EGLGt)
            cum(BLKA, BLKA, DLAt)
            cum(BLKB, BLKB, DLBt)
            S.tt('dve', BEGCt, BETAt, EGCt, ALU.mult)
            S.copy('dve', GHL[:, 0, :], Gt)
            S.tt('dve', TAB[:, 7, :], Gt, GHL[:, 0, :], ALU.subtract)
            S.copy('dve', GHL[:, 1, :], TAB[:, 7, :])
        for op_ in S.ops[b3_first:]:
            if not op_.get('fence'):
                op_['boost'] = 1e6
        if nph <= 3:
            break
        esAB.close()
        S.fence()
        if 'tab' in taps and l == 0:
            tapo['tab'] = nc.dram_tensor("tap_tab", [128, 8, NT * 8], F32, kind="ExternalOutput").ap()
            S.dma(tapo['tab'], TAB[:])

        with ExitStack() as es:
            def t2(name, shape, dt=BF16, nb=2):
                return [sbt(es, "%s%d" % (name, i), shape, dt) for i in range(nb)]
            KtT = t2("cKt", [128, 4, 128], BF16, 4); QtT = t2("cQt", [128, 4, 128], BF16, 4)
            KtokT = t2("cKtok", [128, 4, 128], BF16, 4); VtokT = t2("cVtok", [128, 4, 128], BF16, 4)
            gA = t2("cgA", [128, 4, 128], BF16)
            gAl = t2("cgAl", [128, 4, 128], BF16)
            Dm = t2("cD", [128, 4, 128], F32)
            attn = t2("cattn", [128, 4, 128]); Gm = t2("cGm", [128, 4, 128], F32)
            Lf = t2("cLf", [128, 4, 128], F32)
            Ub = t2("cU", [128, 4, 128], BF16, 12); Tb = t2("cT", [128, 4, 128], BF16, 12)
            Pb = t2("cP", [128, 4, 128], BF16, 12)
            attnT = t2("cattnT", [128, 4, 128], BF16, 4)
            KBG = t2("cKBG", [128, 4, 128], BF16, 4); BV = t2("cBV", [128, 4, 128], BF16, 4); KDEC = t2("cKDEC", [128, 4, 128], BF16, 4)
            NKC = t2("cNKC", [128, 4, 128], BF16, 4)
            vnewL = [t2("cvnew%d" % q, [128, 4, 128]) for q in range(2)]
            SfL = t2("cSf", [128, 4, 128], F32); StmpL = t2("cStmp", [128, 4, 128], F32)
            SbL = [t2("cSb%d" % q, [128, 4, 128]) for q in range(2)]
            ob = t2("cob", [128, 4, 128], F32, 4)
            sbiL = [0, 0]
            for q in range(2):
                S.memset('dve', SfL[q][:], 0.0)
                S.memset('dve', SbL[q][0][:], 0.0)
                S.memset('dve', SbL[q][1][:], 0.0)
            for it in range(NT):
                grp = []
                for dr in range(2):
                    if dr == 0:
                        S.capture()
                    else:
                        grp.append(S.release())
                        S.capture()
                    MA, MB, NEGM = (LE, GT, NEGF) if dr == 0 else (GE, LT, NEGB)
                    STRICT = MB
                    MBb, NEGMb = (GTb, NEGFb) if dr == 0 else (LTb, NEGBb)
                    sc = it if dr == 0 else NT - 1 - it
                    b = dr
                    b4 = dr * 2 + it % 2
                    cur['c'] = dr
                    cur['stage'] = 'prep'
                    Sf, Stmp, Sb, vnew = SfL[dr], StmpL[dr], SbL[dr], vnewL[dr]
                    sbi = sbiL[dr]
                    tsl = slice(sc * 128, (sc + 1) * 128)
                    hs = slice(sc * 8 + dr * 4, sc * 8 + dr * 4 + 4)
                    Kt, Qt, Ktok, Vtok = KtT[b4], QtT[b4], KtokT[b4], VtokT[b4]
                    S.dma(Kt[:], KtS[:, tsl].rearrange("(h d) t -> d h t", d=128))
                    S.dma(Qt[:], QtS[:, tsl].rearrange("(h d) t -> d h t", d=128))
                    S.dma(Ktok[:], KS[tsl, :].rearrange("t (h d) -> t h d", d=128))
                    S.dma(Vtok[:], VS[tsl, :].rearrange("t (h d) -> t h d", d=128))
                    S.tt('pool', gA[b][:], bc_mid(MA, 4), bc_last(GHL[:, 0, hs], 128), ALU.mult)
                    S.tt('pool', gAl[b][:], bc_mid(MA, 4), bc_last(GHL[:, 1, hs], 128), ALU.mult)
                    pd = PF()
                    for h in range(4):
                        S.mm(pd[:, h * 128:(h + 1) * 128], gA[b][:, h, :], MBb, start=True, stop=False)
                        S.mm(pd[:, h * 128:(h + 1) * 128], gAl[b][:, h, :], MBb, start=False, stop=False)
                        S.mm(pd[:, h * 128:(h + 1) * 128], ident_b[:], NEGMb, start=False, stop=True)
                    S.act(Dm[b][:], pd[:].rearrange("p (h j) -> p h j", j=128), AF.Exp)
                    if cph <= 1:
                        continue
                    pg = PF()
                    pq = PF()
                    for h in range(4):
                        S.mm(pg[:, h * 128:(h + 1) * 128], Kt[:, h, :], Kt[:, h, :])
                        S.mm(pq[:, h * 128:(h + 1) * 128], Qt[:, h, :], Kt[:, h, :])
                    v4 = lambda p: p[:].rearrange("p (h j) -> p h j", j=128)
                    S.tt('dve', attn[b][:], v4(pq), Dm[b][:], ALU.mult)
                    S.tt('dve', Gm[b][:], v4(pg), bc_mid(STRICT, 4), ALU.mult)
                    S.tt('pool', Lf[b][:], Gm[b][:], Dm[b][:], ALU.mult)
                    U0 = Ub[b4 * 3]
                    S.tt('dve' if OPT['rebal'] != '0' else 'pool', U0[:], Lf[b][:], bc_last(BETAt[:, hs], 128), ALU.mult)
                    if cph <= 2:
                        continue
                    pl = PH()
                    pa_ = PH()
                    for h in range(4):
                        S.tr(pl[:, h * 128:(h + 1) * 128], U0[:, h, :], ident_b[:])
                        S.tr(pa_[:, h * 128:(h + 1) * 128], attn[b][:, h, :], ident_b[:])
                    vh = lambda p: p.rearrange("p (h j) -> p h j", j=128)
                    T0_ = Tb[b4 * 3]
                    P0_ = Pb[b4 * 3]
                    S.copy('act', T0_[:], vh(pl))
                    S.tt('dve', P0_[:], bc_mid(IDENT, 4), vh(pl), ALU.subtract)
                    S.copy('act', attnT[b4][:], vh(pa_))
                    Uc, Tc, Pc = U0, T0_, P0_
                    cur['stage'] = 'neu' if OPT['cb'] == 'b' else 'prep'
                    for j in range(1, 6):
                        Un, Tn, Pn = Ub[b4 * 3 + j % 3], Tb[b4 * 3 + j % 3], Pb[b4 * 3 + j % 3]
                        pu = PF()
                        for h in range(4):
                            S.mm(pu[:, h * 128:(h + 1) * 128], Tc[:, h, :], Uc[:, h, :])
                        S.copy('act', Un[:], v4(pu))
                        if j < 5:
                            pt = PF()
                            for h in range(4):
                                S.mm(pt[:, h * 128:(h + 1) * 128], Uc[:, h, :], Tc[:, h, :])
                            S.copy('act', Tn[:], v4(pt))
                        pp = PF()
                        for h in range(4):
                            S.mm(pp[:, h * 128:(h + 1) * 128], Un[:, h, :], Pc[:, h, :])
                        S.tt('dve', Pn[:], v4(pp), Pc[:], ALU.add)
                        Uc, Tc, Pc = Un, Tn, Pn
                    if cph <= 3:
                        continue
                    AT = Pc
                    cur['stage'] = 'prep'
                    S.tt('pool', KBG[b4][:], Ktok[:], bc_last(BEGCt[:, hs], 128), ALU.mult)
                    S.tt('dve' if OPT['rebal'] == '2' else 'pool', BV[b4][:], Vtok[:], bc_last(BETAt[:, hs], 128), ALU.mult)
                    S.tt('pool', KDEC[b4][:], Ktok[:], bc_last(EGLGt[:, hs], 128), ALU.mult)
                    pk = PF()
                    for h in range(4):
                        S.mm(pk[:, h * 128:(h + 1) * 128], KBG[b4][:, h, :], AT[:, h, :])
                    S.ts('dve', NKC[b4][:], v4(pk), -1.0, None, ALU.mult)
                    if cph <= 4:
                        continue
                    o_sc = ob[b4]
                    cur['stage'] = 'scan'
                    for blk in ((0, 1) if dr == 0 else (1, 0)):
                        r = slice(blk * 64, blk * 64 + 64)
                        DL = DLAt if blk == 0 else DLBt
                        Scur = Sb[sbi % 2]
                        pqs = PF()
                        for h in range(4):
                            S.mm(pqs[:, h * 128:(h + 1) * 128], Qt[:, h, :], Scur[:, h, :])
                        S.tt('dve', o_sc[r], pqs[r].rearrange("p (h j) -> p h j", j=128),
                             bc_last(EGCt[r, hs], 128), ALU.mult)
                        pv = PF()
                        for h in range(4):
                            S.mm(pv[:, h * 128:(h + 1) * 128], AT[:, h, :], BV[b4][:, h, :], start=True, stop=False)
                            S.mm(pv[:, h * 128:(h + 1) * 128], NKC[b4][:, h, :], Scur[:, h, :], start=False, stop=True)
                        vn = vnew[sbi % 2]
                        S.copy('act', vn[r], pv[r].rearrange("p (h j) -> p h j", j=128))
                        pav = PF()
                        for h in range(4):
                            S.mm(pav[:, h * 128:(h + 1) * 128], attnT[b4][r, h, :], vn[r, h, :])
                        S.tt('dve', o_sc[r], o_sc[r], pav[r].rearrange("p (h j) -> p h j", j=128), ALU.add)
                        pds = PF()
                        for h in range(4):
                            S.mm(pds[:, h * 128:(h + 1) * 128], KDEC[b4][r, h, :], vn[r, h, :])
                        S.tt('pool', Stmp[:], Sf[:], bc_last(DL[:, hs], 128), ALU.mult)
                        S.tt('dve', Sf[:], Stmp[:], v4(pds), ALU.add)
                        sbi += 1
                        S.copy('act', Sb[sbi % 2][:], Sf[:])
                    sbiL[dr] = sbi
                    if cph <= 5:
                        continue
                    S.dma((oS if dr == 0 else oSb)[tsl, :].rearrange("t (h e) -> t h e", e=128), o_sc[:], q='act')
                grp.append(S.release())
                if OPT.get('zip', '1') == '1':
                    S.merge(*grp)
                else:
                    for g_ in grp:
                        S.ops.extend(g_)
                cur['c'] = None
        if nph <= 4:
            break
        S.fence()
        tb.close()
        esW = ExitStack()
        wo = sbt(esW, "wo", [128, 8, 1024], BF16)
        wdn = sbt(esW, "wdn", [128, 4, 1024], BF16)
        wcf = sbt(esW, "wcf", [128, 4, 1024], BF16)

        with ExitStack() as es:
            ucv = sbt(es, "ucv", [128, 4, L], BF16)
            def t3(name, shape, dt=BF16, nb=2):
                return [sbt(es, "%s%d" % (name, i), shape, dt) for i in range(nb)]
            oF = t3("coF", [128, 4, 128], F32)
            oB = t3("coB", [128, 4, 128], F32)
            ssq = t3("cssq", [128, 8], F32)
            onb = t3("conb", [128, 4, 128])
            szat = t3("cszat", [128, 4, 128])
            pat = t3("cpat", [128, 4, 128])
            junkC = t3("cjunk", [128, 128], F32)
            for sc in range(NT if cph > 5 else 0):
                b = sc % 2
                tsl = slice(sc * 128, (sc + 1) * 128)
                S.dma(oF[b][:], oS[tsl, :].rearrange("t (h e) -> t h e", e=128))
                S.dma(oB[b][:], oSb[tsl, :].rearrange("t (h e) -> t h e", e=128))
                S.dma(szat[b][:], szaS[:, tsl].rearrange("(h e) t -> e h t", e=128))
                S.tt('pool', oF[b][:], oF[b][:], oB[b][:], ALU.add)
                for h in range(4):
                    S.act(junkC[b][:], oF[b][:, h, :], AF.Square, accum_out=ssq[b][:, h:h + 1])
                rsqrt_to(es, ssq[b][:, 4:8], ssq[b][:, 0:4], EPS, 1.0 / 128.0)
                S.tt('dve', onb[b][:], oF[b][:], bc_last(ssq[b][:, 4:8], 128), ALU.mult)
                po = PH()
                for h in range(4):
                    S.tr(po[:, h * 128:(h + 1) * 128], onb[b][:, h, :], ident_b[:])
                S.stt('dve', pat[b][:], po.rearrange("p (h j) -> p h j", j=128), dnw, szat[b][:], ALU.mult, ALU.mult)
                S.dma(paS[:, tsl].rearrange("(h e) t -> e h t", e=128), pat[b][:], q='act')
            with ExitStack() as es1:
                wst = None
                wbV = sbt(es1, "wbV", [128, 8, 512], BF16)
                wbG = sbt(es1, "wbG", [128, 8, 512], BF16)
                rbu = [sbt(es1, "rbu%d" % i, [128, L + 2 * HAL], BF16) for i in range(2)]
                dg31 = [sbt(es1, "dg31_%d" % i, [128, 23, 128], BF16) for i in range(2)]
                sg = [sbt(es1, "sgD%d" % i, [128, TW]) for i in range(2)]
                accD = [sbt(es1, "accD%d" % i, [128, TW]) for i in range(2)]
                for r in rbu:
                    S.memset('pool', r[:, 0:HAL], 0.0)
                    S.memset('pool', r[:, HAL + L:], 0.0)
                load_w(es1, w_in[l][:, C_CFV:C_CFV + 512], 8, 512, wst, wbV)
                load_w(es1, w_in[l][:, C_CFG:C_CFG + 512], 8, 512, wst, wbG)
                load_w(es1, w_dn_out[l], 4, 1024, None, wdn)
                load_w(es1, w_cf_out[l], 4, 1024, None, wcf)
                load_w(es1, w_out[l], 8, 1024, None, wo)
                n = 0
                for cc in range(4):
                    r = rbu[cc % 2]
                    d = dg31[cc % 2]
                    for k in range(8, 31):
                        S.act(d[:, k - 8, :], IDENT, AF.Copy, scale=cfw(k, cc))
                    for tt in range(NTT):
                        n += 1
                        pv = PF()
                        pg = PF()
                        proj(wbV, cc * 128, tt * TW, TW, pv)
                        proj(wbG, cc * 128, tt * TW, TW, pg)
                        S.act(sg[n % 2][:], pg[:, 0:TW], AF.Sigmoid)
                        S.tt('dve', r[:, HAL + tt * TW:HAL + (tt + 1) * TW], pv[:, 0:TW], sg[n % 2][:], ALU.mult)
                    NDV = 8
                    for tt in range(NTT):
                        ac = accD[(cc * NTT + tt) % 2]
                        for k in range(NDV):
                            o = HAL + tt * TW + k - 15
                            if k == 0:
                                S.ts('dve', ac[:], r[:, o:o + TW], cfw(k, cc), None, ALU.mult)
                            else:
                                S.stt('dve', ac[:], r[:, o:o + TW], cfw(k, cc), ac[:], ALU.mult, ALU.add)
                        ps = PF()
                        for k in range(NDV, 31):
                            o = HAL + tt * TW + k - 15
                            S.mm(ps[:, 0:TW], d[:, k - NDV, :], r[:, o:o + TW], start=(k == NDV), stop=(k == 30))
                        S.stt('dve', ucv[:, cc, tt * TW:(tt + 1) * TW], ps[:, 0:TW], cfb(cc), ac[:], ALU.add, ALU.add)
            S.fence()
            with ExitStack() as es2:
                wst = None
                wbZ = sbt(es2, "wbZ", [128, 8, 512], BF16)
                sqc = [sbt(es2, "sqc%d" % i, [128, 4, TW], BF16) for i in range(2)]
                mean = [sbt(es2, "mean%d" % i, [128, TW]) for i in range(2)]
                msq = [sbt(es2, "msq%d" % i, [128, TW]) for i in range(2)]
                rsd = [sbt(es2, "rsd%d" % i, [128, TW]) for i in range(2)]
                szb = [sbt(es2, "szb%d" % i, [128, TW]) for i in range(2)]
                t1 = [sbt(es2, "t1_%d" % i, [128, TW]) for i in range(2)]
                t3 = [sbt(es2, "t3_%d" % i, [128, TW]) for i in range(2)]
                pbt = [sbt(es2, "pbt%d" % i, [128, TW], BF16) for i in range(2)]
                load_w(es2, w_in[l][:, C_ZB:C_ZB + 512], 8, 512, wst, wbZ)
                n = 0
                for tt in range(NTT):
                    b = tt % 2
                    tsl = slice(tt * TW, (tt + 1) * TW)
                    for cc in range(4):
                        S.act(sqc[b][:, cc, :], ucv[:, cc, tsl], AF.Square)
                    pm = PF()
                    pq = PF()
                    for cc in range(4):
                        S.mm(pm[:, 0:TW], onesLN_b[:], ucv[:, cc, tsl], start=(cc == 0), stop=(cc == 3))
                    for cc in range(4):
                        S.mm(pq[:, 0:TW], onesLN_b[:], sqc[b][:, cc, :], start=(cc == 0), stop=(cc == 3))
                    S.copy('act', mean[b][:], pm[:, 0:TW])
                    S.tt('pool', msq[b][:], mean[b][:], mean[b][:], ALU.mult)
                    S.tt('dve', msq[b][:], pq[:, 0:TW], msq[b][:], ALU.subtract)
                    rsqrt_to(es2, rsd[b][:], msq[b][:], EPS, wide=True)
                    for cc in range(4):
                        n += 1
                        m = n % 2
                        pz = PF()
                        proj(wbZ, cc * 128, tt * TW, TW, pz)
                        S.act(szb[m][:], pz[:, 0:TW], AF.Silu)
                        S.tt('dve', t1[m][:], ucv[:, cc, tsl], mean[b][:], ALU.subtract)
                        S.tt('pool', t1[m][:], t1[m][:], rsd[b][:], ALU.mult)
                        S.act(t3[m][:], t1[m][:], AF.Silu, bias=lnb(cc), scale=lnw(cc))
                        S.tt('dve', pbt[m][:], t3[m][:], szb[m][:], ALU.mult)
                        S.dma(pbS[cc * 128:(cc + 1) * 128, tsl], pbt[m][:], q='act')
        if nph <= 5:
            break
        S.fence()

        with ExitStack() as es:
            wst = [None, None]
            wg = sbt(es, "wg", [128, 8, 2048], BF16)
            pat = [sbt(es, "pat4_%d" % i, [128, 4, TW], BF16) for i in range(2)]
            pbt = [sbt(es, "pbt4_%d" % i, [128, 4, TW], BF16) for i in range(2)]
            sg0 = [sbt(es, "sg0_%d" % i, [128, TW]) for i in range(2)]
            sg1 = [sbt(es, "sg1_%d" % i, [128, TW]) for i in range(2)]
            yT = [sbt(es, "yT%d" % i, [128, 8, TW], BF16) for i in range(2)]
            xt4 = [sbt(es, "xt4_%d" % i, [128, DM]) for i in range(4)]
            xn = xt4
            junk = sbt(es, "junk4", [128, DM], BF16)
            ss4 = sbt(es, "ss4", [128, 8])
            if last:
                fnw_bc = sbt(es, "fnw_bc", [128, DM])
                S.dma(fnw_bc[:], final_norm_w.partition_broadcast(128))
            elif fuseA:
                nwF = sbt(es, "nwF", [128, DM])
                S.dma(nwF[:], norm_w[l + 1].partition_broadcast(128))
                xqF = sbt(es, "xqF", [128, DM], BF16)
            def ldg(q, eng):
                load_w(es, w_in[l][:, C_GATE + q * 256:C_GATE + (q + 1) * 256], 8, 256, wst[q % 2], wg[:, :, q * 256:(q + 1) * 256], eng=eng)
            for q in range(4):
                ldg(q, 'pool')
                ldg(q + 4, 'act')
            n = 0
            xi = 0
            for tt in range(NTT):
                b = tt % 2
                tsl = slice(tt * TW, (tt + 1) * TW)
                S.dma(pat[b][:], paS[:, tsl].rearrange("(h e) t -> e h t", e=128))
                S.dma(pbt[b][:], pbS[:, tsl].rearrange("(h e) t -> e h t", e=128))
                for c in range(8):
                    n += 1
                    m = n % 2
                    pA = PF()
                    pB = PF()
                    p0 = PF()
                    p1 = PF()
                    cs = slice(c * 128, (c + 1) * 128)
                    for h in range(4):
                        S.mm(pA[:, 0:TW], wdn[:, h, cs], pat[b][:, h, :], start=(h == 0), stop=(h == 3))
                    for h in range(4):
                        S.mm(pB[:, 0:TW], wcf[:, h, cs], pbt[b][:, h, :], start=(h == 0), stop=(h == 3))
                    proj(wg, c * 128, tt * TW, TW, p0)
                    proj(wg, 1024 + c * 128, tt * TW, TW, p1)
                    S.act(sg0[m][:], p0[:, 0:TW], AF.Sigmoid, bias=gateb(c))
                    S.act(sg1[m][:], p1[:, 0:TW], AF.Sigmoid, bias=gateb(8 + c))
                    S.tt('dve', sg0[m][:], pA[:, 0:TW], sg0[m][:], ALU.mult)
                    S.tt('dve', sg1[m][:], pB[:, 0:TW], sg1[m][:], ALU.mult)
                    S.tt('pool', yT[b][:, c, :], sg0[m][:], sg1[m][:], ALU.add)
                for j in range(JT):
                    xi += 1
                    m = xi % 4
                    rows = slice(tt * TW + j * 128, tt * TW + (j + 1) * 128)
                    S.dma(xt4[m][:], x_cur[rows, :])
                    for half in range(2):
                        po = PF()
                        for c in range(8):
                            S.mm(po[:, 0:512], yT[b][:, c, j * 128:(j + 1) * 128], wo[:, c, half * 512:(half + 1) * 512],
                                 start=(c == 0), stop=(c == 7))
                        S.tt('dve', xn[m][:, half * 512:(half + 1) * 512], po[:, 0:512],
                             xt4[m][:, half * 512:(half + 1) * 512], ALU.add)
                    if not last:
                        S.dma(xs[rows, :], xn[m][:])
                        if fuseA:
                            sq = ss4[:, m * 2:m * 2 + 1]
                            rs = ss4[:, m * 2 + 1:m * 2 + 2]
                            S.act(junk[:], xn[m][:], AF.Square, accum_out=sq)
                            rsqrt_to(es, rs, sq, EPS, 1.0 / DM)
                            S.stt('dve', xqF[:], xn[m][:], rs, nwF[:], ALU.mult, ALU.mult)
                            for half in range(2):
                                ph = PH()
                                for k in range(4):
                                    c = half * 4 + k
                                    S.tr(ph[:, k * 128:(k + 1) * 128], xqF[:, c * 128:(c + 1) * 128], ident_b[:])
                                S.copy('act' if half == 0 else 'dve', hT[:, half * 4:half * 4 + 4, rows],
                                       ph[:, 0:512].rearrange("p (k t) -> p k t", t=128))
                    else:
                        sq = ss4[:, m * 2:m * 2 + 1]
                        rs = ss4[:, m * 2 + 1:m * 2 + 2]
                        S.act(junk[:], xn[m][:], AF.Square, accum_out=sq)
                        rsqrt_to(es, rs, sq, EPS, 1.0 / DM)
                        S.stt('dve', xt4[m][:], xn[m][:], rs, fnw_bc[:], ALU.mult, ALU.mult)
                        S.dma(out[rows, :], xt4[m][:])
        S.fence()
        esW.close()
        lay.close()

    S.final()
    semstack = ExitStack()
    stats = S.emit(lambda name: semstack.enter_context(nc.semaphore(name)), reorder=reorder)
    return nc, stats, tapo


_CACHE = {}


def kernel(**inputs):
    x = np.ascontiguousarray(np.asarray(inputs['x'], dtype=np.float32))
    B, L, _ = x.shape
    key = (L,)
    if key not in _CACHE:
        _CACHE[key] = build(L)
    nc, stats, _ = _CACHE[key]
    shared = {}
    for k in ('norm_w', 'w_in', 'qkv_conv_w', 'dn_norm_w', 'w_dn_out', 'cf_conv_w', 'cf_conv_b',
              'cf_ln_w', 'cf_ln_b', 'w_cf_out', 'gate_b', 'w_out', 'final_norm_w'):
        shared[k] = np.ascontiguousarray(np.asarray(inputs[k], dtype=np.float32))
    shared['a_log'] = np.ascontiguousarray(np.asarray(inputs['a_log'], dtype=np.float32).reshape(DEPTH, 8))
    shared['dt_bias'] = np.ascontiguousarray(np.asarray(inputs['dt_bias'], dtype=np.float32).reshape(DEPTH, 8))
    in_maps = [dict(shared, x=x[b]) for b in range(B)]
    res = run_bass_kernel_spmd(nc, in_maps, core_ids=list(range(B)))
    return np.stack([np.asarray(r['out'], dtype=np.float32) for r in res.results], axis=0)
```

```python
import numpy as np
import concourse.bass as bass
import concourse.mybir as mybir

F32 = mybir.dt.float32
BF16 = mybir.dt.bfloat16
I32 = mybir.dt.int32
AF = mybir.ActivationFunctionType
ALU = mybir.AluOpType
AX = mybir.AxisListType


def _prod(xs):
    r = 1
    for v in xs:
        r *= int(v)
    return r


def box(a):
    t = a.tensor
    name = t.name
    apl = a.ap
    off = int(a.offset)
    space = str(a.space)
    if 'DRAM' in space.upper():
        hi = off + sum((c - 1) * abs(s) for s, c in apl) + 1
        return (name, 0, 1, off, hi)
    if 'PSUM' in space.upper():
        return (name, 0, 128, 0, 1 << 30)
    pstride = _prod(t.shape[1:])
    p0 = off // pstride
    f0 = off % pstride
    ps, pc = apl[0]
    f1 = f0 + sum((c - 1) * abs(s) for s, c in apl[1:]) + 1
    return (name, p0, p0 + pc, f0, f1)


def boxes(a):
    b = box(a)
    space = str(a.space).upper()
    if 'DRAM' in space or 'PSUM' in space:
        return [b]
    apl = a.ap
    if len(apl) < 3:
        return [b]
    s1, c1 = apl[1]
    inner = sum((c - 1) * abs(s) for s, c in apl[2:]) + 1
    if c1 <= 1 or c1 > 16 or s1 <= 0 or inner > s1:
        return [b]
    name, p0, p1, f0, _ = b
    return [(name, p0, p1, f0 + k * s1, f0 + k * s1 + inner) for k in range(c1)]


def bc_last(ap, n):
    return bass.AP(ap.tensor, ap.offset, [list(e) for e in ap.ap] + [[0, n]])


def bc_mid(ap, n):
    l = [list(e) for e in ap.ap]
    return bass.AP(ap.tensor, ap.offset, [l[0], [0, n]] + l[1:])


CP_DEFAULT = '1'


class Sched:
    ROT = 8000
    NDMA = 64

    def __init__(self, nc):
        self.nc = nc
        self.engs = {'pe': nc.tensor, 'act': nc.scalar, 'dve': nc.vector,
                     'pool': nc.gpsimd, 'sp': nc.sync}
        self.ops = []

    def add(self, eng, fn, reads, writes, dma=False):
        reads = [a for a in reads if a is not None]
        writes = [a for a in writes if a is not None]
        w0 = writes[0]
        n = 1
        for s_, c_ in list(w0.ap)[1:]:
            n *= int(c_)
        psrc = any('PSUM' in str(a.space).upper() for a in reads)
        f32 = bool(reads) and all(a.dtype == F32 for a in reads[:2])
        nbytes = n * int(list(w0.ap)[0][1]) * (4 if w0.dtype == F32 else 2)
        self.ops.append(dict(eng=eng, fn=fn, r=[bb for a in reads for bb in boxes(a)],
                             w=[bb for a in writes for bb in boxes(a)], dma=dma, fence=False,
                             n=n, psrc=psrc, f32=f32, nbytes=nbytes, nrd=len(reads)))

    def fence(self):
        self.ops.append(dict(fence=True))

    def capture(self):
        self._saved = self.ops
        self.ops = []

    def release(self):
        lst = self.ops
        self.ops = self._saved
        return lst

    def merge(self, *lists):
        n = max(len(x) for x in lists)
        for i in range(n):
            for x in lists:
                if i < len(x):
                    self.ops.append(x[i])

    def final(self):
        self.ops.append(dict(fence=False, final=True, eng='sp', fn=None, dma=False, r=[], w=[]))

    def mm(self, out, lhsT, rhs, start=True, stop=True):
        self.add('pe', lambda: self.nc.tensor.matmul(out, lhsT, rhs, start=start, stop=stop),
                 [lhsT, rhs], [out])

    def tr(self, out, in_, ident):
        self.add('pe', lambda: self.nc.tensor.transpose(out, in_, ident), [in_, ident], [out])

    def act(self, out, in_, func, bias=None, scale=None, accum_out=None):
        kw = {}
        rd = [in_]
        if bias is not None:
            kw['bias'] = bias
            if not isinstance(bias, (int, float)):
                rd.append(bias)
        if scale is not None:
            kw['scale'] = scale
            if not isinstance(scale, (int, float)):
                rd.append(scale)
        wr = [out]
        if accum_out is not None:
            kw['accum_out'] = accum_out
            wr.append(accum_out)
        self.add('act', lambda: self.nc.scalar.activation(out, in_, func, **kw), rd, wr)
        self.ops[-1]['actg'] = {AF.Silu: 'silu', AF.Sigmoid: 'sig', AF.Sqrt: 'sqrt', AF.Exp: 'exp', AF.Ln: 'exp'}.get(func)

    def tt(self, eng, out, in0, in1, op):
        e = self.engs[eng]
        self.add(eng, lambda: e.tensor_tensor(out, in0, in1, op), [in0, in1], [out])

    def ts(self, eng, out, in0, s1, s2, op0, op1=None):
        e = self.engs[eng]
        rd = [in0] + [s for s in (s1, s2) if s is not None and not isinstance(s, (int, float))]
        if op1 is None:
            self.add(eng, lambda: e.tensor_single_scalar(out, in0, s1, op0), rd, [out])
        else:
            self.add(eng, lambda: e.tensor_scalar(out, in0, s1, s2, op0, op1), rd, [out])

    def stt(self, eng, out, in0, scalar, in1, op0, op1):
        e = self.engs[eng]
        rd = [in0, in1] + ([scalar] if not isinstance(scalar, (int, float)) else [])
        self.add(eng, lambda: e.scalar_tensor_tensor(out, in0, scalar, in1, op0, op1), rd, [out])

    def copy(self, eng, out, in_):
        if eng == 'act':
            self.add('act', lambda: self.nc.scalar.copy(out, in_), [in_], [out])
        else:
            e = self.engs[eng]
            self.add(eng, lambda: e.tensor_copy(out, in_), [in_], [out])

    def memset(self, eng, ap, val):
        e = self.engs[eng]
        self.add(eng, lambda: e.memset(ap, val), [], [ap])

    def dma(self, out, in_, q='sp', **kw):
        e = self.engs[q]
        self.add(q, lambda: e.dma_start(out=out, in_=in_, **kw), [in_], [out], dma=True)

    def dma_cast(self, out, in_):
        self.add('pool', lambda: self.nc.gpsimd.dma_start(out=out, in_=in_), [in_], [out], dma=True)

    @staticmethod
    def _ovl(a, b):
        return a[1] < b[2] and b[1] < a[2] and a[3] < b[4] and b[3] < a[4]

    def _dur(self, op):
        e = op['eng']
        n = op['n']
        if op['dma']:
            return 1200.0 if e == 'pool' else 120.0
        if e == 'pe':
            return max(100.0, (4.0 if op['f32'] else 1.0) * n / 2.35 + 8.0)
        if e == 'act':
            return (200.0 + n) / 1.2
        if e == 'dve':
            if op['psrc']:
                return (n + 200.0) / 0.96
            if op['nrd'] >= 2:
                return (n + 151.0) / 0.96
            return (n / 2.0 + 151.0) / 0.96
        if e == 'pool':
            return 60.0 + 1.95 * n
        return 100.0

    def emit(self, sem_ctx, reorder=True):
        import heapq
        ops_all = self.ops
        CE = ('pe', 'act', 'dve', 'pool')
        ALLE = list(CE) + ['sp']
        segs = [[]]
        final_op = None
        for op in ops_all:
            if op['fence']:
                segs.append([])
            elif op.get('final'):
                final_op = op
            else:
                segs[-1].append(op)
        order = []
        nid = 0
        for seg in segs:
            if not seg:
                continue
            state = {}
            for op in seg:
                op['id'] = nid
                nid += 1
                deps = set()
                me = op['id']
                key = op['eng'] if not op['dma'] else ('d', me)
                for b in op['r']:
                    covered = False
                    lst = state.setdefault(b[0], [])
                    for rec in lst:
                        if self._ovl(rec[0], b):
                            if rec[1] is not None:
                                deps.add(rec[1])
                            if b[4] == (1 << 30):
                                for id2, k2 in rec[2].items():
                                    if k2 != key:
                                        deps.add(id2)
                            rec[2][me] = key
                            rb = rec[0]
                            if rb[1] <= b[1] and b[2] <= rb[2] and rb[3] <= b[3] and b[4] <= rb[4]:
                                covered = True
                    if not covered:
                        lst.append([b, None, {me: key}])
                for b in op['w']:
                    lst = state.setdefault(b[0], [])
                    keep = []
                    for rec in lst:
                        if self._ovl(rec[0], b):
                            if rec[1] is not None:
                                deps.add(rec[1])
                            deps.update(rec[2].keys())
                            rb = rec[0]
                            if b[1] <= rb[1] and rb[2] <= b[2] and b[3] <= rb[3] and rb[4] <= b[4]:
                                continue
                        keep.append(rec)
                    keep.append([b, me, {}])
                    state[b[0]] = keep
                deps.discard(me)
                op['deps'] = deps
            base = seg[0]['id']
            if not reorder:
                order.append(list(seg))
                continue
            nseg = len(seg)
            succ = [[] for _ in range(nseg)]
            npred = [0] * nseg
            for op in seg:
                i = op['id'] - base
                npred[i] = len(op['deps'])
                for d in op['deps']:
                    succ[d - base].append(i)
            ready_t = [0.0] * nseg
            done_t = [0.0] * nseg
            use_cp = CP_DEFAULT == '1'
            blev = [0.0] * nseg
            if use_cp:
                for i in range(nseg - 1, -1, -1):
                    m_ = 0.0
                    for s in succ[i]:
                        if blev[s] > m_:
                            m_ = blev[s]
                    blev[i] = m_ + self._dur(seg[i]) + (2000.0 if seg[i]['dma'] else 0.0)
                for i in range(nseg):
                    blev[i] += seg[i].get('boost', 0.0)
            tfree = {e: 0.0 for e in ALLE}
            fut = {e: [] for e in ALLE}
            avl = {e: [] for e in ALLE}
            for i, op in enumerate(seg):
                if npred[i] == 0:
                    heapq.heappush(fut[op['eng']], (0.0, i))
            new = []
            act_grp = [None]
            LAT = 200.0
            left = nseg
            while left:
                best = None
                for e in ALLE:
                    f, a = fut[e], avl[e]
                    while f and f[0][0] <= tfree[e]:
                        k_ = heapq.heappop(f)[1]
                        heapq.heappush(a, (-blev[k_], k_))
                    if a:
                        cand = (tfree[e], a[0][1], e, True)
                    elif f:
                        cand = (f[0][0], f[0][1], e, False)
                    else:
                        continue
                    if best is None or cand[:2] < best[:2]:
                        best = cand
                st, i, e, from_a = best
                if from_a:
                    if e == 'act' and len(avl[e]) > 1:
                        g0 = seg[i].get('actg')
                        if g0 is not None and g0 != act_grp[0]:
                            cands = [kk for kk in avl[e] if seg[kk[1]].get('actg') in (None, act_grp[0])]
                            if cands:
                                i = min(cands)[1]
                        avl[e].remove((-blev[i], i))
                        heapq.heapify(avl[e])
                    else:
                        heapq.heappop(avl[e])
                else:
                    heapq.heappop(fut[e])
                op = seg[i]
                d = self._dur(op)
                if e == 'act':
                    g = op.get('actg')
                    if g is not None and g != act_grp[0]:
                        d += 1300.0
                        act_grp[0] = g
                tfree[e] = st + d
                fin = st + d
                if op['dma']:
                    fin = st + 1800.0 + op['nbytes'] / 180.0
                done_t[i] = fin
                new.append(op)
                left -= 1
                for s in succ[i]:
                    if fin + LAT > ready_t[s]:
                        ready_t[s] = fin + LAT
                    npred[s] -= 1
                    if npred[s] == 0:
                        heapq.heappush(fut[seg[s]['eng']], (ready_t[s], s))
            order.append(new)
            self.seg_ms = getattr(self, 'seg_ms', []) + [(nseg, round(max(done_t) / 1e3), {e: round(sum(self._dur(o) for o in seg if o['eng'] == e) / 1e3) for e in ALLE})]
        pos = {}
        engcnt = {e: 0 for e in CE}
        dmaidx = {}
        dmaq = {}
        NQ = {'sp': self.NDMA, 'pool': 12, 'act': 16}
        nq = {'sp': 0, 'pool': 0, 'act': 0}
        ndma = 0
        flat = []
        for si, seg in enumerate(order):
            for op in seg:
                op['seg'] = si
                if op['dma']:
                    q_ = op['eng']
                    dmaidx[op['id']] = nq[q_]
                    dmaq[op['id']] = q_
                    nq[q_] += 1
                    ndma += 1
                else:
                    pos[op['id']] = (op['eng'], engcnt[op['eng']])
                    engcnt[op['eng']] += 1
                flat.append(op)
        waited = {e: {p: -1 for p in CE} for e in ALLE}
        waited_d = {e: set() for e in ALLE}
        last_pos = {p: -1 for p in CE}
        need_e = set()
        dma_ids_in_order = {'sp': [], 'pool': [], 'act': []}
        fence_pending = {e: None for e in ALLE}
        cur_seg = 0
        for op in flat:
            if op['seg'] != cur_seg:
                cur_seg = op['seg']
                snap = (dict(last_pos), [d_ for q_ in NQ for d_ in dma_ids_in_order[q_][-NQ[q_]:]])
                for e in ALLE:
                    fence_pending[e] = snap
            eng = op['eng']
            best = {}
            dwait = []
            if fence_pending[eng] is not None:
                lp, dl = fence_pending[eng]
                fence_pending[eng] = None
                for p, s in lp.items():
                    if s >= 0 and not (eng == 'pe' and p == 'pe'):
                        best[p] = s
                for di in dl:
                    if di not in waited_d[eng]:
                        waited_d[eng].add(di)
                        dwait.append(di)
            if op['dma']:
                k = dmaidx[op['id']]
                q_ = op['eng']
                if k >= NQ[q_]:
                    di = dma_ids_in_order[q_][k - NQ[q_]]
                    if di not in waited_d[eng]:
                        waited_d[eng].add(di)
                        dwait.append(di)
            for d in op['deps']:
                if d in dmaidx:
                    if d not in waited_d[eng]:
                        waited_d[eng].add(d)
                        dwait.append(d)
                else:
                    p, s = pos[d]
                    if eng == 'pe' and p == 'pe':
                        continue
                    if s > best.get(p, -1):
                        best[p] = s
            ew = []
            for p, s in best.items():
                if s > waited[eng][p]:
                    waited[eng][p] = s
                    ew.append((p, s))
                    need_e.add((p, s))
            op['ew'] = ew
            op['dw'] = dwait
            if op['dma']:
                dma_ids_in_order[op['eng']].append(op['id'])
            else:
                last_pos[eng] = pos[op['id']][1]
        fin_e = [(p, s) for p, s in last_pos.items() if s > waited['sp'][p]]
        for ps_ in fin_e:
            need_e.add(ps_)
        fin_d = [d for q_ in NQ for d in dma_ids_in_order[q_][-NQ[q_]:] if d not in waited_d['sp']]
        cnt = {e: 0 for e in CE}
        evval = {}
        for op in flat:
            if not op['dma']:
                ps_ = pos[op['id']]
                if ps_ in need_e:
                    c = cnt[ps_[0]]
                    cnt[ps_[0]] += 1
                    evval[ps_] = ('%s_%d' % (ps_[0], c // self.ROT), c % self.ROT + 1)
        sems = {}

        def get_sem(name):
            if name not in sems:
                sems[name] = sem_ctx(name)
            return sems[name]

        def dma_sem(d):
            k = dmaidx[d]
            q_ = dmaq[d]
            return get_sem({'sp': 'dma_%d', 'pool': 'sdma_%d', 'act': 'adma_%d'}[q_] % (k % NQ[q_])), 16 * (k // NQ[q_] + 1)
        nwait = 0
        for op in flat:
            e = self.engs[op['eng']]
            for ps_ in op['ew']:
                sname, val = evval[ps_]
                e.wait_ge(get_sem(sname), val)
                nwait += 1
            for d in op['dw']:
                sm, val = dma_sem(d)
                e.wait_ge(sm, val)
                nwait += 1
            ins = op['fn']()
            if op['dma']:
                sm, _ = dma_sem(op['id'])
                ins.then_inc(sm, 16)
            else:
                ps_ = pos[op['id']]
                if ps_ in evval:
                    ins.then_inc(get_sem(evval[ps_][0]), 1)
        sp = self.engs['sp']
        for ps_ in fin_e:
            sname, val = evval[ps_]
            sp.wait_ge(get_sem(sname), val)
        for d in fin_d:
            sm, val = dma_sem(d)
            sp.wait_ge(sm, val)
        self.stats = dict(nops=len(flat), nwait=nwait, nsem=len(sems), cnt=dict(cnt), ndma=ndma)
        return self.stats


from contextlib import ExitStack
from concourse.bass_utils import run_bass_kernel_spmd

DM = 1024
DIN = 5648
C_Q, C_K, C_V, C_ZA, C_AB, C_CFV, C_CFG, C_ZB, C_GATE = 0, 512, 1024, 1536, 2048, 2064, 2576, 3088, 3600
EPS = 1e-6
NEG = -30000.0
DEPTH = 2
OPT = dict(cb='a', rebal='0', fuseA='1', zip='1')


def build(L, depth=DEPTH, taps=(), nph=99, cph=99, reorder=True):
    nc = bass.Bass("TRN2", target_bir_lowering=False)
    NT = L // 128
    TW = min(512, L)
    NTT = L // TW
    JT = TW // 128
    HAL = 16

    def din(name, shape):
        return nc.dram_tensor(name, shape, F32, kind="ExternalInput").ap()
    x_in = din("x", [L, DM])
    norm_w = din("norm_w", [depth, DM])
    w_in = din("w_in", [depth, DM, DIN])
    qkv_conv_w = din("qkv_conv_w", [depth, 5, 1536])
    a_log = din("a_log", [depth, 8])
    dt_bias = din("dt_bias", [depth, 8])
    dn_norm_w = din("dn_norm_w", [depth, 128])
    w_dn_out = din("w_dn_out", [depth, 512, DM])
    cf_conv_w = din("cf_conv_w", [depth, 31, 512])
    cf_conv_b = din("cf_conv_b", [depth, 512])
    cf_ln_w = din("cf_ln_w", [depth, 512])
    cf_ln_b = din("cf_ln_b", [depth, 512])
    w_cf_out = din("w_cf_out", [depth, 512, DM])
    gate_b = din("gate_b", [depth, 2048])
    w_out = din("w_out", [depth, DM, DM])
    final_norm_w = din("final_norm_w", [DM])
    out = nc.dram_tensor("out", [L, DM], F32, kind="ExternalOutput").ap()
    tapo = {}

    def scr(name, shape, dt):
        return nc.dram_tensor(name, shape, dt).ap()
    xs = scr("xs", [L, DM], F32)
    QtS = scr("QtS", [512, L], BF16)
    KtS = scr("KtS", [512, L], BF16)
    KS = scr("KS", [L, 512], BF16)
    VS = scr("VS", [L, 512], BF16)
    szaS = scr("szaS", [512, L], BF16)
    oS = scr("oS", [L, 512], F32)
    oSb = scr("oSb", [L, 512], F32)
    paS = scr("paS", [512, L], BF16)
    pbS = scr("pbS", [512, L], BF16)

    S = Sched(nc)
    top = ExitStack()

    uniq = {'n': 0}

    def sbt(es, name, shape, dt=F32):
        uniq['n'] += 1
        return es.enter_context(nc.sbuf_tensor("%s_u%d" % (name, uniq['n']), shape, dt))

    psF = [top.enter_context(nc.psum_tensor("psF%d" % i, [128, 512], F32)) for i in range(6)]
    psH = [top.enter_context(nc.psum_tensor("psH%d" % i, [128, 1024], BF16)) for i in range(2)]
    rot = {'f': 0, 'h': 0}

    cur = {'c': None}
    rotc = [0, 0]
    roth = [0, 0]

    def PF():
        c = cur['c']
        if c is not None:
            st_ = cur.get('stage')
            if OPT['cb'] == 'a':
                if st_ == 'scan':
                    return psF[3 * c + 2]
                rotc[c] = (rotc[c] + 1) % 2
                return psF[3 * c + rotc[c]]
            if OPT['cb'] == 'e':
                if st_ == 'scan':
                    return psF[4 + c]
                rotc[0] = (rotc[0] + 1) % 4
                return psF[rotc[0]]
            if OPT['cb'] == 'b':
                return psF[3 * c + {'prep': 0, 'neu': 1, 'scan': 2}[st_]]
            if OPT['cb'] == 'c':
                if st_ != 'scan':
                    return psF[3 * c]
                rotc[c] = (rotc[c] + 1) % 2
                return psF[3 * c + 1 + rotc[c]]
            rotc[c] = (rotc[c] + 1) % 3
            return psF[3 * c + rotc[c]]
        rot['f'] = (rot['f'] + 1) % 6
        return psF[rot['f']]

    def PH():
        c = cur['c']
        if c is not None:
            roth[c] ^= 1
            return psH[c][:, roth[c] * 512:roth[c] * 512 + 512]
        rot['h'] = (rot['h'] + 1) % 2
        return psH[rot['h']][:, 0:512]

    alt = {'i': 0}

    def AD():
        alt['i'] ^= 1
        return 'act' if alt['i'] else 'dve'

    ip_i = sbt(top, "ip_i", [128, 128], I32)
    ij_i = sbt(top, "ij_i", [128, 128], I32)
    ip = sbt(top, "ip", [128, 128])
    ij = sbt(top, "ij", [128, 128])
    cst = sbt(top, "cst", [128, 12, 128])
    ident_b = sbt(top, "ident_b", [128, 128], BF16)
    cstb = sbt(top, "cstb", [128, 4, 128], BF16)
    ones128_b = sbt(top, "ones128_b", [128, 128], BF16)
    ones1_b = sbt(top, "ones1_b", [128, 128], BF16)
    onesLN_b = sbt(top, "onesLN_b", [128, 128], BF16)
    hT = sbt(top, "hT", [128, 8, L], BF16)
    S.add('pool', lambda: nc.gpsimd.iota(ip_i[:], [[0, 128]], base=0, channel_multiplier=1), [], [ip_i[:]])
    S.add('pool', lambda: nc.gpsimd.iota(ij_i[:], [[1, 128]], base=0, channel_multiplier=0), [], [ij_i[:]])
    S.copy('dve', ip[:], ip_i[:])
    S.copy('dve', ij[:], ij_i[:])
    IDENT, LE, LT, GE, GT, NEGF, NEGB, BLKA, BLKB, SAME, T0, T1 = [cst[:, i, :] for i in range(12)]
    S.tt('dve', IDENT, ip[:], ij[:], ALU.is_equal)
    S.ts('dve', T0, ip[:], 64.0, None, ALU.is_ge)
    S.ts('dve', T1, ij[:], 64.0, None, ALU.is_ge)
    S.tt('dve', SAME, T0, T1, ALU.is_equal)
    for dst, op in ((LE, ALU.is_le), (LT, ALU.is_lt), (GE, ALU.is_ge), (GT, ALU.is_gt)):
        S.tt('dve', dst, ip[:], ij[:], op)
        S.tt('dve', dst, dst, SAME, ALU.mult)
    S.ts('dve', NEGF, GE, -1.0, -NEG, ALU.add, ALU.mult)
    S.ts('dve', NEGB, LE, -1.0, -NEG, ALU.add, ALU.mult)
    S.ts('dve', BLKA, ip[:], 64.0, None, ALU.is_lt)
    S.ts('dve', BLKB, ip[:], 64.0, None, ALU.is_ge)
    S.copy('dve', ident_b[:], IDENT)
    GTb, LTb, NEGFb, NEGBb = [cstb[:, i, :] for i in range(4)]
    for dst_, src_ in ((GTb, GT), (LTb, LT), (NEGFb, NEGF), (NEGBb, NEGB)):
        S.copy('dve', dst_, src_)
    S.memset('dve', ones128_b[:], 128.0)
    S.memset('dve', ones1_b[:], 1.0)
    S.memset('dve', onesLN_b[:], 1.0 / 512.0)

    def load_w(es_buf, src, nk, ncols, wst, wdst, eng='pool'):
        srcv = src.rearrange("(c p) n -> p c n", p=128)
        step = 256 if ncols > 256 else ncols
        for c0 in range(0, ncols, step):
            S.dma_cast(wdst[:, 0:nk, c0:c0 + step], srcv[:, :, c0:c0 + step])

    def proj(wb, col0, t0, tw, ps):
        for c in range(8):
            S.mm(ps[:, 0:tw], wb[:, c, col0:col0 + 128], hT[:, c, t0:t0 + tw], start=(c == 0), stop=(c == 7))

    def rsqrt_to(es_tmp, dst, src, eps, scale=1.0, wide=False):
        if wide:
            S.act(dst, src, AF.Ln, bias=eps, scale=scale)
            S.act(dst, dst, AF.Exp, scale=-0.5)
        else:
            S.act(dst, src, AF.Sqrt, bias=eps, scale=scale)
            S.add('dve', lambda: nc.vector.reciprocal(dst, dst), [dst], [dst])

    for l in range(depth):
        x_cur = x_in if l == 0 else xs
        last = (l == depth - 1)
        lay = ExitStack()
        stA = sbt(lay, "stA%d" % l, [128, 128])
        stB = sbt(lay, "stB%d" % l, [128, 128])
        prmA = sbt(lay, "prmA%d" % l, [128, 128])
        prmB = sbt(lay, "prmB%d" % l, [128, 128])
        alog_bc = sbt(lay, "alog%d" % l, [128, 8])
        dtb_bc = sbt(lay, "dtb%d" % l, [128, 8])
        negA = sbt(lay, "negA%d" % l, [128, 8])
        S.memset('dve', stA[:], 0.0)
        S.memset('dve', stB[:], 0.0)
        S.dma(stA[0:8, :], norm_w[l].rearrange("(c p) -> c p", p=128))
        S.dma(stA[8:68, :], qkv_conv_w[l].rearrange("k (c p) -> (k c) p", p=128))
        S.dma(stA[68:69, :], dn_norm_w[l].rearrange("(c p) -> c p", p=128))
        S.dma(stA[69:73, :], cf_conv_b[l].rearrange("(c p) -> c p", p=128))
        S.dma(stA[73:77, :], cf_ln_w[l].rearrange("(c p) -> c p", p=128))
        S.dma(stA[77:81, :], cf_ln_b[l].rearrange("(c p) -> c p", p=128))
        S.dma(stA[81:97, :], gate_b[l].rearrange("(c p) -> c p", p=128))
        S.dma(stB[0:124, :], cf_conv_w[l].rearrange("k (c p) -> (k c) p", p=128))
        S.dma(alog_bc[:], a_log[l].partition_broadcast(128))
        S.dma(dtb_bc[:], dt_bias[l].partition_broadcast(128))
        ps = PF()
        S.tr(ps[:, 0:128], stA[:], IDENT)
        S.copy('dve', prmA[:], ps[:, 0:128])
        ps = PF()
        S.tr(ps[:, 0:128], stB[:], IDENT)
        S.copy('dve', prmB[:], ps[:, 0:128])
        S.act(negA[:], alog_bc[:], AF.Exp)
        S.ts('dve', negA[:], negA[:], -1.0, None, ALU.mult)

        def normw(c): return prmA[:, c:c + 1]
        def qkvw(k, cc): return prmA[:, 8 + k * 12 + cc: 9 + k * 12 + cc]
        dnw = prmA[:, 68:69]
        def cfb(cc): return prmA[:, 69 + cc:70 + cc]
        def lnw(cc): return prmA[:, 73 + cc:74 + cc]
        def lnb(cc): return prmA[:, 77 + cc:78 + cc]
        def gateb(c): return prmA[:, 81 + c:82 + c]
        def cfw(k, cc): return prmB[:, k * 4 + cc:k * 4 + cc + 1]

        tb = ExitStack()
        TAB = sbt(tb, "TAB%d" % l, [128, 8, NT * 8])
        GHL = sbt(tb, "GHL%d" % l, [128, 2, NT * 8], BF16)
        Gt, BETAt, EGCt, BEGCt, EGLGt, DLAt, DLBt = [TAB[:, i, :] for i in range(7)]
        esAB = ExitStack()
        fuseA = OPT['fuseA'] != '0'
        for es in ((esAB,) if not (fuseA and l > 0) else ()):
            nw_bc = sbt(esAB, "nwbc%d" % l, [128, DM])
            S.dma(nw_bc[:], norm_w[l].partition_broadcast(128))
            xb = [sbt(es, "xb%d" % i, [128, DM]) for i in range(4)]
            junk = sbt(es, "junkA", [128, DM])
            xsb = [sbt(es, "xsb%d" % i, [128, DM], BF16) for i in range(4)]
            ssA = sbt(es, "ssA", [128, 8])
            for i in range(NT):
                xt = xb[i % 4]
                xq = xsb[i % 4]
                sq = ssA[:, (i % 4) * 2:(i % 4) * 2 + 1]
                rs = ssA[:, (i % 4) * 2 + 1:(i % 4) * 2 + 2]
                S.dma(xt[:], x_cur[i * 128:(i + 1) * 128, :])
                S.act(junk[:], xt[:], AF.Square, accum_out=sq)
                rsqrt_to(es, rs, sq, EPS, 1.0 / DM)
                S.stt('dve', xq[:], xt[:], rs, nw_bc[:], ALU.mult, ALU.mult)
                for half in range(2):
                    ph = PH()
                    for k in range(4):
                        c = half * 4 + k
                        S.tr(ph[:, k * 128:(k + 1) * 128], xq[:, c * 128:(c + 1) * 128], ident_b[:])
                    S.copy('act' if half == 0 else 'dve', hT[:, half * 4:half * 4 + 4, i * 128:(i + 1) * 128],
                           ph[:, 0:512].rearrange("p (k t) -> p k t", t=128))
        if nph <= 1:
            break
        if 'hT' in taps and l == 0:
            tapo['hT'] = nc.dram_tensor("tap_hT", [128, 8, L], BF16, kind="ExternalOutput").ap()
            S.dma(tapo['hT'], hT[:])

        for es in (esAB,):
            wst = None
            wbf = [sbt(es, "wbfB%d" % i, [128, 8, 512], BF16) for i in range(2)]
            rb = [sbt(es, "rbB%d" % i, [128, L + 2 * HAL], BF16) for i in range(2)]
            dg = [sbt(es, "dgB%d" % i, [128, 5, 128], BF16) for i in range(2)]
            sil = [sbt(es, "silB%d" % i, [128, TW]) for i in range(4)]
            sqb = [sbt(es, "sqB%d" % i, [128, TW], BF16) for i in range(4)]
            sd = [sbt(es, "sdB%d" % i, [128, TW]) for i in range(4)]
            qn = [sbt(es, "qnB%d" % i, [128, TW], BF16) for i in range(4)]
            tok = [sbt(es, "tokB%d" % i, [128, JT, 128], BF16) for i in range(4)]
            for r in rb:
                S.memset('pool', r[:, 0:HAL], 0.0)
                S.memset('pool', r[:, HAL + L:], 0.0)
            n = 0
            for grp, (col0, kind) in enumerate(((C_Q, 'q'), (C_K, 'k'), (C_V, 'v'))):
                wb = wbf[grp % 2]
                load_w(es, w_in[l][:, col0:col0 + 512], 8, 512, wst, wb)
                for h in range(4):
                    cc = grp * 4 + h
                    r = rb[cc % 2]
                    d = dg[cc % 2]
                    for k in range(5):
                        S.act(d[:, k, :], IDENT, AF.Copy, scale=qkvw(k, cc))
                    for tt in range(NTT):
                        ps = PF()
                        proj(wb, h * 128, tt * TW, TW, ps)
                        S.copy('dve', r[:, HAL + tt * TW:HAL + (tt + 1) * TW], ps[:, 0:TW])
                    for tt in range(NTT):
                        n += 1
                        ps = PF()
                        for k in range(5):
                            o = HAL + tt * TW + k - 2
                            S.mm(ps[:, 0:TW], d[:, k, :], r[:, o:o + TW], start=(k == 0), stop=(k == 4))
                        tsl = slice(tt * TW, (tt + 1) * TW)
                        if kind == 'v':
                            vq = qn[n % 4]
                            S.act(vq[:], ps[:, 0:TW], AF.Silu)
                            src_bf = vq
                        else:
                            sl = sil[n % 4]
                            S.act(sl[:], ps[:, 0:TW], AF.Silu)
                            S.tt('pool', sqb[n % 4][:], sl[:], sl[:], ALU.mult)
                            ps3 = PF()
                            S.mm(ps3[:, 0:TW], (ones128_b if kind == 'q' else ones1_b)[:], sqb[n % 4][:])
                            sdd = sd[n % 4]
                            rsqrt_to(es, sdd[:], ps3[:, 0:TW], EPS * (128.0 if kind == 'q' else 1.0), wide=True)
                            src_bf = qn[n % 4]
                            S.tt('dve', src_bf[:], sl[:], sdd[:], ALU.mult)
                            S.dma((QtS if kind == 'q' else KtS)[h * 128:(h + 1) * 128, tsl], src_bf[:], q='act')
                        if kind != 'q':
                            ph = PH()
                            for j in range(JT):
                                S.tr(ph[:, j * 128:(j + 1) * 128], src_bf[:, j * 128:(j + 1) * 128], ident_b[:])
                            tk = tok[n % 4]
                            S.copy('dve', tk[:], ph[:, 0:JT * 128].rearrange("p (j d) -> p j d", d=128))
                            dstS = KS if kind == 'k' else VS
                            S.dma(dstS[tsl, h * 128:(h + 1) * 128].rearrange("(j t) d -> t j d", t=128), tk[:], q='act')
            wb = wbf[1]
            load_w(es, w_in[l][:, C_ZA:C_ZA + 512], 8, 512, wst, wb)
            for h in range(4):
                for tt in range(NTT):
                    n += 1
                    ps = PF()
                    proj(wb, h * 128, tt * TW, TW, ps)
                    S.act(qn[n % 4][:], ps[:, 0:TW], AF.Silu)
                    S.dma(szaS[h * 128:(h + 1) * 128, tt * TW:(tt + 1) * TW], qn[n % 4][:], q='act')
        if nph <= 2:
            break

        for es in (esAB,):
            wst = None
            b3_first = len(S.ops)
            w16 = sbt(es, "w16", [128, 8, 16], BF16)
            AB = sbt(es, "AB", [128, NT, 16])
            tmp = sbt(es, "tmp3", [128, 4, NT * 8])
            load_w(es, w_in[l][:, C_AB:C_AB + 16], 8, 16, wst, w16, eng='dve')
            for sc in range(NT):
                ps = PF()
                for c in range(8):
                    S.mm(ps[:, 0:16], hT[:, c, sc * 128:(sc + 1) * 128], w16[:, c, :], start=(c == 0), stop=(c == 7))
                S.copy(AD(), AB[:, sc, :], ps[:, 0:16])

            def v3(ap2):
                return ap2.rearrange("p (s h) -> p s h", h=8)
            xg, ax, e1, mx = [tmp[:, i, :] for i in range(4)]
            S.tt('dve', v3(xg), AB[:, :, 0:8], bc_mid(dtb_bc[:], NT), ALU.add)
            S.act(ax, xg, AF.Abs)
            S.act(e1, ax, AF.Exp, scale=-1.0)
            S.act(e1, e1, AF.Ln, bias=1.0)
            S.ts('dve', mx, xg, 0.0, None, ALU.max)
            S.tt('dve', mx, mx, e1, ALU.add)
            S.tt('dve', v3(Gt), v3(mx), bc_mid(negA[:], NT), ALU.mult)
            S.act(v3(BETAt), AB[:, :, 8:16], AF.Sigmoid)
            W = NT * 8

            def cum(maskF, maskB, dst):
                psa = PF()
                psb = PF()
                S.mm(psa[:, 0:W], maskF, Gt)
                S.mm(psb[:, 0:W], maskB, Gt)
                S.act(v3(dst)[:, :, 0:4], v3(psa[:, 0:W])[:, :, 0:4], AF.Exp)
                S.act(v3(dst)[:, :, 4:8], v3(psb[:, 0:W])[:, :, 4:8], AF.Exp)
            cum(LE, GE, EGCt)
            cum(GT, LT, EGLGt)
            cum(BLKA, BLKA, DLAt)
            cum(BLKB, BLKB, DLBt)
            S.tt('dve', BEGCt, BETAt, EGCt, ALU.mult)
            S.copy('dve', GHL[:, 0, :], Gt)
            S.tt('dve', TAB[:, 7, :], Gt, GHL[:, 0, :], ALU.subtract)
            S.copy('dve', GHL[:, 1, :], TAB[:, 7, :])
        for op_ in S.ops[b3_first:]:
            if not op_.get('fence'):
                op_['boost'] = 1e6
        if nph <= 3:
            break
        esAB.close()
        S.fence()
        if 'tab' in taps and l == 0:
            tapo['tab'] = nc.dram_tensor("tap_tab", [128, 8, NT * 8], F32, kind="ExternalOutput").ap()
            S.dma(tapo['tab'], TAB[:])

        with ExitStack() as es:
            def t2(name, shape, dt=BF16, nb=2):
                return [sbt(es, "%s%d" % (name, i), shape, dt) for i in range(nb)]
            KtT = t2("cKt", [128, 4, 128], BF16, 4); QtT = t2("cQt", [128, 4, 128], BF16, 4)
            KtokT = t2("cKtok", [128, 4, 128], BF16, 4); VtokT = t2("cVtok", [128, 4, 128], BF16, 4)
            gA = t2("cgA", [128, 4, 128], BF16)
            gAl = t2("cgAl", [128, 4, 128], BF16)
            Dm = t2("cD", [128, 4, 128], F32)
            attn = t2("cattn", [128, 4, 128]); Gm = t2("cGm", [128, 4, 128], F32)
            Lf = t2("cLf", [128, 4, 128], F32)
            Ub = t2("cU", [128, 4, 128], BF16, 12); Tb = t2("cT", [128, 4, 128], BF16, 12)
            Pb = t2("cP", [128, 4, 128], BF16, 12)
            attnT = t2("cattnT", [128, 4, 128], BF16, 4)
            KBG = t2("cKBG", [128, 4, 128], BF16, 4); BV = t2("cBV", [128, 4, 128], BF16, 4); KDEC = t2("cKDEC", [128, 4, 128], BF16, 4)
            NKC = t2("cNKC", [128, 4, 128], BF16, 4)
            vnewL = [t2("cvnew%d" % q, [128, 4, 128]) for q in range(2)]
            SfL = t2("cSf", [128, 4, 128], F32); StmpL = t2("cStmp", [128, 4, 128], F32)
            SbL = [t2("cSb%d" % q, [128, 4, 128]) for q in range(2)]
            ob = t2("cob", [128, 4, 128], F32, 4)
            sbiL = [0, 0]
            for q in range(2):
                S.memset('dve', SfL[q][:], 0.0)
                S.memset('dve', SbL[q][0][:], 0.0)
                S.memset('dve', SbL[q][1][:], 0.0)
            for it in range(NT):
                grp = []
                for dr in range(2):
                    if dr == 0:
                        S.capture()
                    else:
                        grp.append(S.release())
                        S.capture()
                    MA, MB, NEGM = (LE, GT, NEGF) if dr == 0 else (GE, LT, NEGB)
                    STRICT = MB
                    MBb, NEGMb = (GTb, NEGFb) if dr == 0 else (LTb, NEGBb)
                    sc = it if dr == 0 else NT - 1 - it
                    b = dr
                    b4 = dr * 2 + it % 2
                    cur['c'] = dr
                    cur['stage'] = 'prep'
                    Sf, Stmp, Sb, vnew = SfL[dr], StmpL[dr], SbL[dr], vnewL[dr]
                    sbi = sbiL[dr]
                    tsl = slice(sc * 128, (sc + 1) * 128)
                    hs = slice(sc * 8 + dr * 4, sc * 8 + dr * 4 + 4)
                    Kt, Qt, Ktok, Vtok = KtT[b4], QtT[b4], KtokT[b4], VtokT[b4]
                    S.dma(Kt[:], KtS[:, tsl].rearrange("(h d) t -> d h t", d=128))
                    S.dma(Qt[:], QtS[:, tsl].rearrange("(h d) t -> d h t", d=128))
                    S.dma(Ktok[:], KS[tsl, :].rearrange("t (h d) -> t h d", d=128))
                    S.dma(Vtok[:], VS[tsl, :].rearrange("t (h d) -> t h d", d=128))
                    S.tt('pool', gA[b][:], bc_mid(MA, 4), bc_last(GHL[:, 0, hs], 128), ALU.mult)
                    S.tt('pool', gAl[b][:], bc_mid(MA, 4), bc_last(GHL[:, 1, hs], 128), ALU.mult)
                    pd = PF()
                    for h in range(4):
                        S.mm(pd[:, h * 128:(h + 1) * 128], gA[b][:, h, :], MBb, start=True, stop=False)
                        S.mm(pd[:, h * 128:(h + 1) * 128], gAl[b][:, h, :], MBb, start=False, stop=False)
                        S.mm(pd[:, h * 128:(h + 1) * 128], ident_b[:], NEGMb, start=False, stop=True)
                    S.act(Dm[b][:], pd[:].rearrange("p (h j) -> p h j", j=128), AF.Exp)
                    if cph <= 1:
                        continue
                    pg = PF()
                    pq = PF()
                    for h in range(4):
                        S.mm(pg[:, h * 128:(h + 1) * 128], Kt[:, h, :], Kt[:, h, :])
                        S.mm(pq[:, h * 128:(h + 1) * 128], Qt[:, h, :], Kt[:, h, :])
                    v4 = lambda p: p[:].rearrange("p (h j) -> p h j", j=128)
                    S.tt('dve', attn[b][:], v4(pq), Dm[b][:], ALU.mult)
                    S.tt('dve', Gm[b][:], v4(pg), bc_mid(STRICT, 4), ALU.mult)
                    S.tt('pool', Lf[b][:], Gm[b][:], Dm[b][:], ALU.mult)
                    U0 = Ub[b4 * 3]
                    S.tt('dve' if OPT['rebal'] != '0' else 'pool', U0[:], Lf[b][:], bc_last(BETAt[:, hs], 128), ALU.mult)
                    if cph <= 2:
                        continue
                    pl = PH()
                    pa_ = PH()
                    for h in range(4):
                        S.tr(pl[:, h * 128:(h + 1) * 128], U0[:, h, :], ident_b[:])
                        S.tr(pa_[:, h * 128:(h + 1) * 128], attn[b][:, h, :], ident_b[:])
                    vh = lambda p: p.rearrange("p (h j) -> p h j", j=128)
                    T0_ = Tb[b4 * 3]
                    P0_ = Pb[b4 * 3]
                    S.copy('act', T0_[:], vh(pl))
                    S.tt('dve', P0_[:], bc_mid(IDENT, 4), vh(pl), ALU.subtract)
                    S.copy('act', attnT[b4][:], vh(pa_))
                    Uc, Tc, Pc = U0, T0_, P0_
                    cur['stage'] = 'neu' if OPT['cb'] == 'b' else 'prep'
                    for j in range(1, 6):
                        Un, Tn, Pn = Ub[b4 * 3 + j % 3], Tb[b4 * 3 + j % 3], Pb[b4 * 3 + j % 3]
                        pu = PF()
                        for h in range(4):
                            S.mm(pu[:, h * 128:(h + 1) * 128], Tc[:, h, :], Uc[:, h, :])
                        S.copy('act', Un[:], v4(pu))
                        if j < 5:
                            pt = PF()
                            for h in range(4):
                                S.mm(pt[:, h * 128:(h + 1) * 128], Uc[:, h, :], Tc[:, h, :])
                            S.copy('act', Tn[:], v4(pt))
                        pp = PF()
                        for h in range(4):
                            S.mm(pp[:, h * 128:(h + 1) * 128], Un[:, h, :], Pc[:, h, :])
                        S.tt('dve', Pn[:], v4(pp), Pc[:], ALU.add)
                        Uc, Tc, Pc = Un, Tn, Pn
                    if cph <= 3:
                        continue
                    AT = Pc
                    cur['stage'] = 'prep'
                    S.tt('pool', KBG[b4][:], Ktok[:], bc_last(BEGCt[:, hs], 128), ALU.mult)
                    S.tt('dve' if OPT['rebal'] == '2' else 'pool', BV[b4][:], Vtok[:], bc_last(BETAt[:, hs], 128), ALU.mult)
                    S.tt('pool', KDEC[b4][:], Ktok[:], bc_last(EGLGt[:, hs], 128), ALU.mult)
                    pk = PF()
                    for h in range(4):
                        S.mm(pk[:, h * 128:(h + 1) * 128], KBG[b4][:, h, :], AT[:, h, :])
                    S.ts('dve', NKC[b4][:], v4(pk), -1.0, None, ALU.mult)
                    if cph <= 4:
                        continue
                    o_sc = ob[b4]
                    cur['stage'] = 'scan'
                    for blk in ((0, 1) if dr == 0 else (1, 0)):
                        r = slice(blk * 64, blk * 64 + 64)
                        DL = DLAt if blk == 0 else DLBt
                        Scur = Sb[sbi % 2]
                        pqs = PF()
                        for h in range(4):
                            S.mm(pqs[:, h * 128:(h + 1) * 128], Qt[:, h, :], Scur[:, h, :])
                        S.tt('dve', o_sc[r], pqs[r].rearrange("p (h j) -> p h j", j=128),
                             bc_last(EGCt[r, hs], 128), ALU.mult)
                        pv = PF()
                        for h in range(4):
                            S.mm(pv[:, h * 128:(h + 1) * 128], AT[:, h, :], BV[b4][:, h, :], start=True, stop=False)
                            S.mm(pv[:, h * 128:(h + 1) * 128], NKC[b4][:, h, :], Scur[:, h, :], start=False, stop=True)
                        vn = vnew[sbi % 2]
                        S.copy('act', vn[r], pv[r].rearrange("p (h j) -> p h j", j=128))
                        pav = PF()
                        for h in range(4):
                            S.mm(pav[:, h * 128:(h + 1) * 128], attnT[b4][r, h, :], vn[r, h, :])
                        S.tt('dve', o_sc[r], o_sc[r], pav[r].rearrange("p (h j) -> p h j", j=128), ALU.add)
                        pds = PF()
                        for h in range(4):
                            S.mm(pds[:, h * 128:(h + 1) * 128], KDEC[b4][r, h, :], vn[r, h, :])
                        S.tt('pool', Stmp[:], Sf[:], bc_last(DL[:, hs], 128), ALU.mult)
                        S.tt('dve', Sf[:], Stmp[:], v4(pds), ALU.add)
                        sbi += 1
                        S.copy('act', Sb[sbi % 2][:], Sf[:])
                    sbiL[dr] = sbi
                    if cph <= 5:
                        continue
                    S.dma((oS if dr == 0 else oSb)[tsl, :].rearrange("t (h e) -> t h e", e=128), o_sc[:], q='act')
                grp.append(S.release())
                if OPT.get('zip', '1') == '1':
                    S.merge(*grp)
                else:
                    for g_ in grp:
                        S.ops.extend(g_)
                cur['c'] = None
        if nph <= 4:
            break
        S.fence()
        tb.close()
        esW = ExitStack()
        wo = sbt(esW, "wo", [128, 8, 1024], BF16)
        wdn = sbt(esW, "wdn", [128, 4, 1024], BF16)
        wcf = sbt(esW, "wcf", [128, 4, 1024], BF16)

        with ExitStack() as es:
            ucv = sbt(es, "ucv", [128, 4, L], BF16)
            def t3(name, shape, dt=BF16, nb=2):
                return [sbt(es, "%s%d" % (name, i), shape, dt) for i in range(nb)]
            oF = t3("coF", [128, 4, 128], F32)
            oB = t3("coB", [128, 4, 128], F32)
            ssq = t3("cssq", [128, 8], F32)
            onb = t3("conb", [128, 4, 128])
            szat = t3("cszat", [128, 4, 128])
            pat = t3("cpat", [128, 4, 128])
            junkC = t3("cjunk", [128, 128], F32)
            for sc in range(NT if cph > 5 else 0):
                b = sc % 2
                tsl = slice(sc * 128, (sc + 1) * 128)
                S.dma(oF[b][:], oS[tsl, :].rearrange("t (h e) -> t h e", e=128))
                S.dma(oB[b][:], oSb[tsl, :].rearrange("t (h e) -> t h e", e=128))
                S.dma(szat[b][:], szaS[:, tsl].rearrange("(h e) t -> e h t", e=128))
                S.tt('pool', oF[b][:], oF[b][:], oB[b][:], ALU.add)
                for h in range(4):
                    S.act(junkC[b][:], oF[b][:, h, :], AF.Square, accum_out=ssq[b][:, h:h + 1])
                rsqrt_to(es, ssq[b][:, 4:8], ssq[b][:, 0:4], EPS, 1.0 / 128.0)
                S.tt('dve', onb[b][:], oF[b][:], bc_last(ssq[b][:, 4:8], 128), ALU.mult)
                po = PH()
                for h in range(4):
                    S.tr(po[:, h * 128:(h + 1) * 128], onb[b][:, h, :], ident_b[:])
                S.stt('dve', pat[b][:], po.rearrange("p (h j) -> p h j", j=128), dnw, szat[b][:], ALU.mult, ALU.mult)
                S.dma(paS[:, tsl].rearrange("(h e) t -> e h t", e=128), pat[b][:], q='act')
            with ExitStack() as es1:
                wst = None
                wbV = sbt(es1, "wbV", [128, 8, 512], BF16)
                wbG = sbt(es1, "wbG", [128, 8, 512], BF16)
                rbu = [sbt(es1, "rbu%d" % i, [128, L + 2 * HAL], BF16) for i in range(2)]
                dg31 = [sbt(es1, "dg31_%d" % i, [128, 23, 128], BF16) for i in range(2)]
                sg = [sbt(es1, "sgD%d" % i, [128, TW]) for i in range(2)]
                accD = [sbt(es1, "accD%d" % i, [128, TW]) for i in range(2)]
                for r in rbu:
                    S.memset('pool', r[:, 0:HAL], 0.0)
                    S.memset('pool', r[:, HAL + L:], 0.0)
                load_w(es1, w_in[l][:, C_CFV:C_CFV + 512], 8, 512, wst, wbV)
                load_w(es1, w_in[l][:, C_CFG:C_CFG + 512], 8, 512, wst, wbG)
                load_w(es1, w_dn_out[l], 4, 1024, None, wdn)
                load_w(es1, w_cf_out[l], 4, 1024, None, wcf)
                load_w(es1, w_out[l], 8, 1024, None, wo)
                n = 0
                for cc in range(4):
                    r = rbu[cc % 2]
                    d = dg31[cc % 2]
                    for k in range(8, 31):
                        S.act(d[:, k - 8, :], IDENT, AF.Copy, scale=cfw(k, cc))
                    for tt in range(NTT):
                        n += 1
                        pv = PF()
                        pg = PF()
                        proj(wbV, cc * 128, tt * TW, TW, pv)
                        proj(wbG, cc * 128, tt * TW, TW, pg)
                        S.act(sg[n % 2][:], pg[:, 0:TW], AF.Sigmoid)
                        S.tt('dve', r[:, HAL + tt * TW:HAL + (tt + 1) * TW], pv[:, 0:TW], sg[n % 2][:], ALU.mult)
                    NDV = 8
                    for tt in range(NTT):
                        ac = accD[(cc * NTT + tt) % 2]
                        for k in range(NDV):
                            o = HAL + tt * TW + k - 15
                            if k == 0:
                                S.ts('dve', ac[:], r[:, o:o + TW], cfw(k, cc), None, ALU.mult)
                            else:
                                S.stt('dve', ac[:], r[:, o:o + TW], cfw(k, cc), ac[:], ALU.mult, ALU.add)
                        ps = PF()
                        for k in range(NDV, 31):
                            o = HAL + tt * TW + k - 15
                            S.mm(ps[:, 0:TW], d[:, k - NDV, :], r[:, o:o + TW], start=(k == NDV), stop=(k == 30))
                        S.stt('dve', ucv[:, cc, tt * TW:(tt + 1) * TW], ps[:, 0:TW], cfb(cc), ac[:], ALU.add, ALU.add)
            S.fence()
            with ExitStack() as es2:
                wst = None
                wbZ = sbt(es2, "wbZ", [128, 8, 512], BF16)
                sqc = [sbt(es2, "sqc%d" % i, [128, 4, TW], BF16) for i in range(2)]
                mean = [sbt(es2, "mean%d" % i, [128, TW]) for i in range(2)]
                msq = [sbt(es2, "msq%d" % i, [128, TW]) for i in range(2)]
                rsd = [sbt(es2, "rsd%d" % i, [128, TW]) for i in range(2)]
                szb = [sbt(es2, "szb%d" % i, [128, TW]) for i in range(2)]
                t1 = [sbt(es2, "t1_%d" % i, [128, TW]) for i in range(2)]
                t3 = [sbt(es2, "t3_%d" % i, [128, TW]) for i in range(2)]
                pbt = [sbt(es2, "pbt%d" % i, [128, TW], BF16) for i in range(2)]
                load_w(es2, w_in[l][:, C_ZB:C_ZB + 512], 8, 512, wst, wbZ)
                n = 0
                for tt in range(NTT):
                    b = tt % 2
                    tsl = slice(tt * TW, (tt + 1) * TW)
                    for cc in range(4):
                        S.act(sqc[b][:, cc, :], ucv[:, cc, tsl], AF.Square)
                    pm = PF()
                    pq = PF()
                    for cc in range(4):
                        S.mm(pm[:, 0:TW], onesLN_b[:], ucv[:, cc, tsl], start=(cc == 0), stop=(cc == 3))
                    for cc in range(4):
                        S.mm(pq[:, 0:TW], onesLN_b[:], sqc[b][:, cc, :], start=(cc == 0), stop=(cc == 3))
                    S.copy('act', mean[b][:], pm[:, 0:TW])
                    S.tt('pool', msq[b][:], mean[b][:], mean[b][:], ALU.mult)
                    S.tt('dve', msq[b][:], pq[:, 0:TW], msq[b][:], ALU.subtract)
                    rsqrt_to(es2, rsd[b][:], msq[b][:], EPS, wide=True)
                    for cc in range(4):
                        n += 1
                        m = n % 2
                        pz = PF()
                        proj(wbZ, cc * 128, tt * TW, TW, pz)
                        S.act(szb[m][:], pz[:, 0:TW], AF.Silu)
                        S.tt('dve', t1[m][:], ucv[:, cc, tsl], mean[b][:], ALU.subtract)
                        S.tt('pool', t1[m][:], t1[m][:], rsd[b][:], ALU.mult)
                        S.act(t3[m][:], t1[m][:], AF.Silu, bias=lnb(cc), scale=lnw(cc))
                        S.tt('dve', pbt[m][:], t3[m][:], szb[m][:], ALU.mult)
                        S.dma(pbS[cc * 128:(cc + 1) * 128, tsl], pbt[m][:], q='act')
        if nph <= 5:
            break
        S.fence()

        with ExitStack() as es:
            wst = [None, None]
            wg = sbt(es, "wg", [128, 8, 2048], BF16)
            pat = [sbt(es, "pat4_%d" % i, [128, 4, TW], BF16) for i in range(2)]
            pbt = [sbt(es, "pbt4_%d" % i, [128, 4, TW], BF16) for i in range(2)]
            sg0 = [sbt(es, "sg0_%d" % i, [128, TW]) for i in range(2)]
            sg1 = [sbt(es, "sg1_%d" % i, [128, TW]) for i in range(2)]
            yT = [sbt(es, "yT%d" % i, [128, 8, TW], BF16) for i in range(2)]
            xt4 = [sbt(es, "xt4_%d" % i, [128, DM]) for i in range(4)]
            xn = xt4
            junk = sbt(es, "junk4", [128, DM], BF16)
            ss4 = sbt(es, "ss4", [128, 8])
            if last:
                fnw_bc = sbt(es, "fnw_bc", [128, DM])
                S.dma(fnw_bc[:], final_norm_w.partition_broadcast(128))
            elif fuseA:
                nwF = sbt(es, "nwF", [128, DM])
                S.dma(nwF[:], norm_w[l + 1].partition_broadcast(128))
                xqF = sbt(es, "xqF", [128, DM], BF16)
            def ldg(q, eng):
                load_w(es, w_in[l][:, C_GATE + q * 256:C_GATE + (q + 1) * 256], 8, 256, wst[q % 2], wg[:, :, q * 256:(q + 1) * 256], eng=eng)
            for q in range(4):
                ldg(q, 'pool')
                ldg(q + 4, 'act')
            n = 0
            xi = 0
            for tt in range(NTT):
                b = tt % 2
                tsl = slice(tt * TW, (tt + 1) * TW)
                S.dma(pat[b][:], paS[:, tsl].rearrange("(h e) t -> e h t", e=128))
                S.dma(pbt[b][:], pbS[:, tsl].rearrange("(h e) t -> e h t", e=128))
                for c in range(8):
                    n += 1
                    m = n % 2
                    pA = PF()
                    pB = PF()
                    p0 = PF()
                    p1 = PF()
                    cs = slice(c * 128, (c + 1) * 128)
                    for h in range(4):
                        S.mm(pA[:, 0:TW], wdn[:, h, cs], pat[b][:, h, :], start=(h == 0), stop=(h == 3))
                    for h in range(4):
                        S.mm(pB[:, 0:TW], wcf[:, h, cs], pbt[b][:, h, :], start=(h == 0), stop=(h == 3))
                    proj(wg, c * 128, tt * TW, TW, p0)
                    proj(wg, 1024 + c * 128, tt * TW, TW, p1)
                    S.act(sg0[m][:], p0[:, 0:TW], AF.Sigmoid, bias=gateb(c))
                    S.act(sg1[m][:], p1[:, 0:TW], AF.Sigmoid, bias=gateb(8 + c))
                    S.tt('dve', sg0[m][:], pA[:, 0:TW], sg0[m][:], ALU.mult)
                    S.tt('dve', sg1[m][:], pB[:, 0:TW], sg1[m][:], ALU.mult)
                    S.tt('pool', yT[b][:, c, :], sg0[m][:], sg1[m][:], ALU.add)
                for j in range(JT):
                    xi += 1
                    m = xi % 4
                    rows = slice(tt * TW + j * 128, tt * TW + (j + 1) * 128)
                    S.dma(xt4[m][:], x_cur[rows, :])
                    for half in range(2):
                        po = PF()
                        for c in range(8):
                            S.mm(po[:, 0:512], yT[b][:, c, j * 128:(j + 1) * 128], wo[:, c, half * 512:(half + 1) * 512],
                                 start=(c == 0), stop=(c == 7))
                        S.tt('dve', xn[m][:, half * 512:(half + 1) * 512], po[:, 0:512],
                             xt4[m][:, half * 512:(half + 1) * 512], ALU.add)
                    if not last:
                        S.dma(xs[rows, :], xn[m][:])
                        if fuseA:
                            sq = ss4[:, m * 2:m * 2 + 1]
                            rs = ss4[:, m * 2 + 1:m * 2 + 2]
                            S.act(junk[:], xn[m][:], AF.Square, accum_out=sq)
                            rsqrt_to(es, rs, sq, EPS, 1.0 / DM)
                            S.stt('dve', xqF[:], xn[m][:], rs, nwF[:], ALU.mult, ALU.mult)
                            for half in range(2):
                                ph = PH()
                                for k in range(4):
                                    c = half * 4 + k
                                    S.tr(ph[:, k * 128:(k + 1) * 128], xqF[:, c * 128:(c + 1) * 128], ident_b[:])
                                S.copy('act' if half == 0 else 'dve', hT[:, half * 4:half * 4 + 4, rows],
                                       ph[:, 0:512].rearrange("p (k t) -> p k t", t=128))
                    else:
                        sq = ss4[:, m * 2:m * 2 + 1]
                        rs = ss4[:, m * 2 + 1:m * 2 + 2]
                        S.act(junk[:], xn[m][:], AF.Square, accum_out=sq)
                        rsqrt_to(es, rs, sq, EPS, 1.0 / DM)
                        S.stt('dve', xt4[m][:], xn[m][:], rs, fnw_bc[:], ALU.mult, ALU.mult)
                        S.dma(out[rows, :], xt4[m][:])
        S.fence()
        esW.close()
        lay.close()

    S.final()
    semstack = ExitStack()
    stats = S.emit(lambda name: semstack.enter_context(nc.semaphore(name)), reorder=reorder)
    return nc, stats, tapo


_CACHE = {}


def kernel(**inputs):
    x = np.ascontiguousarray(np.asarray(inputs['x'], dtype=np.float32))
    B, L, _ = x.shape
    key = (L,)
    if key not in _CACHE:
        _CACHE[key] = build(L)
    nc, stats, _ = _CACHE[key]
    shared = {}
    for k in ('norm_w', 'w_in', 'qkv_conv_w', 'dn_norm_w', 'w_dn_out', 'cf_conv_w', 'cf_conv_b',
              'cf_ln_w', 'cf_ln_b', 'w_cf_out', 'gate_b', 'w_out', 'final_norm_w'):
        shared[k] = np.ascontiguousarray(np.asarray(inputs[k], dtype=np.float32))
    shared['a_log'] = np.ascontiguousarray(np.asarray(inputs['a_log'], dtype=np.float32).reshape(DEPTH, 8))
    shared['dt_bias'] = np.ascontiguousarray(np.asarray(inputs['dt_bias'], dtype=np.float32).reshape(DEPTH, 8))
    in_maps = [dict(shared, x=x[b]) for b in range(B)]
    res = run_bass_kernel_spmd(nc, in_maps, core_ids=list(range(B)))
    return np.stack([np.asarray(r['out'], dtype=np.float32) for r in res.results], axis=0)
```

```python
import numpy as np
import concourse.bass as bass
import concourse.mybir as mybir

F32 = mybir.dt.float32
BF16 = mybir.dt.bfloat16
I32 = mybir.dt.int32
AF = mybir.ActivationFunctionType
ALU = mybir.AluOpType
AX = mybir.AxisListType


def _prod(xs):
    r = 1
    for v in xs:
        r *= int(v)
    return r


def box(a):
    t = a.tensor
    name = t.name
    apl = a.ap
    off = int(a.offset)
    space = str(a.space)
    if 'DRAM' in space.upper():
        hi = off + sum((c - 1) * abs(s) for s, c in apl) + 1
        return (name, 0, 1, off, hi)
    if 'PSUM' in space.upper():
        return (name, 0, 128, 0, 1 << 30)
    pstride = _prod(t.shape[1:])
    p0 = off // pstride
    f0 = off % pstride
    ps, pc = apl[0]
    f1 = f0 + sum((c - 1) * abs(s) for s, c in apl[1:]) + 1
    return (name, p0, p0 + pc, f0, f1)


def boxes(a):
    b = box(a)
    space = str(a.space).upper()
    if 'DRAM' in space or 'PSUM' in space:
        return [b]
    apl = a.ap
    if len(apl) < 3:
        return [b]
    s1, c1 = apl[1]
    inner = sum((c - 1) * abs(s) for s, c in apl[2:]) + 1
    if c1 <= 1 or c1 > 16 or s1 <= 0 or inner > s1:
        return [b]
    name, p0, p1, f0, _ = b
    return [(name, p0, p1, f0 + k * s1, f0 + k * s1 + inner) for k in range(c1)]


def bc_last(ap, n):
    return bass.AP(ap.tensor, ap.offset, [list(e) for e in ap.ap] + [[0, n]])


def bc_mid(ap, n):
    l = [list(e) for e in ap.ap]
    return bass.AP(ap.tensor, ap.offset, [l[0], [0, n]] + l[1:])


CP_DEFAULT = '1'


class Sched:
    ROT = 8000
    NDMA = 64

    def __init__(self, nc):
        self.nc = nc
        self.engs = {'pe': nc.tensor, 'act': nc.scalar, 'dve': nc.vector,
                     'pool': nc.gpsimd, 'sp': nc.sync}
        self.ops = []

    def add(self, eng, fn, reads, writes, dma=False):
        reads = [a for a in reads if a is not None]
        writes = [a for a in writes if a is not None]
        w0 = writes[0]
        n = 1
        for s_, c_ in list(w0.ap)[1:]:
            n *= int(c_)
        psrc = any('PSUM' in str(a.space).upper() for a in reads)
        f32 = bool(reads) and all(a.dtype == F32 for a in reads[:2])
        nbytes = n * int(list(w0.ap)[0][1]) * (4 if w0.dtype == F32 else 2)
        self.ops.append(dict(eng=eng, fn=fn, r=[bb for a in reads for bb in boxes(a)],
                             w=[bb for a in writes for bb in boxes(a)], dma=dma, fence=False,
                             n=n, psrc=psrc, f32=f32, nbytes=nbytes, nrd=len(reads)))

    def fence(self):
        self.ops.append(dict(fence=True))

    def capture(self):
        self._saved = self.ops
        self.ops = []

    def release(self):
        lst = self.ops
        self.ops = self._saved
        return lst

    def merge(self, *lists):
        n = max(len(x) for x in lists)
        for i in range(n):
            for x in lists:
                if i < len(x):
                    self.ops.append(x[i])

    def final(self):
        self.ops.append(dict(fence=False, final=True, eng='sp', fn=None, dma=False, r=[], w=[]))

    def mm(self, out, lhsT, rhs, start=True, stop=True):
        self.add('pe', lambda: self.nc.tensor.matmul(out, lhsT, rhs, start=start, stop=stop),
                 [lhsT, rhs], [out])

    def tr(self, out, in_, ident):
        self.add('pe', lambda: self.nc.tensor.transpose(out, in_, ident), [in_, ident], [out])

    def act(self, out, in_, func, bias=None, scale=None, accum_out=None):
        kw = {}
        rd = [in_]
        if bias is not None:
            kw['bias'] = bias
            if not isinstance(bias, (int, float)):
                rd.append(bias)
        if scale is not None:
            kw['scale'] = scale
            if not isinstance(scale, (int, float)):
                rd.append(scale)
        wr = [out]
        if accum_out is not None:
            kw['accum_out'] = accum_out
            wr.append(accum_out)
        self.add('act', lambda: self.nc.scalar.activation(out, in_, func, **kw), rd, wr)
        self.ops[-1]['actg'] = {AF.Silu: 'silu', AF.Sigmoid: 'sig', AF.Sqrt: 'sqrt', AF.Exp: 'exp', AF.Ln: 'exp'}.get(func)

    def tt(self, eng, out, in0, in1, op):
        e = self.engs[eng]
        self.add(eng, lambda: e.tensor_tensor(out, in0, in1, op), [in0, in1], [out])

    def ts(self, eng, out, in0, s1, s2, op0, op1=None):
        e = self.engs[eng]
        rd = [in0] + [s for s in (s1, s2) if s is not None and not isinstance(s, (int, float))]
        if op1 is None:
            self.add(eng, lambda: e.tensor_single_scalar(out, in0, s1, op0), rd, [out])
        else:
            self.add(eng, lambda: e.tensor_scalar(out, in0, s1, s2, op0, op1), rd, [out])

    def stt(self, eng, out, in0, scalar, in1, op0, op1):
        e = self.engs[eng]
        rd = [in0, in1] + ([scalar] if not isinstance(scalar, (int, float)) else [])
        self.add(eng, lambda: e.scalar_tensor_tensor(out, in0, scalar, in1, op0, op1), rd, [out])

    def copy(self, eng, out, in_):
        if eng == 'act':
            self.add('act', lambda: self.nc.scalar.copy(out, in_), [in_], [out])
        else:
            e = self.engs[eng]
            self.add(eng, lambda: e.tensor_copy(out, in_), [in_], [out])

    def memset(self, eng, ap, val):
        e = self.engs[eng]
        self.add(eng, lambda: e.memset(ap, val), [], [ap])

    def dma(self, out, in_, q='sp', **kw):
        e = self.engs[q]
        self.add(q, lambda: e.dma_start(out=out, in_=in_, **kw), [in_], [out], dma=True)

    def dma_cast(self, out, in_):
        self.add('pool', lambda: self.nc.gpsimd.dma_start(out=out, in_=in_), [in_], [out], dma=True)

    @staticmethod
    def _ovl(a, b):
        return a[1] < b[2] and b[1] < a[2] and a[3] < b[4] and b[3] < a[4]

    def _dur(self, op):
        e = op['eng']
        n = op['n']
        if op['dma']:
            return 1200.0 if e == 'pool' else 120.0
        if e == 'pe':
            return max(100.0, (4.0 if op['f32'] else 1.0) * n / 2.35 + 8.0)
        if e == 'act':
            return (200.0 + n) / 1.2
        if e == 'dve':
            if op['psrc']:
                return (n + 200.0) / 0.96
            if op['nrd'] >= 2:
                return (n + 151.0) / 0.96
            return (n / 2.0 + 151.0) / 0.96
        if e == 'pool':
            return 60.0 + 1.95 * n
        return 100.0

    def emit(self, sem_ctx, reorder=True):
        import heapq
        ops_all = self.ops
        CE = ('pe', 'act', 'dve', 'pool')
        ALLE = list(CE) + ['sp']
        segs = [[]]
        final_op = None
        for op in ops_all:
            if op['fence']:
                segs.append([])
            elif op.get('final'):
                final_op = op
            else:
                segs[-1].append(op)
        order = []
        nid = 0
        for seg in segs:
            if not seg:
                continue
            state = {}
            for op in seg:
                op['id'] = nid
                nid += 1
                deps = set()
                me = op['id']
                key = op['eng'] if not op['dma'] else ('d', me)
                for b in op['r']:
                    covered = False
                    lst = state.setdefault(b[0], [])
                    for rec in lst:
                        if self._ovl(rec[0], b):
                            if rec[1] is not None:
                                deps.add(rec[1])
                            if b[4] == (1 << 30):
                                for id2, k2 in rec[2].items():
                                    if k2 != key:
                                        deps.add(id2)
                            rec[2][me] = key
                            rb = rec[0]
                            if rb[1] <= b[1] and b[2] <= rb[2] and rb[3] <= b[3] and b[4] <= rb[4]:
                                covered = True
                    if not covered:
                        lst.append([b, None, {me: key}])
                for b in op['w']:
                    lst = state.setdefault(b[0], [])
                    keep = []
                    for rec in lst:
                        if self._ovl(rec[0], b):
                            if rec[1] is not None:
                                deps.add(rec[1])
                            deps.update(rec[2].keys())
                            rb = rec[0]
                            if b[1] <= rb[1] and rb[2] <= b[2] and b[3] <= rb[3] and rb[4] <= b[4]:
                                continue
                        keep.append(rec)
                    keep.append([b, me, {}])
                    state[b[0]] = keep
                deps.discard(me)
                op['deps'] = deps
            base = seg[0]['id']
            if not reorder:
                order.append(list(seg))
                continue
            nseg = len(seg)
            succ = [[] for _ in range(nseg)]
            npred = [0] * nseg
            for op in seg:
                i = op['id'] - base
                npred[i] = len(op['deps'])
                for d in op['deps']:
                    succ[d - base].append(i)
            ready_t = [0.0] * nseg
            done_t = [0.0] * nseg
            use_cp = CP_DEFAULT == '1'
            blev = [0.0] * nseg
            if use_cp:
                for i in range(nseg - 1, -1, -1):
                    m_ = 0.0
                    for s in succ[i]:
                        if blev[s] > m_:
                            m_ = blev[s]
                    blev[i] = m_ + self._dur(seg[i]) + (2000.0 if seg[i]['dma'] else 0.0)
                for i in range(nseg):
                    blev[i] += seg[i].get('boost', 0.0)
            tfree = {e: 0.0 for e in ALLE}
            fut = {e: [] for e in ALLE}
            avl = {e: [] for e in ALLE}
            for i, op in enumerate(seg):
                if npred[i] == 0:
                    heapq.heappush(fut[op['eng']], (0.0, i))
            new = []
            act_grp = [None]
            LAT = 200.0
            left = nseg
            while left:
                best = None
                for e in ALLE:
                    f, a = fut[e], avl[e]
                    while f and f[0][0] <= tfree[e]:
                        k_ = heapq.heappop(f)[1]
                        heapq.heappush(a, (-blev[k_], k_))
                    if a:
                        cand = (tfree[e], a[0][1], e, True)
                    elif f:
                        cand = (f[0][0], f[0][1], e, False)
                    else:
                        continue
                    if best is None or cand[:2] < best[:2]:
                        best = cand
                st, i, e, from_a = best
                if from_a:
                    if e == 'act' and len(avl[e]) > 1:
                        g0 = seg[i].get('actg')
                        if g0 is not None and g0 != act_grp[0]:
                            cands = [kk for kk in avl[e] if seg[kk[1]].get('actg') in (None, act_grp[0])]
                            if cands:
                                i = min(cands)[1]
                        avl[e].remove((-blev[i], i))
                        heapq.heapify(avl[e])
                    else:
                        heapq.heappop(avl[e])
                else:
                    heapq.heappop(fut[e])
                op = seg[i]
                d = self._dur(op)
                if e == 'act':
                    g = op.get('actg')
                    if g is not None and g != act_grp[0]:
                        d += 1300.0
                        act_grp[0] = g
                tfree[e] = st + d
                fin = st + d
                if op['dma']:
                    fin = st + 1800.0 + op['nbytes'] / 180.0
                done_t[i] = fin
                new.append(op)
                left -= 1
                for s in succ[i]:
                    if fin + LAT > ready_t[s]:
                        ready_t[s] = fin + LAT
                    npred[s] -= 1
                    if npred[s] == 0:
                        heapq.heappush(fut[seg[s]['eng']], (ready_t[s], s))
            order.append(new)
            self.seg_ms = getattr(self, 'seg_ms', []) + [(nseg, round(max(done_t) / 1e3), {e: round(sum(self._dur(o) for o in seg if o['eng'] == e) / 1e3) for e in ALLE})]
        pos = {}
        engcnt = {e: 0 for e in CE}
        dmaidx = {}
        dmaq = {}
        NQ = {'sp': self.NDMA, 'pool': 14, 'act': 14}
        nq = {'sp': 0, 'pool': 0, 'act': 0}
        ndma = 0
        flat = []
        for si, seg in enumerate(order):
            for op in seg:
                op['seg'] = si
                if op['dma']:
                    q_ = op['eng']
                    dmaidx[op['id']] = nq[q_]
                    dmaq[op['id']] = q_
                    nq[q_] += 1
                    ndma += 1
                else:
                    pos[op['id']] = (op['eng'], engcnt[op['eng']])
                    engcnt[op['eng']] += 1
                flat.append(op)
        waited = {e: {p: -1 for p in CE} for e in ALLE}
        waited_d = {e: set() for e in ALLE}
        last_pos = {p: -1 for p in CE}
        need_e = set()
        dma_ids_in_order = {'sp': [], 'pool': [], 'act': []}
        fence_pending = {e: None for e in ALLE}
        cur_seg = 0
        for op in flat:
            if op['seg'] != cur_seg:
                cur_seg = op['seg']
                snap = (dict(last_pos), [d_ for q_ in NQ for d_ in dma_ids_in_order[q_][-NQ[q_]:]])
                for e in ALLE:
                    fence_pending[e] = snap
            eng = op['eng']
            best = {}
            dwait = []
            if fence_pending[eng] is not None:
                lp, dl = fence_pending[eng]
                fence_pending[eng] = None
                for p, s in lp.items():
                    if s >= 0 and not (eng == 'pe' and p == 'pe'):
                        best[p] = s
                for di in dl:
                    if di not in waited_d[eng]:
                        waited_d[eng].add(di)
                        dwait.append(di)
            if op['dma']:
                k = dmaidx[op['id']]
                q_ = op['eng']
                if k >= NQ[q_]:
                    di = dma_ids_in_order[q_][k - NQ[q_]]
                    if di not in waited_d[eng]:
                        waited_d[eng].add(di)
                        dwait.append(di)
            for d in op['deps']:
                if d in dmaidx:
                    if d not in waited_d[eng]:
                        waited_d[eng].add(d)
                        dwait.append(d)
                else:
                    p, s = pos[d]
                    if eng == 'pe' and p == 'pe':
                        continue
                    if s > best.get(p, -1):
                        best[p] = s
            ew = []
            for p, s in best.items():
                if s > waited[eng][p]:
                    waited[eng][p] = s
                    ew.append((p, s))
                    need_e.add((p, s))
            op['ew'] = ew
            op['dw'] = dwait
            if op['dma']:
                dma_ids_in_order[op['eng']].append(op['id'])
            else:
                last_pos[eng] = pos[op['id']][1]
        fin_e = [(p, s) for p, s in last_pos.items() if s > waited['sp'][p]]
        for ps_ in fin_e:
            need_e.add(ps_)
        fin_d = [d for q_ in NQ for d in dma_ids_in_order[q_][-NQ[q_]:] if d not in waited_d['sp']]
        cnt = {e: 0 for e in CE}
        evval = {}
        for op in flat:
            if not op['dma']:
                ps_ = pos[op['id']]
                if ps_ in need_e:
                    c = cnt[ps_[0]]
                    cnt[ps_[0]] += 1
                    evval[ps_] = ('%s_%d' % (ps_[0], c // self.ROT), c % self.ROT + 1)
        sems = {}

        def get_sem(name):
            if name not in sems:
                sems[name] = sem_ctx(name)
            return sems[name]

        def dma_sem(d):
            k = dmaidx[d]
            q_ = dmaq[d]
            return get_sem({'sp': 'dma_%d', 'pool': 'sdma_%d', 'act': 'adma_%d'}[q_] % (k % NQ[q_])), 16 * (k // NQ[q_] + 1)
        nwait = 0
        for op in flat:
            e = self.engs[op['eng']]
            for ps_ in op['ew']:
                sname, val = evval[ps_]
                e.wait_ge(get_sem(sname), val)
                nwait += 1
            for d in op['dw']:
                sm, val = dma_sem(d)
                e.wait_ge(sm, val)
                nwait += 1
            ins = op['fn']()
            if op['dma']:
                sm, _ = dma_sem(op['id'])
                ins.then_inc(sm, 16)
            else:
                ps_ = pos[op['id']]
                if ps_ in evval:
                    ins.then_inc(get_sem(evval[ps_][0]), 1)
        sp = self.engs['sp']
        for ps_ in fin_e:
            sname, val = evval[ps_]
            sp.wait_ge(get_sem(sname), val)
        for d in fin_d:
            sm, val = dma_sem(d)
            sp.wait_ge(sm, val)
        self.stats = dict(nops=len(flat), nwait=nwait, nsem=len(sems), cnt=dict(cnt), ndma=ndma)
        return self.stats


from contextlib import ExitStack
from concourse.bass_utils import run_bass_kernel_spmd

DM = 1024
DIN = 5648
C_Q, C_K, C_V, C_ZA, C_AB, C_CFV, C_CFG, C_ZB, C_GATE = 0, 512, 1024, 1536, 2048, 2064, 2576, 3088, 3600
EPS = 1e-6
NEG = -30000.0
DEPTH = 2
OPT = dict(cb='a', rebal='0', fuseA='1', zip='1')


def build(L, depth=DEPTH, taps=(), nph=99, cph=99, reorder=True):
    nc = bass.Bass("TRN2", target_bir_lowering=False)
    NT = L // 128
    TW = min(512, L)
    NTT = L // TW
    JT = TW // 128
    HAL = 16

    def din(name, shape):
        return nc.dram_tensor(name, shape, F32, kind="ExternalInput").ap()
    x_in = din("x", [L, DM])
    norm_w = din("norm_w", [depth, DM])
    w_in = din("w_in", [depth, DM, DIN])
    qkv_conv_w = din("qkv_conv_w", [depth, 5, 1536])
    a_log = din("a_log", [depth, 8])
    dt_bias = din("dt_bias", [depth, 8])
    dn_norm_w = din("dn_norm_w", [depth, 128])
    w_dn_out = din("w_dn_out", [depth, 512, DM])
    cf_conv_w = din("cf_conv_w", [depth, 31, 512])
    cf_conv_b = din("cf_conv_b", [depth, 512])
    cf_ln_w = din("cf_ln_w", [depth, 512])
    cf_ln_b = din("cf_ln_b", [depth, 512])
    w_cf_out = din("w_cf_out", [depth, 512, DM])
    gate_b = din("gate_b", [depth, 2048])
    w_out = din("w_out", [depth, DM, DM])
    final_norm_w = din("final_norm_w", [DM])
    out = nc.dram_tensor("out", [L, DM], F32, kind="ExternalOutput").ap()
    tapo = {}

    def scr(name, shape, dt):
        return nc.dram_tensor(name, shape, dt).ap()
    xs = scr("xs", [L, DM], F32)
    QtS = scr("QtS", [512, L], BF16)
    KtS = scr("KtS", [512, L], BF16)
    KS = scr("KS", [L, 512], BF16)
    VS = scr("VS", [L, 512], BF16)
    szaS = scr("szaS", [512, L], BF16)
    oS = scr("oS", [L, 512], F32)
    oSb = scr("oSb", [L, 512], F32)
    paS = scr("paS", [512, L], BF16)
    pbS = scr("pbS", [512, L], BF16)

    S = Sched(nc)
    top = ExitStack()

    uniq = {'n': 0}

    def sbt(es, name, shape, dt=F32):
        uniq['n'] += 1
        return es.enter_context(nc.sbuf_tensor("%s_u%d" % (name, uniq['n']), shape, dt))

    psF = [top.enter_context(nc.psum_tensor("psF%d" % i, [128, 512], F32)) for i in range(6)]
    psH = [top.enter_context(nc.psum_tensor("psH%d" % i, [128, 1024], BF16)) for i in range(2)]
    rot = {'f': 0, 'h': 0}

    cur = {'c': None}
    rotc = [0, 0]
    roth = [0, 0]

    def PF():
        c = cur['c']
        if c is not None:
            st_ = cur.get('stage')
            if OPT['cb'] == 'a':
                if st_ == 'scan':
                    return psF[3 * c + 2]
                rotc[c] = (rotc[c] + 1) % 2
                return psF[3 * c + rotc[c]]
            if OPT['cb'] == 'e':
                if st_ == 'scan':
                    return psF[4 + c]
                rotc[0] = (rotc[0] + 1) % 4
                return psF[rotc[0]]
            if OPT['cb'] == 'b':
                return psF[3 * c + {'prep': 0, 'neu': 1, 'scan': 2}[st_]]
            if OPT['cb'] == 'c':
                if st_ != 'scan':
                    return psF[3 * c]
                rotc[c] = (rotc[c] + 1) % 2
                return psF[3 * c + 1 + rotc[c]]
            rotc[c] = (rotc[c] + 1) % 3
            return psF[3 * c + rotc[c]]
        rot['f'] = (rot['f'] + 1) % 6
        return psF[rot['f']]

    def PH():
        c = cur['c']
        if c is not None:
            roth[c] ^= 1
            return psH[c][:, roth[c] * 512:roth[c] * 512 + 512]
        rot['h'] = (rot['h'] + 1) % 2
        return psH[rot['h']][:, 0:512]

    alt = {'i': 0}

    def AD():
        alt['i'] ^= 1
        return 'act' if alt['i'] else 'dve'

    ip_i = sbt(top, "ip_i", [128, 128], I32)
    ij_i = sbt(top, "ij_i", [128, 128], I32)
    ip = sbt(top, "ip", [128, 128])
    ij = sbt(top, "ij", [128, 128])
    cst = sbt(top, "cst", [128, 12, 128])
    ident_b = sbt(top, "ident_b", [128, 128], BF16)
    cstb = sbt(top, "cstb", [128, 4, 128], BF16)
    ones128_b = sbt(top, "ones128_b", [128, 128], BF16)
    ones1_b = sbt(top, "ones1_b", [128, 128], BF16)
    onesLN_b = sbt(top, "onesLN_b", [128, 128], BF16)
    hT = sbt(top, "hT", [128, 8, L], BF16)
    S.add('pool', lambda: nc.gpsimd.iota(ip_i[:], [[0, 128]], base=0, channel_multiplier=1), [], [ip_i[:]])
    S.add('pool', lambda: nc.gpsimd.iota(ij_i[:], [[1, 128]], base=0, channel_multiplier=0), [], [ij_i[:]])
    S.copy('dve', ip[:], ip_i[:])
    S.copy('dve', ij[:], ij_i[:])
    IDENT, LE, LT, GE, GT, NEGF, NEGB, BLKA, BLKB, SAME, T0, T1 = [cst[:, i, :] for i in range(12)]
    S.tt('dve', IDENT, ip[:], ij[:], ALU.is_equal)
    S.ts('dve', T0, ip[:], 64.0, None, ALU.is_ge)
    S.ts('dve', T1, ij[:], 64.0, None, ALU.is_ge)
    S.tt('dve', SAME, T0, T1, ALU.is_equal)
    for dst, op in ((LE, ALU.is_le), (LT, ALU.is_lt), (GE, ALU.is_ge), (GT, ALU.is_gt)):
        S.tt('dve', dst, ip[:], ij[:], op)
        S.tt('dve', dst, dst, SAME, ALU.mult)
    S.ts('dve', NEGF, GE, -1.0, -NEG, ALU.add, ALU.mult)
    S.ts('dve', NEGB, LE, -1.0, -NEG, ALU.add, ALU.mult)
    S.ts('dve', BLKA, ip[:], 64.0, None, ALU.is_lt)
    S.ts('dve', BLKB, ip[:], 64.0, None, ALU.is_ge)
    S.copy('dve', ident_b[:], IDENT)
    GTb, LTb, NEGFb, NEGBb = [cstb[:, i, :] for i in range(4)]
    for dst_, src_ in ((GTb, GT), (LTb, LT), (NEGFb, NEGF), (NEGBb, NEGB)):
        S.copy('dve', dst_, src_)
    S.memset('dve', ones128_b[:], 128.0)
    S.memset('dve', ones1_b[:], 1.0)
    S.memset('dve', onesLN_b[:], 1.0 / 512.0)

    def load_w(es_buf, src, nk, ncols, wst, wdst, eng='pool'):
        srcv = src.rearrange("(c p) n -> p c n", p=128)
        step = 256 if ncols > 256 else ncols
        for c0 in range(0, ncols, step):
            S.dma_cast(wdst[:, 0:nk, c0:c0 + step], srcv[:, :, c0:c0 + step])

    def proj(wb, col0, t0, tw, ps):
        for c in range(8):
            S.mm(ps[:, 0:tw], wb[:, c, col0:col0 + 128], hT[:, c, t0:t0 + tw], start=(c == 0), stop=(c == 7))

    def rsqrt_to(es_tmp, dst, src, eps, scale=1.0, wide=False):
        if wide:
            S.act(dst, src, AF.Ln, bias=eps, scale=scale)
            S.act(dst, dst, AF.Exp, scale=-0.5)
        else:
            S.act(dst, src, AF.Sqrt, bias=eps, scale=scale)
            S.add('dve', lambda: nc.vector.reciprocal(dst, dst), [dst], [dst])

    for l in range(depth):
        x_cur = x_in if l == 0 else xs
        last = (l == depth - 1)
        lay = ExitStack()
        stA = sbt(lay, "stA%d" % l, [128, 128])
        stB = sbt(lay, "stB%d" % l, [128, 128])
        prmA = sbt(lay, "prmA%d" % l, [128, 128])
        prmB = sbt(lay, "prmB%d" % l, [128, 128])
        alog_bc = sbt(lay, "alog%d" % l, [128, 8])
        dtb_bc = sbt(lay, "dtb%d" % l, [128, 8])
        negA = sbt(lay, "negA%d" % l, [128, 8])
        S.memset('dve', stA[:], 0.0)
        S.memset('dve', stB[:], 0.0)
        S.dma(stA[0:8, :], norm_w[l].rearrange("(c p) -> c p", p=128))
        S.dma(stA[8:68, :], qkv_conv_w[l].rearrange("k (c p) -> (k c) p", p=128))
        S.dma(stA[68:69, :], dn_norm_w[l].rearrange("(c p) -> c p", p=128))
        S.dma(stA[69:73, :], cf_conv_b[l].rearrange("(c p) -> c p", p=128))
        S.dma(stA[73:77, :], cf_ln_w[l].rearrange("(c p) -> c p", p=128))
        S.dma(stA[77:81, :], cf_ln_b[l].rearrange("(c p) -> c p", p=128))
        S.dma(stA[81:97, :], gate_b[l].rearrange("(c p) -> c p", p=128))
        S.dma(stB[0:124, :], cf_conv_w[l].rearrange("k (c p) -> (k c) p", p=128))
        S.dma(alog_bc[:], a_log[l].partition_broadcast(128))
        S.dma(dtb_bc[:], dt_bias[l].partition_broadcast(128))
        ps = PF()
        S.tr(ps[:, 0:128], stA[:], IDENT)
        S.copy('dve', prmA[:], ps[:, 0:128])
        ps = PF()
        S.tr(ps[:, 0:128], stB[:], IDENT)
        S.copy('dve', prmB[:], ps[:, 0:128])
        S.act(negA[:], alog_bc[:], AF.Exp)
        S.ts('dve', negA[:], negA[:], -1.0, None, ALU.mult)

        def normw(c): return prmA[:, c:c + 1]
        def qkvw(k, cc): return prmA[:, 8 + k * 12 + cc: 9 + k * 12 + cc]
        dnw = prmA[:, 68:69]
        def cfb(cc): return prmA[:, 69 + cc:70 + cc]
        def lnw(cc): return prmA[:, 73 + cc:74 + cc]
        def lnb(cc): return prmA[:, 77 + cc:78 + cc]
        def gateb(c): return prmA[:, 81 + c:82 + c]
        def cfw(k, cc): return prmB[:, k * 4 + cc:k * 4 + cc + 1]

        tb = ExitStack()
        TAB = sbt(tb, "TAB%d" % l, [128, 8, NT * 8])
        GHL = sbt(tb, "GHL%d" % l, [128, 2, NT * 8], BF16)
        Gt, BETAt, EGCt, BEGCt, EGLGt, DLAt, DLBt = [TAB[:, i, :] for i in range(7)]
        esAB = ExitStack()
        fuseA = OPT['fuseA'] != '0'
        for es in ((esAB,) if not (fuseA and l > 0) else ()):
            nw_bc = sbt(esAB, "nwbc%d" % l, [128, DM])
            S.dma(nw_bc[:], norm_w[l].partition_broadcast(128))
            xb = [sbt(es, "xb%d" % i, [128, DM]) for i in range(4)]
            junk = sbt(es, "junkA", [128, DM])
            xsb = [sbt(es, "xsb%d" % i, [128, DM], BF16) for i in range(4)]
            ssA = sbt(es, "ssA", [128, 8])
            for i in range(NT):
                xt = xb[i % 4]
                xq = xsb[i % 4]
                sq = ssA[:, (i % 4) * 2:(i % 4) * 2 + 1]
                rs = ssA[:, (i % 4) * 2 + 1:(i % 4) * 2 + 2]
                S.dma(xt[:], x_cur[i * 128:(i + 1) * 128, :])
                S.act(junk[:], xt[:], AF.Square, accum_out=sq)
                rsqrt_to(es, rs, sq, EPS, 1.0 / DM)
                S.stt('dve', xq[:], xt[:], rs, nw_bc[:], ALU.mult, ALU.mult)
                for half in range(2):
                    ph = PH()
                    for k in range(4):
                        c = half * 4 + k
                        S.tr(ph[:, k * 128:(k + 1) * 128], xq[:, c * 128:(c + 1) * 128], ident_b[:])
                    S.copy('act' if half == 0 else 'dve', hT[:, half * 4:half * 4 + 4, i * 128:(i + 1) * 128],
                           ph[:, 0:512].rearrange("p (k t) -> p k t", t=128))
        if nph <= 1:
            break
        if 'hT' in taps and l == 0:
            tapo['hT'] = nc.dram_tensor("tap_hT", [128, 8, L], BF16, kind="ExternalOutput").ap()
            S.dma(tapo['hT'], hT[:])

        for es in (esAB,):
            wst = None
            wbf = [sbt(es, "wbfB%d" % i, [128, 8, 512], BF16) for i in range(2)]
            rb = [sbt(es, "rbB%d" % i, [128, L + 2 * HAL], BF16) for i in range(2)]
            dg = [sbt(es, "dgB%d" % i, [128, 5, 128], BF16) for i in range(2)]
            sil = [sbt(es, "silB%d" % i, [128, TW]) for i in range(4)]
            sqb = [sbt(es, "sqB%d" % i, [128, TW], BF16) for i in range(4)]
            sd = [sbt(es, "sdB%d" % i, [128, TW]) for i in range(4)]
            qn = [sbt(es, "qnB%d" % i, [128, TW], BF16) for i in range(4)]
            tok = [sbt(es, "tokB%d" % i, [128, JT, 128], BF16) for i in range(4)]
            for r in rb:
                S.memset('pool', r[:, 0:HAL], 0.0)
                S.memset('pool', r[:, HAL + L:], 0.0)
            n = 0
            for grp, (col0, kind) in enumerate(((C_Q, 'q'), (C_K, 'k'), (C_V, 'v'))):
                wb = wbf[grp % 2]
                load_w(es, w_in[l][:, col0:col0 + 512], 8, 512, wst, wb)
                for h in range(4):
                    cc = grp * 4 + h
                    r = rb[cc % 2]
                    d = dg[cc % 2]
                    for k in range(5):
                        S.act(d[:, k, :], IDENT, AF.Copy, scale=qkvw(k, cc))
                    for tt in range(NTT):
                        ps = PF()
                        proj(wb, h * 128, tt * TW, TW, ps)
                        S.copy('dve', r[:, HAL + tt * TW:HAL + (tt + 1) * TW], ps[:, 0:TW])
                    for tt in range(NTT):
                        n += 1
                        ps = PF()
                        for k in range(5):
                            o = HAL + tt * TW + k - 2
                            S.mm(ps[:, 0:TW], d[:, k, :], r[:, o:o + TW], start=(k == 0), stop=(k == 4))
                        tsl = slice(tt * TW, (tt + 1) * TW)
                        if kind == 'v':
                            vq = qn[n % 4]
                            S.act(vq[:], ps[:, 0:TW], AF.Silu)
                            src_bf = vq
                        else:
                            sl = sil[n % 4]
                            S.act(sl[:], ps[:, 0:TW], AF.Silu)
                            S.tt('pool', sqb[n % 4][:], sl[:], sl[:], ALU.mult)
                            ps3 = PF()
                            S.mm(ps3[:, 0:TW], (ones128_b if kind == 'q' else ones1_b)[:], sqb[n % 4][:])
                            sdd = sd[n % 4]
                            rsqrt_to(es, sdd[:], ps3[:, 0:TW], EPS * (128.0 if kind == 'q' else 1.0), wide=True)
                            src_bf = qn[n % 4]
                            S.tt('dve', src_bf[:], sl[:], sdd[:], ALU.mult)
                            S.dma((QtS if kind == 'q' else KtS)[h * 128:(h + 1) * 128, tsl], src_bf[:], q='act')
                        if kind != 'q':
                            ph = PH()
                            for j in range(JT):
                                S.tr(ph[:, j * 128:(j + 1) * 128], src_bf[:, j * 128:(j + 1) * 128], ident_b[:])
                            tk = tok[n % 4]
                            S.copy('dve', tk[:], ph[:, 0:JT * 128].rearrange("p (j d) -> p j d", d=128))
                            dstS = KS if kind == 'k' else VS
                            S.dma(dstS[tsl, h * 128:(h + 1) * 128].rearrange("(j t) d -> t j d", t=128), tk[:], q='act')
            wb = wbf[1]
            load_w(es, w_in[l][:, C_ZA:C_ZA + 512], 8, 512, wst, wb)
            for h in range(4):
                for tt in range(NTT):
                    n += 1
                    ps = PF()
                    proj(wb, h * 128, tt * TW, TW, ps)
                    S.act(qn[n % 4][:], ps[:, 0:TW], AF.Silu)
                    S.dma(szaS[h * 128:(h + 1) * 128, tt * TW:(tt + 1) * TW], qn[n % 4][:], q='act')
        if nph <= 2:
            break

        for es in (esAB,):
            wst = None
            b3_first = len(S.ops)
            w16 = sbt(es, "w16", [128, 8, 16], BF16)
            AB = sbt(es, "AB", [128, NT, 16])
            tmp = sbt(es, "tmp3", [128, 4, NT * 8])
            load_w(es, w_in[l][:, C_AB:C_AB + 16], 8, 16, wst, w16, eng='dve')
            for sc in range(NT):
                ps = PF()
                for c in range(8):
                    S.mm(ps[:, 0:16], hT[:, c, sc * 128:(sc + 1) * 128], w16[:, c, :], start=(c == 0), stop=(c == 7))
                S.copy(AD(), AB[:, sc, :], ps[:, 0:16])

            def v3(ap2):
                return ap2.rearrange("p (s h) -> p s h", h=8)
            xg, ax, e1, mx = [tmp[:, i, :] for i in range(4)]
            S.tt('dve', v3(xg), AB[:, :, 0:8], bc_mid(dtb_bc[:], NT), ALU.add)
            S.act(ax, xg, AF.Abs)
            S.act(e1, ax, AF.Exp, scale=-1.0)
            S.act(e1, e1, AF.Ln, bias=1.0)
            S.ts('dve', mx, xg, 0.0, None, ALU.max)
            S.tt('dve', mx, mx, e1, ALU.add)
            S.tt('dve', v3(Gt), v3(mx), bc_mid(negA[:], NT), ALU.mult)
            S.act(v3(BETAt), AB[:, :, 8:16], AF.Sigmoid)
            W = NT * 8

            def cum(maskF, maskB, dst):
                psa = PF()
                psb = PF()
                S.mm(psa[:, 0:W], maskF, Gt)
                S.mm(psb[:, 0:W], maskB, Gt)
                S.act(v3(dst)[:, :, 0:4], v3(psa[:, 0:W])[:, :, 0:4], AF.Exp)
                S.act(v3(dst)[:, :, 4:8], v3(psb[:, 0:W])[:, :, 4:8], AF.Exp)
            cum(LE, GE, EGCt)
            cum(GT, LT, EGLGt)
            cum(BLKA, BLKA, DLAt)
            cum(BLKB, BLKB, DLBt)
            S.tt('dve', BEGCt, BETAt, EGCt, ALU.mult)
            S.copy('dve', GHL[:, 0, :], Gt)
            S.tt('dve', TAB[:, 7, :], Gt, GHL[:, 0, :], ALU.subtract)
            S.copy('dve', GHL[:, 1, :], TAB[:, 7, :])
        for op_ in S.ops[b3_first:]:
            if not op_.get('fence'):
                op_['boost'] = 1e6
        if nph <= 3:
            break
        esAB.close()
        S.fence()
        if 'tab' in taps and l == 0:
            tapo['tab'] = nc.dram_tensor("tap_tab", [128, 8, NT * 8], F32, kind="ExternalOutput").ap()
            S.dma(tapo['tab'], TAB[:])

        with ExitStack() as es:
            def t2(name, shape, dt=BF16, nb=2):
                return [sbt(es, "%s%d" % (name, i), shape, dt) for i in range(nb)]
            KtT = t2("cKt", [128, 4, 128], BF16, 4); QtT = t2("cQt", [128, 4, 128], BF16, 4)
            KtokT = t2("cKtok", [128, 4, 128], BF16, 4); VtokT = t2("cVtok", [128, 4, 128], BF16, 4)
            gA = t2("cgA", [128, 4, 128], BF16)
            gAl = t2("cgAl", [128, 4, 128], BF16)
            Dm = t2("cD", [128, 4, 128], F32)
            attn = t2("cattn", [128, 4, 128]); Gm = t2("cGm", [128, 4, 128], F32)
            Lf = t2("cLf", [128, 4, 128], F32)
            Ub = t2("cU", [128, 4, 128], BF16, 12); Tb = t2("cT", [128, 4, 128], BF16, 12)
            Pb = t2("cP", [128, 4, 128], BF16, 12)
            attnT = t2("cattnT", [128, 4, 128], BF16, 4)
            KBG = t2("cKBG", [128, 4, 128], BF16, 4); BV = t2("cBV", [128, 4, 128], BF16, 4); KDEC = t2("cKDEC", [128, 4, 128], BF16, 4)
            NKC = t2("cNKC", [128, 4, 128], BF16, 4)
            vnewL = [t2("cvnew%d" % q, [128, 4, 128]) for q in range(2)]
            SfL = t2("cSf", [128, 4, 128], F32); StmpL = t2("cStmp", [128, 4, 128], F32)
            SbL = [t2("cSb%d" % q, [128, 4, 128]) for q in range(2)]
            ob = t2("cob", [128, 4, 128], F32, 4)
            sbiL = [0, 0]
            for q in range(2):
                S.memset('dve', SfL[q][:], 0.0)
                S.memset('dve', SbL[q][0][:], 0.0)
                S.memset('dve', SbL[q][1][:], 0.0)
            for it in range(NT):
                grp = []
                for dr in range(2):
                    if dr == 0:
                        S.capture()
                    else:
                        grp.append(S.release())
                        S.capture()
                    MA, MB, NEGM = (LE, GT, NEGF) if dr == 0 else (GE, LT, NEGB)
                    STRICT = MB
                    MBb, NEGMb = (GTb, NEGFb) if dr == 0 else (LTb, NEGBb)
                    sc = it if dr == 0 else NT - 1 - it
                    b = dr
                    b4 = dr * 2 + it % 2
                    cur['c'] = dr
                    cur['stage'] = 'prep'
                    Sf, Stmp, Sb, vnew = SfL[dr], StmpL[dr], SbL[dr], vnewL[dr]
                    sbi = sbiL[dr]
                    tsl = slice(sc * 128, (sc + 1) * 128)
                    hs = slice(sc * 8 + dr * 4, sc * 8 + dr * 4 + 4)
                    Kt, Qt, Ktok, Vtok = KtT[b4], QtT[b4], KtokT[b4], VtokT[b4]
                    S.dma(Kt[:], KtS[:, tsl].rearrange("(h d) t -> d h t", d=128))
                    S.dma(Qt[:], QtS[:, tsl].rearrange("(h d) t -> d h t", d=128))
                    S.dma(Ktok[:], KS[tsl, :].rearrange("t (h d) -> t h d", d=128))
                    S.dma(Vtok[:], VS[tsl, :].rearrange("t (h d) -> t h d", d=128))
                    S.tt('pool', gA[b][:], bc_mid(MA, 4), bc_last(GHL[:, 0, hs], 128), ALU.mult)
                    S.tt('pool', gAl[b][:], bc_mid(MA, 4), bc_last(GHL[:, 1, hs], 128), ALU.mult)
                    pd = PF()
                    for h in range(4):
                        S.mm(pd[:, h * 128:(h + 1) * 128], gA[b][:, h, :], MBb, start=True, stop=False)
                        S.mm(pd[:, h * 128:(h + 1) * 128], gAl[b][:, h, :], MBb, start=False, stop=False)
                        S.mm(pd[:, h * 128:(h + 1) * 128], ident_b[:], NEGMb, start=False, stop=True)
                    S.act(Dm[b][:], pd[:].rearrange("p (h j) -> p h j", j=128), AF.Exp)
                    if cph <= 1:
                        continue
                    pg = PF()
                    pq = PF()
                    for h in range(4):
                        S.mm(pg[:, h * 128:(h + 1) * 128], Kt[:, h, :], Kt[:, h, :])
                        S.mm(pq[:, h * 128:(h + 1) * 128], Qt[:, h, :], Kt[:, h, :])
                    v4 = lambda p: p[:].rearrange("p (h j) -> p h j", j=128)
                    S.tt('dve', attn[b][:], v4(pq), Dm[b][:], ALU.mult)
                    S.tt('dve', Gm[b][:], v4(pg), bc_mid(STRICT, 4), ALU.mult)
                    S.tt('pool', Lf[b][:], Gm[b][:], Dm[b][:], ALU.mult)
                    U0 = Ub[b4 * 3]
                    S.tt('dve' if OPT['rebal'] != '0' else 'pool', U0[:], Lf[b][:], bc_last(BETAt[:, hs], 128), ALU.mult)
                    if cph <= 2:
                        continue
                    pl = PH()
                    pa_ = PH()
                    for h in range(4):
                        S.tr(pl[:, h * 128:(h + 1) * 128], U0[:, h, :], ident_b[:])
                        S.tr(pa_[:, h * 128:(h + 1) * 128], attn[b][:, h, :], ident_b[:])
                    vh = lambda p: p.rearrange("p (h j) -> p h j", j=128)
                    T0_ = Tb[b4 * 3]
                    P0_ = Pb[b4 * 3]
                    S.copy('act', T0_[:], vh(pl))
                    S.tt('dve', P0_[:], bc_mid(IDENT, 4), vh(pl), ALU.subtract)
                    S.copy('act', attnT[b4][:], vh(pa_))
                    Uc, Tc, Pc = U0, T0_, P0_
                    cur['stage'] = 'neu' if OPT['cb'] == 'b' else 'prep'
                    for j in range(1, 6):
                        Un, Tn, Pn = Ub[b4 * 3 + j % 3], Tb[b4 * 3 + j % 3], Pb[b4 * 3 + j % 3]
                        pu = PF()
                        for h in range(4):
                            S.mm(pu[:, h * 128:(h + 1) * 128], Tc[:, h, :], Uc[:, h, :])
                        S.copy('act', Un[:], v4(pu))
                        if j < 5:
                            pt = PF()
                            for h in range(4):
                                S.mm(pt[:, h * 128:(h + 1) * 128], Uc[:, h, :], Tc[:, h, :])
                            S.copy('act', Tn[:], v4(pt))
                        pp = PF()
                        for h in range(4):
                            S.mm(pp[:, h * 128:(h + 1) * 128], Un[:, h, :], Pc[:, h, :])
                        S.tt('dve', Pn[:], v4(pp), Pc[:], ALU.add)
                        Uc, Tc, Pc = Un, Tn, Pn
                    if cph <= 3:
                        continue
                    AT = Pc
                    cur['stage'] = 'prep'
                    S.tt('pool', KBG[b4][:], Ktok[:], bc_last(BEGCt[:, hs], 128), ALU.mult)
                    S.tt('dve' if OPT['rebal'] == '2' else 'pool', BV[b4][:], Vtok[:], bc_last(BETAt[:, hs], 128), ALU.mult)
                    S.tt('pool', KDEC[b4][:], Ktok[:], bc_last(EGLGt[:, hs], 128), ALU.mult)
                    pk = PF()
                    for h in range(4):
                        S.mm(pk[:, h * 128:(h + 1) * 128], KBG[b4][:, h, :], AT[:, h, :])
                    S.ts('dve', NKC[b4][:], v4(pk), -1.0, None, ALU.mult)
                    if cph <= 4:
                        continue
                    o_sc = ob[b4]
                    cur['stage'] = 'scan'
                    for blk in ((0, 1) if dr == 0 else (1, 0)):
                        r = slice(blk * 64, blk * 64 + 64)
                        DL = DLAt if blk == 0 else DLBt
                        Scur = Sb[sbi % 2]
                        pqs = PF()
                        for h in range(4):
                            S.mm(pqs[:, h * 128:(h + 1) * 128], Qt[:, h, :], Scur[:, h, :])
                        S.tt('dve', o_sc[r], pqs[r].rearrange("p (h j) -> p h j", j=128),
                             bc_last(EGCt[r, hs], 128), ALU.mult)
                        pv = PF()
                        for h in range(4):
                            S.mm(pv[:, h * 128:(h + 1) * 128], AT[:, h, :], BV[b4][:, h, :], start=True, stop=False)
                            S.mm(pv[:, h * 128:(h + 1) * 128], NKC[b4][:, h, :], Scur[:, h, :], start=False, stop=True)
                        vn = vnew[sbi % 2]
                        S.copy('act', vn[r], pv[r].rearrange("p (h j) -> p h j", j=128))
                        pav = PF()
                        for h in range(4):
                            S.mm(pav[:, h * 128:(h + 1) * 128], attnT[b4][r, h, :], vn[r, h, :])
                        S.tt('dve', o_sc[r], o_sc[r], pav[r].rearrange("p (h j) -> p h j", j=128), ALU.add)
                        pds = PF()
                        for h in range(4):
                            S.mm(pds[:, h * 128:(h + 1) * 128], KDEC[b4][r, h, :], vn[r, h, :])
                        S.tt('pool', Stmp[:], Sf[:], bc_last(DL[:, hs], 128), ALU.mult)
                        S.tt('dve', Sf[:], Stmp[:], v4(pds), ALU.add)
                        sbi += 1
                        S.copy('act', Sb[sbi % 2][:], Sf[:])
                    sbiL[dr] = sbi
                    if cph <= 5:
                        continue
                    S.dma((oS if dr == 0 else oSb)[tsl, :].rearrange("t (h e) -> t h e", e=128), o_sc[:], q='act')
                grp.append(S.release())
                if OPT.get('zip', '1') == '1':
                    S.merge(*grp)
                else:
                    for g_ in grp:
                        S.ops.extend(g_)
                cur['c'] = None
        if nph <= 4:
            break
        S.fence()
        tb.close()
        esW = ExitStack()
        wo = sbt(esW, "wo", [128, 8, 1024], BF16)
        wdn = sbt(esW, "wdn", [128, 4, 1024], BF16)
        wcf = sbt(esW, "wcf", [128, 4, 1024], BF16)

        with ExitStack() as es:
            ucv = sbt(es, "ucv", [128, 4, L], BF16)
            def t3(name, shape, dt=BF16, nb=2):
                return [sbt(es, "%s%d" % (name, i), shape, dt) for i in range(nb)]
            oF = t3("coF", [128, 4, 128], F32)
            oB = t3("coB", [128, 4, 128], F32)
            ssq = t3("cssq", [128, 8], F32)
            onb = t3("conb", [128, 4, 128])
            szat = t3("cszat", [128, 4, 128])
            pat = t3("cpat", [128, 4, 128])
            junkC = t3("cjunk", [128, 128], F32)
            for sc in range(NT if cph > 5 else 0):
                b = sc % 2
                tsl = slice(sc * 128, (sc + 1) * 128)
                S.dma(oF[b][:], oS[tsl, :].rearrange("t (h e) -> t h e", e=128))
                S.dma(oB[b][:], oSb[tsl, :].rearrange("t (h e) -> t h e", e=128))
                S.dma(szat[b][:], szaS[:, tsl].rearrange("(h e) t -> e h t", e=128))
                S.tt('pool', oF[b][:], oF[b][:], oB[b][:], ALU.add)
                for h in range(4):
                    S.act(junkC[b][:], oF[b][:, h, :], AF.Square, accum_out=ssq[b][:, h:h + 1])
                rsqrt_to(es, ssq[b][:, 4:8], ssq[b][:, 0:4], EPS, 1.0 / 128.0)
                S.tt('dve', onb[b][:], oF[b][:], bc_last(ssq[b][:, 4:8], 128), ALU.mult)
                po = PH()
                for h in range(4):
                    S.tr(po[:, h * 128:(h + 1) * 128], onb[b][:, h, :], ident_b[:])
                S.stt('dve', pat[b][:], po.rearrange("p (h j) -> p h j", j=128), dnw, szat[b][:], ALU.mult, ALU.mult)
                S.dma(paS[:, tsl].rearrange("(h e) t -> e h t", e=128), pat[b][:], q='act')
            with ExitStack() as es1:
                wst = None
                wbV = sbt(es1, "wbV", [128, 8, 512], BF16)
                wbG = sbt(es1, "wbG", [128, 8, 512], BF16)
                rbu = [sbt(es1, "rbu%d" % i, [128, L + 2 * HAL], BF16) for i in range(2)]
                dg31 = [sbt(es1, "dg31_%d" % i, [128, 23, 128], BF16) for i in range(2)]
                sg = [sbt(es1, "sgD%d" % i, [128, TW]) for i in range(2)]
                accD = [sbt(es1, "accD%d" % i, [128, TW]) for i in range(2)]
                for r in rbu:
                    S.memset('pool', r[:, 0:HAL], 0.0)
                    S.memset('pool', r[:, HAL + L:], 0.0)
                load_w(es1, w_in[l][:, C_CFV:C_CFV + 512], 8, 512, wst, wbV)
                load_w(es1, w_in[l][:, C_CFG:C_CFG + 512], 8, 512, wst, wbG)
                load_w(es1, w_dn_out[l], 4, 1024, None, wdn)
                load_w(es1, w_cf_out[l], 4, 1024, None, wcf)
                load_w(es1, w_out[l], 8, 1024, None, wo)
                n = 0
                for cc in range(4):
                    r = rbu[cc % 2]
                    d = dg31[cc % 2]
                    for k in range(8, 31):
                        S.act(d[:, k - 8, :], IDENT, AF.Copy, scale=cfw(k, cc))
                    for tt in range(NTT):
                        n += 1
                        pv = PF()
                        pg = PF()
                        proj(wbV, cc * 128, tt * TW, TW, pv)
                        proj(wbG, cc * 128, tt * TW, TW, pg)
                        S.act(sg[n % 2][:], pg[:, 0:TW], AF.Sigmoid)
                        S.tt('dve', r[:, HAL + tt * TW:HAL + (tt + 1) * TW], pv[:, 0:TW], sg[n % 2][:], ALU.mult)
                    NDV = 8
                    for tt in range(NTT):
                        ac = accD[(cc * NTT + tt) % 2]
                        for k in range(NDV):
                            o = HAL + tt * TW + k - 15
                            if k == 0:
                                S.ts('dve', ac[:], r[:, o:o + TW], cfw(k, cc), None, ALU.mult)
                            else:
                                S.stt('dve', ac[:], r[:, o:o + TW], cfw(k, cc), ac[:], ALU.mult, ALU.add)
                        ps = PF()
                        for k in range(NDV, 31):
                            o = HAL + tt * TW + k - 15
                            S.mm(ps[:, 0:TW], d[:, k - NDV, :], r[:, o:o + TW], start=(k == NDV), stop=(k == 30))
                        S.stt('dve', ucv[:, cc, tt * TW:(tt + 1) * TW], ps[:, 0:TW], cfb(cc), ac[:], ALU.add, ALU.add)
            S.fence()
            with ExitStack() as es2:
                wst = None
                wbZ = sbt(es2, "wbZ", [128, 8, 512], BF16)
                sqc = [sbt(es2, "sqc%d" % i, [128, 4, TW], BF16) for i in range(2)]
                mean = [sbt(es2, "mean%d" % i, [128, TW]) for i in range(2)]
                msq = [sbt(es2, "msq%d" % i, [128, TW]) for i in range(2)]
                rsd = [sbt(es2, "rsd%d" % i, [128, TW]) for i in range(2)]
                szb = [sbt(es2, "szb%d" % i, [128, TW]) for i in range(2)]
                t1 = [sbt(es2, "t1_%d" % i, [128, TW]) for i in range(2)]
                t3 = [sbt(es2, "t3_%d" % i, [128, TW]) for i in range(2)]
                pbt = [sbt(es2, "pbt%d" % i, [128, TW], BF16) for i in range(2)]
                load_w(es2, w_in[l][:, C_ZB:C_ZB + 512], 8, 512, wst, wbZ)
                n = 0
                for tt in range(NTT):
                    b = tt % 2
                    tsl = slice(tt * TW, (tt + 1) * TW)
                    for cc in range(4):
                        S.act(sqc[b][:, cc, :], ucv[:, cc, tsl], AF.Square)
                    pm = PF()
                    pq = PF()
                    for cc in range(4):
                        S.mm(pm[:, 0:TW], onesLN_b[:], ucv[:, cc, tsl], start=(cc == 0), stop=(cc == 3))
                    for cc in range(4):
                        S.mm(pq[:, 0:TW], onesLN_b[:], sqc[b][:, cc, :], start=(cc == 0), stop=(cc == 3))
                    S.copy('act', mean[b][:], pm[:, 0:TW])
                    S.tt('pool', msq[b][:], mean[b][:], mean[b][:], ALU.mult)
                    S.tt('dve', msq[b][:], pq[:, 0:TW], msq[b][:], ALU.subtract)
                    rsqrt_to(es2, rsd[b][:], msq[b][:], EPS, wide=True)
                    for cc in range(4):
                        n += 1
                        m = n % 2
                        pz = PF()
                        proj(wbZ, cc * 128, tt * TW, TW, pz)
                        S.act(szb[m][:], pz[:, 0:TW], AF.Silu)
                        S.tt('dve', t1[m][:], ucv[:, cc, tsl], mean[b][:], ALU.subtract)
                        S.tt('pool', t1[m][:], t1[m][:], rsd[b][:], ALU.mult)
                        S.act(t3[m][:], t1[m][:], AF.Silu, bias=lnb(cc), scale=lnw(cc))
                        S.tt('dve', pbt[m][:], t3[m][:], szb[m][:], ALU.mult)
                        S.dma(pbS[cc * 128:(cc + 1) * 128, tsl], pbt[m][:], q='act')
        if nph <= 5:
            break
        S.fence()

        with ExitStack() as es:
            wst = [None, None]
            wg = sbt(es, "wg", [128, 8, 2048], BF16)
            pat = [sbt(es, "pat4_%d" % i, [128, 4, TW], BF16) for i in range(2)]
            pbt = [sbt(es, "pbt4_%d" % i, [128, 4, TW], BF16) for i in range(2)]
            sg0 = [sbt(es, "sg0_%d" % i, [128, TW]) for i in range(2)]
            sg1 = [sbt(es, "sg1_%d" % i, [128, TW]) for i in range(2)]
            yT = [sbt(es, "yT%d" % i, [128, 8, TW], BF16) for i in range(2)]
            xt4 = [sbt(es, "xt4_%d" % i, [128, DM]) for i in range(4)]
            xn = xt4
            junk = sbt(es, "junk4", [128, DM], BF16)
            ss4 = sbt(es, "ss4", [128, 8])
            if last:
                fnw_bc = sbt(es, "fnw_bc", [128, DM])
                S.dma(fnw_bc[:], final_norm_w.partition_broadcast(128))
            elif fuseA:
                nwF = sbt(es, "nwF", [128, DM])
                S.dma(nwF[:], norm_w[l + 1].partition_broadcast(128))
                xqF = sbt(es, "xqF", [128, DM], BF16)
            def ldg(q, eng):
                load_w(es, w_in[l][:, C_GATE + q * 256:C_GATE + (q + 1) * 256], 8, 256, wst[q % 2], wg[:, :, q * 256:(q + 1) * 256], eng=eng)
            for q in range(4):
                ldg(q, 'pool')
                ldg(q + 4, 'act')
            n = 0
            xi = 0
            for tt in range(NTT):
                b = tt % 2
                tsl = slice(tt * TW, (tt + 1) * TW)
                S.dma(pat[b][:], paS[:, tsl].rearrange("(h e) t -> e h t", e=128))
                S.dma(pbt[b][:], pbS[:, tsl].rearrange("(h e) t -> e h t", e=128))
                for c in range(8):
                    n += 1
                    m = n % 2
                    pA = PF()
                    pB = PF()
                    p0 = PF()
                    p1 = PF()
                    cs = slice(c * 128, (c + 1) * 128)
                    for h in range(4):
                        S.mm(pA[:, 0:TW], wdn[:, h, cs], pat[b][:, h, :], start=(h == 0), stop=(h == 3))
                    for h in range(4):
                        S.mm(pB[:, 0:TW], wcf[:, h, cs], pbt[b][:, h, :], start=(h == 0), stop=(h == 3))
                    proj(wg, c * 128, tt * TW, TW, p0)
                    proj(wg, 1024 + c * 128, tt * TW, TW, p1)
                    S.act(sg0[m][:], p0[:, 0:TW], AF.Sigmoid, bias=gateb(c))
                    S.act(sg1[m][:], p1[:, 0:TW], AF.Sigmoid, bias=gateb(8 + c))
                    S.tt('dve', sg0[m][:], pA[:, 0:TW], sg0[m][:], ALU.mult)
                    S.tt('dve', sg1[m][:], pB[:, 0:TW], sg1[m][:], ALU.mult)
                    S.tt('pool', yT[b][:, c, :], sg0[m][:], sg1[m][:], ALU.add)
                for j in range(JT):
                    xi += 1
                    m = xi % 4
                    rows = slice(tt * TW + j * 128, tt * TW + (j + 1) * 128)
                    S.dma(xt4[m][:], x_cur[rows, :])
                    for half in range(2):
                        po = PF()
                        for c in range(8):
                            S.mm(po[:, 0:512], yT[b][:, c, j * 128:(j + 1) * 128], wo[:, c, half * 512:(half + 1) * 512],
                                 start=(c == 0), stop=(c == 7))
                        S.tt('dve', xn[m][:, half * 512:(half + 1) * 512], po[:, 0:512],
                             xt4[m][:, half * 512:(half + 1) * 512], ALU.add)
                    if not last:
                        S.dma(xs[rows, :], xn[m][:])
                        if fuseA:
                            sq = ss4[:, m * 2:m * 2 + 1]
                            rs = ss4[:, m * 2 + 1:m * 2 + 2]
                            S.act(junk[:], xn[m][:], AF.Square, accum_out=sq)
                            rsqrt_to(es, rs, sq, EPS, 1.0 / DM)
                            S.stt('dve', xqF[:], xn[m][:], rs, nwF[:], ALU.mult, ALU.mult)
                            for half in range(2):
                                ph = PH()
                                for k in range(4):
                                    c = half * 4 + k
                                    S.tr(ph[:, k * 128:(k + 1) * 128], xqF[:, c * 128:(c + 1) * 128], ident_b[:])
                                S.copy('act' if half == 0 else 'dve', hT[:, half * 4:half * 4 + 4, rows],
                                       ph[:, 0:512].rearrange("p (k t) -> p k t", t=128))
                    else:
                        sq = ss4[:, m * 2:m * 2 + 1]
                        rs = ss4[:, m * 2 + 1:m * 2 + 2]
                        S.act(junk[:], xn[m][:], AF.Square, accum_out=sq)
                        rsqrt_to(es, rs, sq, EPS, 1.0 / DM)
                        S.stt('dve', xt4[m][:], xn[m][:], rs, fnw_bc[:], ALU.mult, ALU.mult)
                        S.dma(out[rows, :], xt4[m][:])
        S.fence()
        esW.close()
        lay.close()

    S.final()
    semstack = ExitStack()
    stats = S.emit(lambda name: semstack.enter_context(nc.semaphore(name)), reorder=reorder)
    return nc, stats, tapo


_CACHE = {}


def kernel(**inputs):
    x = np.ascontiguousarray(np.asarray(inputs['x'], dtype=np.float32))
    B, L, _ = x.shape
    key = (L,)
    if key not in _CACHE:
        _CACHE[key] = build(L)
    nc, stats, _ = _CACHE[key]
    shared = {}
    for k in ('norm_w', 'w_in', 'qkv_conv_w', 'dn_norm_w', 'w_dn_out', 'cf_conv_w', 'cf_conv_b',
              'cf_ln_w', 'cf_ln_b', 'w_cf_out', 'gate_b', 'w_out', 'final_norm_w'):
        shared[k] = np.ascontiguousarray(np.asarray(inputs[k], dtype=np.float32))
    shared['a_log'] = np.ascontiguousarray(np.asarray(inputs['a_log'], dtype=np.float32).reshape(DEPTH, 8))
    shared['dt_bias'] = np.ascontiguousarray(np.asarray(inputs['dt_bias'], dtype=np.float32).reshape(DEPTH, 8))
    in_maps = [dict(shared, x=x[b]) for b in range(B)]
    res = run_bass_kernel_spmd(nc, in_maps, core_ids=list(range(B)))
    return np.stack([np.asarray(r['out'], dtype=np.float32) for r in res.results], axis=0)
```

```python
import numpy as np
import concourse.bass as bass
import concourse.mybir as mybir

F32 = mybir.dt.float32
BF16 = mybir.dt.bfloat16
I32 = mybir.dt.int32
AF = mybir.ActivationFunctionType
ALU = mybir.AluOpType
AX = mybir.AxisListType


def _prod(xs):
    r = 1
    for v in xs:
        r *= int(v)
    return r


def box(a):
    t = a.tensor
    name = t.name
    apl = a.ap
    off = int(a.offset)
    space = str(a.space)
    if 'DRAM' in space.upper():
        hi = off + sum((c - 1) * abs(s) for s, c in apl) + 1
        return (name, 0, 1, off, hi)
    if 'PSUM' in space.upper():
        return (name, 0, 128, 0, 1 << 30)
    pstride = _prod(t.shape[1:])
    p0 = off // pstride
    f0 = off % pstride
    ps, pc = apl[0]
    f1 = f0 + sum((c - 1) * abs(s) for s, c in apl[1:]) + 1
    return (name, p0, p0 + pc, f0, f1)


def boxes(a):
    b = box(a)
    space = str(a.space).upper()
    if 'DRAM' in space or 'PSUM' in space:
        return [b]
    apl = a.ap
    if len(apl) < 3:
        return [b]
    s1, c1 = apl[1]
    inner = sum((c - 1) * abs(s) for s, c in apl[2:]) + 1
    if c1 <= 1 or c1 > 16 or s1 <= 0 or inner > s1:
        return [b]
    name, p0, p1, f0, _ = b
    return [(name, p0, p1, f0 + k * s1, f0 + k * s1 + inner) for k in range(c1)]


def bc_last(ap, n):
    return bass.AP(ap.tensor, ap.offset, [list(e) for e in ap.ap] + [[0, n]])


def bc_mid(ap, n):
    l = [list(e) for e in ap.ap]
    return bass.AP(ap.tensor, ap.offset, [l[0], [0, n]] + l[1:])


CP_DEFAULT = '1'


class Sched:
    ROT = 8000
    NDMA = 64

    def __init__(self, nc):
        self.nc = nc
        self.engs = {'pe': nc.tensor, 'act': nc.scalar, 'dve': nc.vector,
                     'pool': nc.gpsimd, 'sp': nc.sync}
        self.ops = []

    def add(self, eng, fn, reads, writes, dma=False):
        reads = [a for a in reads if a is not None]
        writes = [a for a in writes if a is not None]
        w0 = writes[0]
        n = 1
        for s_, c_ in list(w0.ap)[1:]:
            n *= int(c_)
        psrc = any('PSUM' in str(a.space).upper() for a in reads)
        f32 = bool(reads) and all(a.dtype == F32 for a in reads[:2])
        nbytes = n * int(list(w0.ap)[0][1]) * (4 if w0.dtype == F32 else 2)
        self.ops.append(dict(eng=eng, fn=fn, r=[bb for a in reads for bb in boxes(a)],
                             w=[bb for a in writes for bb in boxes(a)], dma=dma, fence=False,
                             n=n, psrc=psrc, f32=f32, nbytes=nbytes, nrd=len(reads)))

    def fence(self):
        self.ops.append(dict(fence=True))

    def capture(self):
        self._saved = self.ops
        self.ops = []

    def release(self):
        lst = self.ops
        self.ops = self._saved
        return lst

    def merge(self, *lists):
        n = max(len(x) for x in lists)
        for i in range(n):
            for x in lists:
                if i < len(x):
                    self.ops.append(x[i])

    def final(self):
        self.ops.append(dict(fence=False, final=True, eng='sp', fn=None, dma=False, r=[], w=[]))

    def mm(self, out, lhsT, rhs, start=True, stop=True):
        self.add('pe', lambda: self.nc.tensor.matmul(out, lhsT, rhs, start=start, stop=stop),
                 [lhsT, rhs], [out])

    def tr(self, out, in_, ident):
        self.add('pe', lambda: self.nc.tensor.transpose(out, in_, ident), [in_, ident], [out])

    def act(self, out, in_, func, bias=None, scale=None, accum_out=None):
        kw = {}
        rd = [in_]
        if bias is not None:
            kw['bias'] = bias
            if not isinstance(bias, (int, float)):
                rd.append(bias)
        if scale is not None:
            kw['scale'] = scale
            if not isinstance(scale, (int, float)):
                rd.append(scale)
        wr = [out]
        if accum_out is not None:
            kw['accum_out'] = accum_out
            wr.append(accum_out)
        self.add('act', lambda: self.nc.scalar.activation(out, in_, func, **kw), rd, wr)
        self.ops[-1]['actg'] = {AF.Silu: 'silu', AF.Sigmoid: 'sig', AF.Sqrt: 'sqrt', AF.Exp: 'exp', AF.Ln: 'exp'}.get(func)

    def tt(self, eng, out, in0, in1, op):
        e = self.engs[eng]
        self.add(eng, lambda: e.tensor_tensor(out, in0, in1, op), [in0, in1], [out])

    def ts(self, eng, out, in0, s1, s2, op0, op1=None):
        e = self.engs[eng]
        rd = [in0] + [s for s in (s1, s2) if s is not None and not isinstance(s, (int, float))]
        if op1 is None:
            self.add(eng, lambda: e.tensor_single_scalar(out, in0, s1, op0), rd, [out])
        else:
            self.add(eng, lambda: e.tensor_scalar(out, in0, s1, s2, op0, op1), rd, [out])

    def stt(self, eng, out, in0, scalar, in1, op0, op1):
        e = self.engs[eng]
        rd = [in0, in1] + ([scalar] if not isinstance(scalar, (int, float)) else [])
        self.add(eng, lambda: e.scalar_tensor_tensor(out, in0, scalar, in1, op0, op1), rd, [out])

    def copy(self, eng, out, in_):
        if eng == 'act':
            self.add('act', lambda: self.nc.scalar.copy(out, in_), [in_], [out])
        else:
            e = self.engs[eng]
            self.add(eng, lambda: e.tensor_copy(out, in_), [in_], [out])

    def memset(self, eng, ap, val):
        e = self.engs[eng]
        self.add(eng, lambda: e.memset(ap, val), [], [ap])

    def dma(self, out, in_, q='sp', **kw):
        e = self.engs[q]
        self.add(q, lambda: e.dma_start(out=out, in_=in_, **kw), [in_], [out], dma=True)

    def dma_cast(self, out, in_):
        self.add('pool', lambda: self.nc.gpsimd.dma_start(out=out, in_=in_), [in_], [out], dma=True)

    @staticmethod
    def _ovl(a, b):
        return a[1] < b[2] and b[1] < a[2] and a[3] < b[4] and b[3] < a[4]

    def _dur(self, op):
        e = op['eng']
        n = op['n']
        if op['dma']:
            return 1200.0 if e == 'pool' else 120.0
        if e == 'pe':
            return max(100.0, (4.0 if op['f32'] else 1.0) * n / 2.35 + 8.0)
        if e == 'act':
            return (200.0 + n) / 1.2
        if e == 'dve':
            if op['psrc']:
                return (n + 200.0) / 0.96
            if op['nrd'] >= 2:
                return (n + 151.0) / 0.96
            return (n / 2.0 + 151.0) / 0.96
        if e == 'pool':
            return 60.0 + 1.95 * n
        return 100.0

    def emit(self, sem_ctx, reorder=True):
        import heapq
        ops_all = self.ops
        CE = ('pe', 'act', 'dve', 'pool')
        ALLE = list(CE) + ['sp']
        segs = [[]]
        final_op = None
        for op in ops_all:
            if op['fence']:
                segs.append([])
            elif op.get('final'):
                final_op = op
            else:
                segs[-1].append(op)
        order = []
        nid = 0
        for seg in segs:
            if not seg:
                continue
            state = {}
            for op in seg:
                op['id'] = nid
                nid += 1
                deps = set()
                me = op['id']
                key = op['eng'] if not op['dma'] else ('d', me)
                for b in op['r']:
                    covered = False
                    lst = state.setdefault(b[0], [])
                    for rec in lst:
                        if self._ovl(rec[0], b):
                            if rec[1] is not None:
                                deps.add(rec[1])
                            if b[4] == (1 << 30):
                                for id2, k2 in rec[2].items():
                                    if k2 != key:
                                        deps.add(id2)
                            rec[2][me] = key
                            rb = rec[0]
                            if rb[1] <= b[1] and b[2] <= rb[2] and rb[3] <= b[3] and b[4] <= rb[4]:
                                covered = True
                    if not covered:
                        lst.append([b, None, {me: key}])
                for b in op['w']:
                    lst = state.setdefault(b[0], [])
                    keep = []
                    for rec in lst:
                        if self._ovl(rec[0], b):
                            if rec[1] is not None:
                                deps.add(rec[1])
                            deps.update(rec[2].keys())
                            rb = rec[0]
                            if b[1] <= rb[1] and rb[2] <= b[2] and b[3] <= rb[3] and rb[4] <= b[4]:
                                continue
                        keep.append(rec)
                    keep.append([b, me, {}])
                    state[b[0]] = keep
                deps.discard(me)
                op['deps'] = deps
            base = seg[0]['id']
            if not reorder:
                order.append(list(seg))
                continue
            nseg = len(seg)
            succ = [[] for _ in range(nseg)]
            npred = [0] * nseg
            for op in seg:
                i = op['id'] - base
                npred[i] = len(op['deps'])
                for d in op['deps']:
                    succ[d - base].append(i)
            ready_t = [0.0] * nseg
            done_t = [0.0] * nseg
            use_cp = CP_DEFAULT == '1'
            blev = [0.0] * nseg
            if use_cp:
                for i in range(nseg - 1, -1, -1):
                    m_ = 0.0
                    for s in succ[i]:
                        if blev[s] > m_:
                            m_ = blev[s]
                    blev[i] = m_ + self._dur(seg[i]) + (2000.0 if seg[i]['dma'] else 0.0)
                for i in range(nseg):
                    blev[i] += seg[i].get('boost', 0.0)
            tfree = {e: 0.0 for e in ALLE}
            fut = {e: [] for e in ALLE}
            avl = {e: [] for e in ALLE}
            for i, op in enumerate(seg):
                if npred[i] == 0:
                    heapq.heappush(fut[op['eng']], (0.0, i))
            new = []
            act_grp = [None]
            LAT = 200.0
            left = nseg
            while left:
                best = None
                for e in ALLE:
                    f, a = fut[e], avl[e]
                    while f and f[0][0] <= tfree[e]:
                        k_ = heapq.heappop(f)[1]
                        heapq.heappush(a, (-blev[k_], k_))
                    if a:
                        cand = (tfree[e], a[0][1], e, True)
                    elif f:
                        cand = (f[0][0], f[0][1], e, False)
                    else:
                        continue
                    if best is None or cand[:2] < best[:2]:
                        best = cand
                st, i, e, from_a = best
                if from_a:
                    if e == 'act' and len(avl[e]) > 1:
                        g0 = seg[i].get('actg')
                        if g0 is not None and g0 != act_grp[0]:
                            cands = [kk for kk in avl[e] if seg[kk[1]].get('actg') in (None, act_grp[0])]
                            if cands:
                                i = min(cands)[1]
                        avl[e].remove((-blev[i], i))
                        heapq.heapify(avl[e])
                    else:
                        heapq.heappop(avl[e])
                else:
                    heapq.heappop(fut[e])
                op = seg[i]
                d = self._dur(op)
                if e == 'act':
                    g = op.get('actg')
                    if g is not None and g != act_grp[0]:
                        d += 1300.0
                        act_grp[0] = g
                tfree[e] = st + d
                fin = st + d
                if op['dma']:
                    fin = st + 1800.0 + op['nbytes'] / 180.0
                done_t[i] = fin
                new.append(op)
                left -= 1
                for s in succ[i]:
                    if fin + LAT > ready_t[s]:
                        ready_t[s] = fin + LAT
                    npred[s] -= 1
                    if npred[s] == 0:
                        heapq.heappush(fut[seg[s]['eng']], (ready_t[s], s))
            order.append(new)
            self.seg_ms = getattr(self, 'seg_ms', []) + [(nseg, round(max(done_t) / 1e3), {e: round(sum(self._dur(o) for o in seg if o['eng'] == e) / 1e3) for e in ALLE})]
        pos = {}
        engcnt = {e: 0 for e in CE}
        dmaidx = {}
        dmaq = {}
        NQ = {'sp': self.NDMA, 'pool': 16, 'act': 12}
        nq = {'sp': 0, 'pool': 0, 'act': 0}
        ndma = 0
        flat = []
        for si, seg in enumerate(order):
            for op in seg:
                op['seg'] = si
                if op['dma']:
                    q_ = op['eng']
                    dmaidx[op['id']] = nq[q_]
                    dmaq[op['id']] = q_
                    nq[q_] += 1
                    ndma += 1
                else:
                    pos[op['id']] = (op['eng'], engcnt[op['eng']])
                    engcnt[op['eng']] += 1
                flat.append(op)
        waited = {e: {p: -1 for p in CE} for e in ALLE}
        waited_d = {e: set() for e in ALLE}
        last_pos = {p: -1 for p in CE}
        need_e = set()
        dma_ids_in_order = {'sp': [], 'pool': [], 'act': []}
        fence_pending = {e: None for e in ALLE}
        cur_seg = 0
        for op in flat:
            if op['seg'] != cur_seg:
                cur_seg = op['seg']
                snap = (dict(last_pos), [d_ for q_ in NQ for d_ in dma_ids_in_order[q_][-NQ[q_]:]])
                for e in ALLE:
                    fence_pending[e] = snap
            eng = op['eng']
            best = {}
            dwait = []
            if fence_pending[eng] is not None:
                lp, dl = fence_pending[eng]
                fence_pending[eng] = None
                for p, s in lp.items():
                    if s >= 0 and not (eng == 'pe' and p == 'pe'):
                        best[p] = s
                for di in dl:
                    if di not in waited_d[eng]:
                        waited_d[eng].add(di)
                        dwait.append(di)
            if op['dma']:
                k = dmaidx[op['id']]
                q_ = op['eng']
                if k >= NQ[q_]:
                    di = dma_ids_in_order[q_][k - NQ[q_]]
                    if di not in waited_d[eng]:
                        waited_d[eng].add(di)
                        dwait.append(di)
            for d in op['deps']:
                if d in dmaidx:
                    if d not in waited_d[eng]:
                        waited_d[eng].add(d)
                        dwait.append(d)
                else:
                    p, s = pos[d]
                    if eng == 'pe' and p == 'pe':
                        continue
                    if s > best.get(p, -1):
                        best[p] = s
            ew = []
            for p, s in best.items():
                if s > waited[eng][p]:
                    waited[eng][p] = s
                    ew.append((p, s))
                    need_e.add((p, s))
            op['ew'] = ew
            op['dw'] = dwait
            if op['dma']:
                dma_ids_in_order[op['eng']].append(op['id'])
            else:
                last_pos[eng] = pos[op['id']][1]
        fin_e = [(p, s) for p, s in last_pos.items() if s > waited['sp'][p]]
        for ps_ in fin_e:
            need_e.add(ps_)
        fin_d = [d for q_ in NQ for d in dma_ids_in_order[q_][-NQ[q_]:] if d not in waited_d['sp']]
        cnt = {e: 0 for e in CE}
        evval = {}
        for op in flat:
            if not op['dma']:
                ps_ = pos[op['id']]
                if ps_ in need_e:
                    c = cnt[ps_[0]]
                    cnt[ps_[0]] += 1
                    evval[ps_] = ('%s_%d' % (ps_[0], c // self.ROT), c % self.ROT + 1)
        sems = {}

        def get_sem(name):
            if name not in sems:
                sems[name] = sem_ctx(name)
            return sems[name]

        def dma_sem(d):
            k = dmaidx[d]
            q_ = dmaq[d]
            return get_sem({'sp': 'dma_%d', 'pool': 'sdma_%d', 'act': 'adma_%d'}[q_] % (k % NQ[q_])), 16 * (k // NQ[q_] + 1)
        nwait = 0
        for op in flat:
            e = self.engs[op['eng']]
            for ps_ in op['ew']:
                sname, val = evval[ps_]
                e.wait_ge(get_sem(sname), val)
                nwait += 1
            for d in op['dw']:
                sm, val = dma_sem(d)
                e.wait_ge(sm, val)
                nwait += 1
            ins = op['fn']()
            if op['dma']:
                sm, _ = dma_sem(op['id'])
                ins.then_inc(sm, 16)
            else:
                ps_ = pos[op['id']]
                if ps_ in evval:
                    ins.then_inc(get_sem(evval[ps_][0]), 1)
        sp = self.engs['sp']
        for ps_ in fin_e:
            sname, val = evval[ps_]
            sp.wait_ge(get_sem(sname), val)
        for d in fin_d:
            sm, val = dma_sem(d)
            sp.wait_ge(sm, val)
        self.stats = dict(nops=len(flat), nwait=nwait, nsem=len(sems), cnt=dict(cnt), ndma=ndma)
        return self.stats


from contextlib import ExitStack
from concourse.bass_utils import run_bass_kernel_spmd

DM = 1024
DIN = 5648
C_Q, C_K, C_V, C_ZA, C_AB, C_CFV, C_CFG, C_ZB, C_GATE = 0, 512, 1024, 1536, 2048, 2064, 2576, 3088, 3600
EPS = 1e-6
NEG = -30000.0
DEPTH = 2
OPT = dict(cb='a', rebal='0', fuseA='1', zip='1')


def build(L, depth=DEPTH, taps=(), nph=99, cph=99, reorder=True):
    nc = bass.Bass("TRN2", target_bir_lowering=False)
    NT = L // 128
    TW = min(512, L)
    NTT = L // TW
    JT = TW // 128
    HAL = 16

    def din(name, shape):
        return nc.dram_tensor(name, shape, F32, kind="ExternalInput").ap()
    x_in = din("x", [L, DM])
    norm_w = din("norm_w", [depth, DM])
    w_in = din("w_in", [depth, DM, DIN])
    qkv_conv_w = din("qkv_conv_w", [depth, 5, 1536])
    a_log = din("a_log", [depth, 8])
    dt_bias = din("dt_bias", [depth, 8])
    dn_norm_w = din("dn_norm_w", [depth, 128])
    w_dn_out = din("w_dn_out", [depth, 512, DM])
    cf_conv_w = din("cf_conv_w", [depth, 31, 512])
    cf_conv_b = din("cf_conv_b", [depth, 512])
    cf_ln_w = din("cf_ln_w", [depth, 512])
    cf_ln_b = din("cf_ln_b", [depth, 512])
    w_cf_out = din("w_cf_out", [depth, 512, DM])
    gate_b = din("gate_b", [depth, 2048])
    w_out = din("w_out", [depth, DM, DM])
    final_norm_w = din("final_norm_w", [DM])
    out = nc.dram_tensor("out", [L, DM], F32, kind="ExternalOutput").ap()
    tapo = {}

    def scr(name, shape, dt):
        return nc.dram_tensor(name, shape, dt).ap()
    xs = scr("xs", [L, DM], F32)
    QtS = scr("QtS", [512, L], BF16)
    KtS = scr("KtS", [512, L], BF16)
    KS = scr("KS", [L, 512], BF16)
    VS = scr("VS", [L, 512], BF16)
    szaS = scr("szaS", [512, L], BF16)
    oS = scr("oS", [L, 512], F32)
    oSb = scr("oSb", [L, 512], F32)
    paS = scr("paS", [512, L], BF16)
    pbS = scr("pbS", [512, L], BF16)

    S = Sched(nc)
    top = ExitStack()

    uniq = {'n': 0}

    def sbt(es, name, shape, dt=F32):
        uniq['n'] += 1
        return es.enter_context(nc.sbuf_tensor("%s_u%d" % (name, uniq['n']), shape, dt))

    psF = [top.enter_context(nc.psum_tensor("psF%d" % i, [128, 512], F32)) for i in range(6)]
    psH = [top.enter_context(nc.psum_tensor("psH%d" % i, [128, 1024], BF16)) for i in range(2)]
    rot = {'f': 0, 'h': 0}

    cur = {'c': None}
    rotc = [0, 0]
    roth = [0, 0]

    def PF():
        c = cur['c']
        if c is not None:
            st_ = cur.get('stage')
            if OPT['cb'] == 'a':
                if st_ == 'scan':
                    return psF[3 * c + 2]
                rotc[c] = (rotc[c] + 1) % 2
                return psF[3 * c + rotc[c]]
            if OPT['cb'] == 'e':
                if st_ == 'scan':
                    return psF[4 + c]
                rotc[0] = (rotc[0] + 1) % 4
                return psF[rotc[0]]
            if OPT['cb'] == 'b':
                return psF[3 * c + {'prep': 0, 'neu': 1, 'scan': 2}[st_]]
            if OPT['cb'] == 'c':
                if st_ != 'scan':
                    return psF[3 * c]
                rotc[c] = (rotc[c] + 1) % 2
                return psF[3 * c + 1 + rotc[c]]
            rotc[c] = (rotc[c] + 1) % 3
            return psF[3 * c + rotc[c]]
        rot['f'] = (rot['f'] + 1) % 6
        return psF[rot['f']]

    def PH():
        c = cur['c']
        if c is not None:
            roth[c] ^= 1
            return psH[c][:, roth[c] * 512:roth[c] * 512 + 512]
        rot['h'] = (rot['h'] + 1) % 2
        return psH[rot['h']][:, 0:512]

    alt = {'i': 0}

    def AD():
        alt['i'] ^= 1
        return 'act' if alt['i'] else 'dve'

    ip_i = sbt(top, "ip_i", [128, 128], I32)
    ij_i = sbt(top, "ij_i", [128, 128], I32)
    ip = sbt(top, "ip", [128, 128])
    ij = sbt(top, "ij", [128, 128])
    cst = sbt(top, "cst", [128, 12, 128])
    ident_b = sbt(top, "ident_b", [128, 128], BF16)
    cstb = sbt(top, "cstb", [128, 4, 128], BF16)
    ones128_b = sbt(top, "ones128_b", [128, 128], BF16)
    ones1_b = sbt(top, "ones1_b", [128, 128], BF16)
    onesLN_b = sbt(top, "onesLN_b", [128, 128], BF16)
    hT = sbt(top, "hT", [128, 8, L], BF16)
    S.add('pool', lambda: nc.gpsimd.iota(ip_i[:], [[0, 128]], base=0, channel_multiplier=1), [], [ip_i[:]])
    S.add('pool', lambda: nc.gpsimd.iota(ij_i[:], [[1, 128]], base=0, channel_multiplier=0), [], [ij_i[:]])
    S.copy('dve', ip[:], ip_i[:])
    S.copy('dve', ij[:], ij_i[:])
    IDENT, LE, LT, GE, GT, NEGF, NEGB, BLKA, BLKB, SAME, T0, T1 = [cst[:, i, :] for i in range(12)]
    S.tt('dve', IDENT, ip[:], ij[:], ALU.is_equal)
    S.ts('dve', T0, ip[:], 64.0, None, ALU.is_ge)
    S.ts('dve', T1, ij[:], 64.0, None, ALU.is_ge)
    S.tt('dve', SAME, T0, T1, ALU.is_equal)
    for dst, op in ((LE, ALU.is_le), (LT, ALU.is_lt), (GE, ALU.is_ge), (GT, ALU.is_gt)):
        S.tt('dve', dst, ip[:], ij[:], op)
        S.tt('dve', dst, dst, SAME, ALU.mult)
    S.ts('dve', NEGF, GE, -1.0, -NEG, ALU.add, ALU.mult)
    S.ts('dve', NEGB, LE, -1.0, -NEG, ALU.add, ALU.mult)
    S.ts('dve', BLKA, ip[:], 64.0, None, ALU.is_lt)
    S.ts('dve', BLKB, ip[:], 64.0, None, ALU.is_ge)
    S.copy('dve', ident_b[:], IDENT)
    GTb, LTb, NEGFb, NEGBb = [cstb[:, i, :] for i in range(4)]
    for dst_, src_ in ((GTb, GT), (LTb, LT), (NEGFb, NEGF), (NEGBb, NEGB)):
        S.copy('dve', dst_, src_)
    S.memset('dve', ones128_b[:], 128.0)
    S.memset('dve', ones1_b[:], 1.0)
    S.memset('dve', onesLN_b[:], 1.0 / 512.0)

    def load_w(es_buf, src, nk, ncols, wst, wdst, eng='pool'):
        srcv = src.rearrange("(c p) n -> p c n", p=128)
        step = 256 if ncols > 256 else ncols
        for c0 in range(0, ncols, step):
            S.dma_cast(wdst[:, 0:nk, c0:c0 + step], srcv[:, :, c0:c0 + step])

    def proj(wb, col0, t0, tw, ps):
        for c in range(8):
            S.mm(ps[:, 0:tw], wb[:, c, col0:col0 + 128], hT[:, c, t0:t0 + tw], start=(c == 0), stop=(c == 7))

    def rsqrt_to(es_tmp, dst, src, eps, scale=1.0, wide=False):
        if wide:
            S.act(dst, src, AF.Ln, bias=eps, scale=scale)
            S.act(dst, dst, AF.Exp, scale=-0.5)
        else:
            S.act(dst, src, AF.Sqrt, bias=eps, scale=scale)
            S.add('dve', lambda: nc.vector.reciprocal(dst, dst), [dst], [dst])

    for l in range(depth):
        x_cur = x_in if l == 0 else xs
        last = (l == depth - 1)
        lay = ExitStack()
        stA = sbt(lay, "stA%d" % l, [128, 128])
        stB = sbt(lay, "stB%d" % l, [128, 128])
        prmA = sbt(lay, "prmA%d" % l, [128, 128])
        prmB = sbt(lay, "prmB%d" % l, [128, 128])
        alog_bc = sbt(lay, "alog%d" % l, [128, 8])
        dtb_bc = sbt(lay, "dtb%d" % l, [128, 8])
        negA = sbt(lay, "negA%d" % l, [128, 8])
        S.memset('dve', stA[:], 0.0)
        S.memset('dve', stB[:], 0.0)
        S.dma(stA[0:8, :], norm_w[l].rearrange("(c p) -> c p", p=128))
        S.dma(stA[8:68, :], qkv_conv_w[l].rearrange("k (c p) -> (k c) p", p=128))
        S.dma(stA[68:69, :], dn_norm_w[l].rearrange("(c p) -> c p", p=128))
        S.dma(stA[69:73, :], cf_conv_b[l].rearrange("(c p) -> c p", p=128))
        S.dma(stA[73:77, :], cf_ln_w[l].rearrange("(c p) -> c p", p=128))
        S.dma(stA[77:81, :], cf_ln_b[l].rearrange("(c p) -> c p", p=128))
        S.dma(stA[81:97, :], gate_b[l].rearrange("(c p) -> c p", p=128))
        S.dma(stB[0:124, :], cf_conv_w[l].rearrange("k (c p) -> (k c) p", p=128))
        S.dma(alog_bc[:], a_log[l].partition_broadcast(128))
        S.dma(dtb_bc[:], dt_bias[l].partition_broadcast(128))
        ps = PF()
        S.tr(ps[:, 0:128], stA[:], IDENT)
        S.copy('dve', prmA[:], ps[:, 0:128])
        ps = PF()
        S.tr(ps[:, 0:128], stB[:], IDENT)
        S.copy('dve', prmB[:], ps[:, 0:128])
        S.act(negA[:], alog_bc[:], AF.Exp)
        S.ts('dve', negA[:], negA[:], -1.0, None, ALU.mult)

        def normw(c): return prmA[:, c:c + 1]
        def qkvw(k, cc): return prmA[:, 8 + k * 12 + cc: 9 + k * 12 + cc]
        dnw = prmA[:, 68:69]
        def cfb(cc): return prmA[:, 69 + cc:70 + cc]
        def lnw(cc): return prmA[:, 73 + cc:74 + cc]
        def lnb(cc): return prmA[:, 77 + cc:78 + cc]
        def gateb(c): return prmA[:, 81 + c:82 + c]
        def cfw(k, cc): return prmB[:, k * 4 + cc:k * 4 + cc + 1]

        tb = ExitStack()
        TAB = sbt(tb, "TAB%d" % l, [128, 8, NT * 8])
        GHL = sbt(tb, "GHL%d" % l, [128, 2, NT * 8], BF16)
        Gt, BETAt, EGCt, BEGCt, EGLGt, DLAt, DLBt = [TAB[:, i, :] for i in range(7)]
        esAB = ExitStack()
        fuseA = OPT['fuseA'] != '0'
        for es in ((esAB,) if not (fuseA and l > 0) else ()):
            nw_bc = sbt(esAB, "nwbc%d" % l, [128, DM])
            S.dma(nw_bc[:], norm_w[l].partition_broadcast(128))
            xb = [sbt(es, "xb%d" % i, [128, DM]) for i in range(4)]
            junk = sbt(es, "junkA", [128, DM])
            xsb = [sbt(es, "xsb%d" % i, [128, DM], BF16) for i in range(4)]
            ssA = sbt(es, "ssA", [128, 8])
            for i in range(NT):
                xt = xb[i % 4]
                xq = xsb[i % 4]
                sq = ssA[:, (i % 4) * 2:(i % 4) * 2 + 1]
                rs = ssA[:, (i % 4) * 2 + 1:(i % 4) * 2 + 2]
                S.dma(xt[:], x_cur[i * 128:(i + 1) * 128, :])
                S.act(junk[:], xt[:], AF.Square, accum_out=sq)
                rsqrt_to(es, rs, sq, EPS, 1.0 / DM)
                S.stt('dve', xq[:], xt[:], rs, nw_bc[:], ALU.mult, ALU.mult)
                for half in range(2):
                    ph = PH()
                    for k in range(4):
                        c = half * 4 + k
                        S.tr(ph[:, k * 128:(k + 1) * 128], xq[:, c * 128:(c + 1) * 128], ident_b[:])
                    S.copy('act' if half == 0 else 'dve', hT[:, half * 4:half * 4 + 4, i * 128:(i + 1) * 128],
                           ph[:, 0:512].rearrange("p (k t) -> p k t", t=128))
        if nph <= 1:
            break
        if 'hT' in taps and l == 0:
            tapo['hT'] = nc.dram_tensor("tap_hT", [128, 8, L], BF16, kind="ExternalOutput").ap()
            S.dma(tapo['hT'], hT[:])

        for es in (esAB,):
            wst = None
            wbf = [sbt(es, "wbfB%d" % i, [128, 8, 512], BF16) for i in range(2)]
            rb = [sbt(es, "rbB%d" % i, [128, L + 2 * HAL], BF16) for i in range(2)]
            dg = [sbt(es, "dgB%d" % i, [128, 5, 128], BF16) for i in range(2)]
            sil = [sbt(es, "silB%d" % i, [128, TW]) for i in range(4)]
            sqb = [sbt(es, "sqB%d" % i, [128, TW], BF16) for i in range(4)]
            sd = [sbt(es, "sdB%d" % i, [128, TW]) for i in range(4)]
            qn = [sbt(es, "qnB%d" % i, [128, TW], BF16) for i in range(4)]
            tok = [sbt(es, "tokB%d" % i, [128, JT, 128], BF16) for i in range(4)]
            for r in rb:
                S.memset('pool', r[:, 0:HAL], 0.0)
                S.memset('pool', r[:, HAL + L:], 0.0)
            n = 0
            for grp, (col0, kind) in enumerate(((C_Q, 'q'), (C_K, 'k'), (C_V, 'v'))):
                wb = wbf[grp % 2]
                load_w(es, w_in[l][:, col0:col0 + 512], 8, 512, wst, wb)
                for h in range(4):
                    cc = grp * 4 + h
                    r = rb[cc % 2]
                    d = dg[cc % 2]
                    for k in range(5):
                        S.act(d[:, k, :], IDENT, AF.Copy, scale=qkvw(k, cc))
                    for tt in range(NTT):
                        ps = PF()
                        proj(wb, h * 128, tt * TW, TW, ps)
                        S.copy('dve', r[:, HAL + tt * TW:HAL + (tt + 1) * TW], ps[:, 0:TW])
                    for tt in range(NTT):
                        n += 1
                        ps = PF()
                        for k in range(5):
                            o = HAL + tt * TW + k - 2
                            S.mm(ps[:, 0:TW], d[:, k, :], r[:, o:o + TW], start=(k == 0), stop=(k == 4))
                        tsl = slice(tt * TW, (tt + 1) * TW)
                        if kind == 'v':
                            vq = qn[n % 4]
                            S.act(vq[:], ps[:, 0:TW], AF.Silu)
                            src_bf = vq
                        else:
                            sl = sil[n % 4]
                            S.act(sl[:], ps[:, 0:TW], AF.Silu)
                            S.tt('pool', sqb[n % 4][:], sl[:], sl[:], ALU.mult)
                            ps3 = PF()
                            S.mm(ps3[:, 0:TW], (ones128_b if kind == 'q' else ones1_b)[:], sqb[n % 4][:])
                            sdd = sd[n % 4]
                            rsqrt_to(es, sdd[:], ps3[:, 0:TW], EPS * (128.0 if kind == 'q' else 1.0), wide=True)
                            src_bf = qn[n % 4]
                            S.tt('dve', src_bf[:], sl[:], sdd[:], ALU.mult)
                            S.dma((QtS if kind == 'q' else KtS)[h * 128:(h + 1) * 128, tsl], src_bf[:], q='act')
                        if kind != 'q':
                            ph = PH()
                            for j in range(JT):
                                S.tr(ph[:, j * 128:(j + 1) * 128], src_bf[:, j * 128:(j + 1) * 128], ident_b[:])
                            tk = tok[n % 4]
                            S.copy('dve', tk[:], ph[:, 0:JT * 128].rearrange("p (j d) -> p j d", d=128))
                            dstS = KS if kind == 'k' else VS
                            S.dma(dstS[tsl, h * 128:(h + 1) * 128].rearrange("(j t) d -> t j d", t=128), tk[:], q='act')
            wb = wbf[1]
            load_w(es, w_in[l][:, C_ZA:C_ZA + 512], 8, 512, wst, wb)
            for h in range(4):
                for tt in range(NTT):
                    n += 1
                    ps = PF()
                    proj(wb, h * 128, tt * TW, TW, ps)
                    S.act(qn[n % 4][:], ps[:, 0:TW], AF.Silu)
                    S.dma(szaS[h * 128:(h + 1) * 128, tt * TW:(tt + 1) * TW], qn[n % 4][:], q='act')
        if nph <= 2:
            break

        for es in (esAB,):
            wst = None
            b3_first = len(S.ops)
            w16 = sbt(es, "w16", [128, 8, 16], BF16)
            AB = sbt(es, "AB", [128, NT, 16])
            tmp = sbt(es, "tmp3", [128, 4, NT * 8])
            load_w(es, w_in[l][:, C_AB:C_AB + 16], 8, 16, wst, w16, eng='dve')
            for sc in range(NT):
                ps = PF()
                for c in range(8):
                    S.mm(ps[:, 0:16], hT[:, c, sc * 128:(sc + 1) * 128], w16[:, c, :], start=(c == 0), stop=(c == 7))
                S.copy(AD(), AB[:, sc, :], ps[:, 0:16])

            def v3(ap2):
                return ap2.rearrange("p (s h) -> p s h", h=8)
            xg, ax, e1, mx = [tmp[:, i, :] for i in range(4)]
            S.tt('dve', v3(xg), AB[:, :, 0:8], bc_mid(dtb_bc[:], NT), ALU.add)
            S.act(ax, xg, AF.Abs)
            S.act(e1, ax, AF.Exp, scale=-1.0)
            S.act(e1, e1, AF.Ln, bias=1.0)
            S.ts('dve', mx, xg, 0.0, None, ALU.max)
            S.tt('dve', mx, mx, e1, ALU.add)
            S.tt('dve', v3(Gt), v3(mx), bc_mid(negA[:], NT), ALU.mult)
            S.act(v3(BETAt), AB[:, :, 8:16], AF.Sigmoid)
            W = NT * 8

            def cum(maskF, maskB, dst):
                psa = PF()
                psb = PF()
                S.mm(psa[:, 0:W], maskF, Gt)
                S.mm(psb[:, 0:W], maskB, Gt)
                S.act(v3(dst)[:, :, 0:4], v3(psa[:, 0:W])[:, :, 0:4], AF.Exp)
                S.act(v3(dst)[:, :, 4:8], v3(psb[:, 0:W])[:, :, 4:8], AF.Exp)
            cum(LE, GE, EGCt)
            cum(GT, LT, EGLGt)
            cum(BLKA, BLKA, DLAt)
            cum(BLKB, BLKB, DLBt)
            S.tt('dve', BEGCt, BETAt, EGCt, ALU.mult)
            S.copy('dve', GHL[:, 0, :], Gt)
            S.tt('dve', TAB[:, 7, :], Gt, GHL[:, 0, :], ALU.subtract)
            S.copy('dve', GHL[:, 1, :], TAB[:, 7, :])
        for op_ in S.ops[b3_first:]:
            if not op_.get('fence'):
                op_['boost'] = 1e6
        if nph <= 3:
            break
        esAB.close()
        S.fence()
        if 'tab' in taps and l == 0:
            tapo['tab'] = nc.dram_tensor("tap_tab", [128, 8, NT * 8], F32, kind="ExternalOutput").ap()
            S.dma(tapo['tab'], TAB[:])

        with ExitStack() as es:
            def t2(name, shape, dt=BF16, nb=2):
                return [sbt(es, "%s%d" % (name, i), shape, dt) for i in range(nb)]
            KtT = t2("cKt", [128, 4, 128], BF16, 4); QtT = t2("cQt", [128, 4, 128], BF16, 4)
            KtokT = t2("cKtok", [128, 4, 128], BF16, 4); VtokT = t2("cVtok", [128, 4, 128], BF16, 4)
            gA = t2("cgA", [128, 4, 128], BF16)
            gAl = t2("cgAl", [128, 4, 128], BF16)
            Dm = t2("cD", [128, 4, 128], F32)
            attn = t2("cattn", [128, 4, 128]); Gm = t2("cGm", [128, 4, 128], F32)
            Lf = t2("cLf", [128, 4, 128], F32)
            Ub = t2("cU", [128, 4, 128], BF16, 12); Tb = t2("cT", [128, 4, 128], BF16, 12)
            Pb = t2("cP", [128, 4, 128], BF16, 12)
            attnT = t2("cattnT", [128, 4, 128], BF16, 4)
            KBG = t2("cKBG", [128, 4, 128], BF16, 4); BV = t2("cBV", [128, 4, 128], BF16, 4); KDEC = t2("cKDEC", [128, 4, 128], BF16, 4)
            NKC = t2("cNKC", [128, 4, 128], BF16, 4)
            vnewL = [t2("cvnew%d" % q, [128, 4, 128]) for q in range(2)]
            SfL = t2("cSf", [128, 4, 128], F32); StmpL = t2("cStmp", [128, 4, 128], F32)
            SbL = [t2("cSb%d" % q, [128, 4, 128]) for q in range(2)]
            ob = t2("cob", [128, 4, 128], F32, 4)
            sbiL = [0, 0]
            for q in range(2):
                S.memset('dve', SfL[q][:], 0.0)
                S.memset('dve', SbL[q][0][:], 0.0)
                S.memset('dve', SbL[q][1][:], 0.0)
            for it in range(NT):
                grp = []
                for dr in range(2):
                    if dr == 0:
                        S.capture()
                    else:
                        grp.append(S.release())
                        S.capture()
                    MA, MB, NEGM = (LE, GT, NEGF) if dr == 0 else (GE, LT, NEGB)
                    STRICT = MB
                    MBb, NEGMb = (GTb, NEGFb) if dr == 0 else (LTb, NEGBb)
                    sc = it if dr == 0 else NT - 1 - it
                    b = dr
                    b4 = dr * 2 + it % 2
                    cur['c'] = dr
                    cur['stage'] = 'prep'
                    Sf, Stmp, Sb, vnew = SfL[dr], StmpL[dr], SbL[dr], vnewL[dr]
                    sbi = sbiL[dr]
                    tsl = slice(sc * 128, (sc + 1) * 128)
                    hs = slice(sc * 8 + dr * 4, sc * 8 + dr * 4 + 4)
                    Kt, Qt, Ktok, Vtok = KtT[b4], QtT[b4], KtokT[b4], VtokT[b4]
                    S.dma(Kt[:], KtS[:, tsl].rearrange("(h d) t -> d h t", d=128))
                    S.dma(Qt[:], QtS[:, tsl].rearrange("(h d) t -> d h t", d=128))
                    S.dma(Ktok[:], KS[tsl, :].rearrange("t (h d) -> t h d", d=128))
                    S.dma(Vtok[:], VS[tsl, :].rearrange("t (h d) -> t h d", d=128))
                    S.tt('pool', gA[b][:], bc_mid(MA, 4), bc_last(GHL[:, 0, hs], 128), ALU.mult)
                    S.tt('pool', gAl[b][:], bc_mid(MA, 4), bc_last(GHL[:, 1, hs], 128), ALU.mult)
                    pd = PF()
                    for h in range(4):
                        S.mm(pd[:, h * 128:(h + 1) * 128], gA[b][:, h, :], MBb, start=True, stop=False)
                        S.mm(pd[:, h * 128:(h + 1) * 128], gAl[b][:, h, :], MBb, start=False, stop=False)
                        S.mm(pd[:, h * 128:(h + 1) * 128], ident_b[:], NEGMb, start=False, stop=True)
                    S.act(Dm[b][:], pd[:].rearrange("p (h j) -> p h j", j=128), AF.Exp)
                    if cph <= 1:
                        continue
                    pg = PF()
                    pq = PF()
                    for h in range(4):
                        S.mm(pg[:, h * 128:(h + 1) * 128], Kt[:, h, :], Kt[:, h, :])
                        S.mm(pq[:, h * 128:(h + 1) * 128], Qt[:, h, :], Kt[:, h, :])
                    v4 = lambda p: p[:].rearrange("p (h j) -> p h j", j=128)
                    S.tt('dve', attn[b][:], v4(pq), Dm[b][:], ALU.mult)
                    S.tt('dve', Gm[b][:], v4(pg), bc_mid(STRICT, 4), ALU.mult)
                    S.tt('pool', Lf[b][:], Gm[b][:], Dm[b][:], ALU.mult)
                    U0 = Ub[b4 * 3]
                    S.tt('dve' if OPT['rebal'] != '0' else 'pool', U0[:], Lf[b][:], bc_last(BETAt[:, hs], 128), ALU.mult)
                    if cph <= 2:
                        continue
                    pl = PH()
                    pa_ = PH()
                    for h in range(4):
                        S.tr(pl[:, h * 128:(h + 1) * 128], U0[:, h, :], ident_b[:])
                        S.tr(pa_[:, h * 128:(h + 1) * 128], attn[b][:, h, :], ident_b[:])
                    vh = lambda p: p.rearrange("p (h j) -> p h j", j=128)
                    T0_ = Tb[b4 * 3]
                    P0_ = Pb[b4 * 3]
                    S.copy('act', T0_[:], vh(pl))
                    S.tt('dve', P0_[:], bc_mid(IDENT, 4), vh(pl), ALU.subtract)
                    S.copy('act', attnT[b4][:], vh(pa_))
                    Uc, Tc, Pc = U0, T0_, P0_
                    cur['stage'] = 'neu' if OPT['cb'] == 'b' else 'prep'
                    for j in range(1, 6):
                        Un, Tn, Pn = Ub[b4 * 3 + j % 3], Tb[b4 * 3 + j % 3], Pb[b4 * 3 + j % 3]
                        pu = PF()
                        for h in range(4):
                            S.mm(pu[:, h * 128:(h + 1) * 128], Tc[:, h, :], Uc[:, h, :])
                        S.copy('act', Un[:], v4(pu))
                        if j < 5:
                            pt = PF()
                            for h in range(4):
                                S.mm(pt[:, h * 128:(h + 1) * 128], Uc[:, h, :], Tc[:, h, :])
                            S.copy('act', Tn[:], v4(pt))
                        pp = PF()
                        for h in range(4):
                            S.mm(pp[:, h * 128:(h + 1) * 128], Un[:, h, :], Pc[:, h, :])
                        S.tt('dve', Pn[:], v4(pp), Pc[:], ALU.add)
                        Uc, Tc, Pc = Un, Tn, Pn
                    if cph <= 3:
                        continue
                    AT = Pc
                    cur['stage'] = 'prep'
                    S.tt('pool', KBG[b4][:], Ktok[:], bc_last(BEGCt[:, hs], 128), ALU.mult)
                    S.tt('dve' if OPT['rebal'] == '2' else 'pool', BV[b4][:], Vtok[:], bc_last(BETAt[:, hs], 128), ALU.mult)
                    S.tt('pool', KDEC[b4][:], Ktok[:], bc_last(EGLGt[:, hs], 128), ALU.mult)
                    pk = PF()
                    for h in range(4):
                        S.mm(pk[:, h * 128:(h + 1) * 128], KBG[b4][:, h, :], AT[:, h, :])
                    S.ts('dve', NKC[b4][:], v4(pk), -1.0, None, ALU.mult)
                    if cph <= 4:
                        continue
                    o_sc = ob[b4]
                    cur['stage'] = 'scan'
                    for blk in ((0, 1) if dr == 0 else (1, 0)):
                        r = slice(blk * 64, blk * 64 + 64)
                        DL = DLAt if blk == 0 else DLBt
                        Scur = Sb[sbi % 2]
                        pqs = PF()
                        for h in range(4):
                            S.mm(pqs[:, h * 128:(h + 1) * 128], Qt[:, h, :], Scur[:, h, :])
                        S.tt('dve', o_sc[r], pqs[r].rearrange("p (h j) -> p h j", j=128),
                             bc_last(EGCt[r, hs], 128), ALU.mult)
                        pv = PF()
                        for h in range(4):
                            S.mm(pv[:, h * 128:(h + 1) * 128], AT[:, h, :], BV[b4][:, h, :], start=True, stop=False)
                            S.mm(pv[:, h * 128:(h + 1) * 128], NKC[b4][:, h, :], Scur[:, h, :], start=False, stop=True)
                        vn = vnew[sbi % 2]
                        S.copy('act', vn[r], pv[r].rearrange("p (h j) -> p h j", j=128))
                        pav = PF()
                        for h in range(4):
                            S.mm(pav[:, h * 128:(h + 1) * 128], attnT[b4][r, h, :], vn[r, h, :])
                        S.tt('dve', o_sc[r], o_sc[r], pav[r].rearrange("p (h j) -> p h j", j=128), ALU.add)
                        pds = PF()
                        for h in range(4):
                            S.mm(pds[:, h * 128:(h + 1) * 128], KDEC[b4][r, h, :], vn[r, h, :])
                        S.tt('pool', Stmp[:], Sf[:], bc_last(DL[:, hs], 128), ALU.mult)
                        S.tt('dve', Sf[:], Stmp[:], v4(pds), ALU.add)
                        sbi += 1
                        S.copy('act', Sb[sbi % 2][:], Sf[:])
                    sbiL[dr] = sbi
                    if cph <= 5:
                        continue
                    S.dma((oS if dr == 0 else oSb)[tsl, :].rearrange("t (h e) -> t h e", e=128), o_sc[:], q='act')
                grp.append(S.release())
                if OPT.get('zip', '1') == '1':
                    S.merge(*grp)
                else:
                    for g_ in grp:
                        S.ops.extend(g_)
                cur['c'] = None
        if nph <= 4:
            break
        S.fence()
        tb.close()
        esW = ExitStack()
        wo = sbt(esW, "wo", [128, 8, 1024], BF16)
        wdn = sbt(esW, "wdn", [128, 4, 1024], BF16)
        wcf = sbt(esW, "wcf", [128, 4, 1024], BF16)

        with ExitStack() as es:
            ucv = sbt(es, "ucv", [128, 4, L], BF16)
            def t3(name, shape, dt=BF16, nb=2):
                return [sbt(es, "%s%d" % (name, i), shape, dt) for i in range(nb)]
            oF = t3("coF", [128, 4, 128], F32)
            oB = t3("coB", [128, 4, 128], F32)
            ssq = t3("cssq", [128, 8], F32)
            onb = t3("conb", [128, 4, 128])
            szat = t3("cszat", [128, 4, 128])
            pat = t3("cpat", [128, 4, 128])
            junkC = t3("cjunk", [128, 128], F32)
            for sc in range(NT if cph > 5 else 0):
                b = sc % 2
                tsl = slice(sc * 128, (sc + 1) * 128)
                S.dma(oF[b][:], oS[tsl, :].rearrange("t (h e) -> t h e", e=128))
                S.dma(oB[b][:], oSb[tsl, :].rearrange("t (h e) -> t h e", e=128))
                S.dma(szat[b][:], szaS[:, tsl].rearrange("(h e) t -> e h t", e=128))
                S.tt('pool', oF[b][:], oF[b][:], oB[b][:], ALU.add)
                for h in range(4):
                    S.act(junkC[b][:], oF[b][:, h, :], AF.Square, accum_out=ssq[b][:, h:h + 1])
                rsqrt_to(es, ssq[b][:, 4:8], ssq[b][:, 0:4], EPS, 1.0 / 128.0)
                S.tt('dve', onb[b][:], oF[b][:], bc_last(ssq[b][:, 4:8], 128), ALU.mult)
                po = PH()
                for h in range(4):
                    S.tr(po[:, h * 128:(h + 1) * 128], onb[b][:, h, :], ident_b[:])
                S.stt('dve', pat[b][:], po.rearrange("p (h j) -> p h j", j=128), dnw, szat[b][:], ALU.mult, ALU.mult)
                S.dma(paS[:, tsl].rearrange("(h e) t -> e h t", e=128), pat[b][:], q='act')
            with ExitStack() as es1:
                wst = None
                wbV = sbt(es1, "wbV", [128, 8, 512], BF16)
                wbG = sbt(es1, "wbG", [128, 8, 512], BF16)
                rbu = [sbt(es1, "rbu%d" % i, [128, L + 2 * HAL], BF16) for i in range(2)]
                dg31 = [sbt(es1, "dg31_%d" % i, [128, 23, 128], BF16) for i in range(2)]
                sg = [sbt(es1, "sgD%d" % i, [128, TW]) for i in range(2)]
                accD = [sbt(es1, "accD%d" % i, [128, TW]) for i in range(2)]
                for r in rbu:
                    S.memset('pool', r[:, 0:HAL], 0.0)
                    S.memset('pool', r[:, HAL + L:], 0.0)
                load_w(es1, w_in[l][:, C_CFV:C_CFV + 512], 8, 512, wst, wbV)
                load_w(es1, w_in[l][:, C_CFG:C_CFG + 512], 8, 512, wst, wbG)
                load_w(es1, w_dn_out[l], 4, 1024, None, wdn)
                load_w(es1, w_cf_out[l], 4, 1024, None, wcf)
                load_w(es1, w_out[l], 8, 1024, None, wo)
                n = 0
                for cc in range(4):
                    r = rbu[cc % 2]
                    d = dg31[cc % 2]
                    for k in range(8, 31):
                        S.act(d[:, k - 8, :], IDENT, AF.Copy, scale=cfw(k, cc))
                    for tt in range(NTT):
                        n += 1
                        pv = PF()
                        pg = PF()
                        proj(wbV, cc * 128, tt * TW, TW, pv)
                        proj(wbG, cc * 128, tt * TW, TW, pg)
                        S.act(sg[n % 2][:], pg[:, 0:TW], AF.Sigmoid)
                        S.tt('dve', r[:, HAL + tt * TW:HAL + (tt + 1) * TW], pv[:, 0:TW], sg[n % 2][:], ALU.mult)
                    NDV = 8
                    for tt in range(NTT):
                        ac = accD[(cc * NTT + tt) % 2]
                        for k in range(NDV):
                            o = HAL + tt * TW + k - 15
                            if k == 0:
                                S.ts('dve', ac[:], r[:, o:o + TW], cfw(k, cc), None, ALU.mult)
                            else:
                                S.stt('dve', ac[:], r[:, o:o + TW], cfw(k, cc), ac[:], ALU.mult, ALU.add)
                        ps = PF()
                        for k in range(NDV, 31):
                            o = HAL + tt * TW + k - 15
                            S.mm(ps[:, 0:TW], d[:, k - NDV, :], r[:, o:o + TW], start=(k == NDV), stop=(k == 30))
                        S.stt('dve', ucv[:, cc, tt * TW:(tt + 1) * TW], ps[:, 0:TW], cfb(cc), ac[:], ALU.add, ALU.add)
            S.fence()
            with ExitStack() as es2:
                wst = None
                wbZ = sbt(es2, "wbZ", [128, 8, 512], BF16)
                sqc = [sbt(es2, "sqc%d" % i, [128, 4, TW], BF16) for i in range(2)]
                mean = [sbt(es2, "mean%d" % i, [128, TW]) for i in range(2)]
                msq = [sbt(es2, "msq%d" % i, [128, TW]) for i in range(2)]
                rsd = [sbt(es2, "rsd%d" % i, [128, TW]) for i in range(2)]
                szb = [sbt(es2, "szb%d" % i, [128, TW]) for i in range(2)]
                t1 = [sbt(es2, "t1_%d" % i, [128, TW]) for i in range(2)]
                t3 = [sbt(es2, "t3_%d" % i, [128, TW]) for i in range(2)]
                pbt = [sbt(es2, "pbt%d" % i, [128, TW], BF16) for i in range(2)]
                load_w(es2, w_in[l][:, C_ZB:C_ZB + 512], 8, 512, wst, wbZ)
                n = 0
                for tt in range(NTT):
                    b = tt % 2
                    tsl = slice(tt * TW, (tt + 1) * TW)
                    for cc in range(4):
                        S.act(sqc[b][:, cc, :], ucv[:, cc, tsl], AF.Square)
                    pm = PF()
                    pq = PF()
                    for cc in range(4):
                        S.mm(pm[:, 0:TW], onesLN_b[:], ucv[:, cc, tsl], start=(cc == 0), stop=(cc == 3))
                    for cc in range(4):
                        S.mm(pq[:, 0:TW], onesLN_b[:], sqc[b][:, cc, :], start=(cc == 0), stop=(cc == 3))
                    S.copy('act', mean[b][:], pm[:, 0:TW])
                    S.tt('pool', msq[b][:], mean[b][:], mean[b][:], ALU.mult)
                    S.tt('dve', msq[b][:], pq[:, 0:TW], msq[b][:], ALU.subtract)
                    rsqrt_to(es2, rsd[b][:], msq[b][:], EPS, wide=True)
                    for cc in range(4):
                        n += 1
                        m = n % 2
                        pz = PF()
                        proj(wbZ, cc * 128, tt * TW, TW, pz)
                        S.act(szb[m][:], pz[:, 0:TW], AF.Silu)
                        S.tt('dve', t1[m][:], ucv[:, cc, tsl], mean[b][:], ALU.subtract)
                        S.tt('pool', t1[m][:], t1[m][:], rsd[b][:], ALU.mult)
                        S.act(t3[m][:], t1[m][:], AF.Silu, bias=lnb(cc), scale=lnw(cc))
                        S.tt('dve', pbt[m][:], t3[m][:], szb[m][:], ALU.mult)
                        S.dma(pbS[cc * 128:(cc + 1) * 128, tsl], pbt[m][:], q='act')
        if nph <= 5:
            break
        S.fence()

        with ExitStack() as es:
            wst = [None, None]
            wg = sbt(es, "wg", [128, 8, 2048], BF16)
            pat = [sbt(es, "pat4_%d" % i, [128, 4, TW], BF16) for i in range(2)]
            pbt = [sbt(es, "pbt4_%d" % i, [128, 4, TW], BF16) for i in range(2)]
            sg0 = [sbt(es, "sg0_%d" % i, [128, TW]) for i in range(2)]
            sg1 = [sbt(es, "sg1_%d" % i, [128, TW]) for i in range(2)]
            yT = [sbt(es, "yT%d" % i, [128, 8, TW], BF16) for i in range(2)]
            xt4 = [sbt(es, "xt4_%d" % i, [128, DM]) for i in range(4)]
            xn = xt4
            junk = sbt(es, "junk4", [128, DM], BF16)
            ss4 = sbt(es, "ss4", [128, 8])
            if last:
                fnw_bc = sbt(es, "fnw_bc", [128, DM])
                S.dma(fnw_bc[:], final_norm_w.partition_broadcast(128))
            elif fuseA:
                nwF = sbt(es, "nwF", [128, DM])
                S.dma(nwF[:], norm_w[l + 1].partition_broadcast(128))
                xqF = sbt(es, "xqF", [128, DM], BF16)
            def ldg(q, eng):
                load_w(es, w_in[l][:, C_GATE + q * 256:C_GATE + (q + 1) * 256], 8, 256, wst[q % 2], wg[:, :, q * 256:(q + 1) * 256], eng=eng)
            for q in range(4):
                ldg(q, 'pool')
                ldg(q + 4, 'act')
            n = 0
            xi = 0
            for tt in range(NTT):
                b = tt % 2
                tsl = slice(tt * TW, (tt + 1) * TW)
                S.dma(pat[b][:], paS[:, tsl].rearrange("(h e) t -> e h t", e=128))
                S.dma(pbt[b][:], pbS[:, tsl].rearrange("(h e) t -> e h t", e=128))
                for c in range(8):
                    n += 1
                    m = n % 2
                    pA = PF()
                    pB = PF()
                    p0 = PF()
                    p1 = PF()
                    cs = slice(c * 128, (c + 1) * 128)
                    for h in range(4):
                        S.mm(pA[:, 0:TW], wdn[:, h, cs], pat[b][:, h, :], start=(h == 0), stop=(h == 3))
                    for h in range(4):
                        S.mm(pB[:, 0:TW], wcf[:, h, cs], pbt[b][:, h, :], start=(h == 0), stop=(h == 3))
                    proj(wg, c * 128, tt * TW, TW, p0)
                    proj(wg, 1024 + c * 128, tt * TW, TW, p1)
                    S.act(sg0[m][:], p0[:, 0:TW], AF.Sigmoid, bias=gateb(c))
                    S.act(sg1[m][:], p1[:, 0:TW], AF.Sigmoid, bias=gateb(8 + c))
                    S.tt('dve', sg0[m][:], pA[:, 0:TW], sg0[m][:], ALU.mult)
                    S.tt('dve', sg1[m][:], pB[:, 0:TW], sg1[m][:], ALU.mult)
                    S.tt('pool', yT[b][:, c, :], sg0[m][:], sg1[m][:], ALU.add)
                for j in range(JT):
                    xi += 1
                    m = xi % 4
                    rows = slice(tt * TW + j * 128, tt * TW + (j + 1) * 128)
                    S.dma(xt4[m][:], x_cur[rows, :])
                    for half in range(2):
                        po = PF()
                        for c in range(8):
                            S.mm(po[:, 0:512], yT[b][:, c, j * 128:(j + 1) * 128], wo[:, c, half * 512:(half + 1) * 512],
                                 start=(c == 0), stop=(c == 7))
                        S.tt('dve', xn[m][:, half * 512:(half + 1) * 512], po[:, 0:512],
                             xt4[m][:, half * 512:(half + 1) * 512], ALU.add)
                    if not last:
                        S.dma(xs[rows, :], xn[m][:])
                        if fuseA:
                            sq = ss4[:, m * 2:m * 2 + 1]
                            rs = ss4[:, m * 2 + 1:m * 2 + 2]
                            S.act(junk[:], xn[m][:], AF.Square, accum_out=sq)
                            rsqrt_to(es, rs, sq, EPS, 1.0 / DM)
                            S.stt('dve', xqF[:], xn[m][:], rs, nwF[:], ALU.mult, ALU.mult)
                            for half in range(2):
                                ph = PH()
                                for k in range(4):
                                    c = half * 4 + k
                                    S.tr(ph[:, k * 128:(k + 1) * 128], xqF[:, c * 128:(c + 1) * 128], ident_b[:])
                                S.copy('act' if half == 0 else 'dve', hT[:, half * 4:half * 4 + 4, rows],
                                       ph[:, 0:512].rearrange("p (k t) -> p k t", t=128))
                    else:
                        sq = ss4[:, m * 2:m * 2 + 1]
                        rs = ss4[:, m * 2 + 1:m * 2 + 2]
                        S.act(junk[:], xn[m][:], AF.Square, accum_out=sq)
                        rsqrt_to(es, rs, sq, EPS, 1.0 / DM)
                        S.stt('dve', xt4[m][:], xn[m][:], rs, fnw_bc[:], ALU.mult, ALU.mult)
                        S.dma(out[rows, :], xt4[m][:])
        S.fence()
        esW.close()
        lay.close()

    S.final()
    semstack = ExitStack()
    stats = S.emit(lambda name: semstack.enter_context(nc.semaphore(name)), reorder=reorder)
    return nc, stats, tapo


_CACHE = {}


def kernel(**inputs):
    x = np.ascontiguousarray(np.asarray(inputs['x'], dtype=np.float32))
    B, L, _ = x.shape
    key = (L,)
    if key not in _CACHE:
        _CACHE[key] = build(L)
    nc, stats, _ = _CACHE[key]
    shared = {}
    for k in ('norm_w', 'w_in', 'qkv_conv_w', 'dn_norm_w', 'w_dn_out', 'cf_conv_w', 'cf_conv_b',
              'cf_ln_w', 'cf_ln_b', 'w_cf_out', 'gate_b', 'w_out', 'final_norm_w'):
        shared[k] = np.ascontiguousarray(np.asarray(inputs[k], dtype=np.float32))
    shared['a_log'] = np.ascontiguousarray(np.asarray(inputs['a_log'], dtype=np.float32).reshape(DEPTH, 8))
    shared['dt_bias'] = np.ascontiguousarray(np.asarray(inputs['dt_bias'], dtype=np.float32).reshape(DEPTH, 8))
    in_maps = [dict(shared, x=x[b]) for b in range(B)]
    res = run_bass_kernel_spmd(nc, in_maps, core_ids=list(range(B)))
    return np.stack([np.asarray(r['out'], dtype=np.float32) for r in res.results], axis=0)
```
